# Optimizing a Trainium2 kernel written in Bass

```python
import math
import jax, jax.numpy as jnp
from jax import lax
import numpy as np

D_MODEL = 1024
BATCH = 2
SEQ = 8192
DEPTH = 1

CTX_LEN = 256
GRID_W = 64
EPS = 1e-6

SSD_EXPAND = 2
D_INNER = SSD_EXPAND * D_MODEL
HEAD_DIM = 64
N_HEADS = D_INNER // HEAD_DIM
N_GROUPS = 8
HEADS_PER_GROUP = N_HEADS // N_GROUPS
D_STATE = 128
D_CONV = 5
CHUNK = 128
DT_MIN = 1e-3
DT_MAX = 1e-1
GS = N_GROUPS * D_STATE
CONV_DIM = D_INNER + 2 * GS

FFT_WIDTH = D_MODEL
FFT_GROUPS = 8
FFT_GROUP_DIM = FFT_WIDTH // FFT_GROUPS

N_BRANCH = 2

DT_START = CONV_DIM
Z_START = DT_START + 2 * N_HEADS
FFT_START = Z_START + D_INNER
GATE_START = FFT_START + FFT_WIDTH
IN_WIDTH = GATE_START + N_BRANCH * D_MODEL
SSD_IN_WIDTH = Z_START

D_FF = 256 * ((8 * D_MODEL + 3 * 256 - 1) // (3 * 256))

kernel_name = 'hybrid_ssd_fourier_dit_block'


def rmsnorm(x, g):
    xf = x.astype(jnp.float32)
    y = xf * lax.rsqrt(jnp.mean(xf * xf, axis=-1, keepdims=True) + EPS)
    return (y * g.astype(jnp.float32)).astype(x.dtype)


def modulate(h, shift, scale):
    return h * (1 + scale) + shift


def dwconv_centred(u, w, b):
    out = lax.conv_general_dilated(
        u, w[:, None, :].astype(u.dtype), window_strides=(1,),
        padding=[(D_CONV // 2, D_CONV // 2)],
        dimension_numbers=('NWC', 'WIO', 'NWC'),
        feature_group_count=u.shape[-1])
    return out + b.astype(u.dtype)


def ssd_inputs(u, conv_w, conv_b):
    b, l, _ = u.shape
    xbc = jax.nn.silu(dwconv_centred(u[..., :CONV_DIM], conv_w, conv_b))
    xs = xbc[..., :D_INNER].reshape(b, l, N_GROUPS, HEADS_PER_GROUP, HEAD_DIM)
    bm = xbc[..., D_INNER:D_INNER + GS].reshape(b, l, N_GROUPS, D_STATE)
    cm = xbc[..., D_INNER + GS:CONV_DIM].reshape(b, l, N_GROUPS, D_STATE)
    dt_raw = u[..., DT_START:Z_START].reshape(b, l, 2, N_GROUPS, HEADS_PER_GROUP)
    return xs, bm, cm, dt_raw


def decay_matrix(a_cum):
    q = a_cum.shape[-1]
    diff = a_cum[..., :, None] - a_cum[..., None, :]
    mask = jnp.tril(jnp.ones((q, q), dtype=bool))
    return jnp.exp(jnp.where(mask, diff, -jnp.inf))


def ssd_chunked(xs, dt, a, bm, cm, init_state, need_y):
    b, l = xs.shape[:2]
    nc = l // CHUNK
    xdt = (xs.astype(jnp.float32) * dt[..., None]).reshape(
        b, nc, CHUNK, N_GROUPS, HEADS_PER_GROUP, HEAD_DIM)
    a_cum = jnp.cumsum((dt * a).reshape(b, nc, CHUNK, N_GROUPS, HEADS_PER_GROUP), axis=2)
    bc = bm.astype(jnp.float32).reshape(b, nc, CHUNK, N_GROUPS, D_STATE)
    cc = cm.astype(jnp.float32).reshape(b, nc, CHUNK, N_GROUPS, D_STATE)
    decay_to_end = jnp.exp(a_cum[:, :, -1:] - a_cum)
    states = jnp.einsum('bcsgn,bcsgj,bcsgjp->bcgjpn', bc, decay_to_end, xdt)
    chunk_decay = jnp.exp(a_cum[:, :, -1])

    def step(carry, inp):
        st, dec = inp
        return carry * dec[..., None, None] + st, carry

    final, entering = lax.scan(step, init_state,
                               (jnp.moveaxis(states, 1, 0), jnp.moveaxis(chunk_decay, 1, 0)))
    if not need_y:
        return None, final
    entering = jnp.moveaxis(entering, 0, 1)
    cb = jnp.einsum('bclgn,bcsgn->bcgls', cc, bc)
    lmat = decay_matrix(jnp.moveaxis(a_cum, 2, -1))
    y_diag = jnp.einsum('bcgjls,bcsgjp->bclgjp', cb[:, :, :, None] * lmat, xdt)
    y_off = jnp.einsum('bclgn,bcgjpn,bclgj->bclgjp', cc, entering, jnp.exp(a_cum))
    y = (y_diag + y_off).reshape(b, l, N_GROUPS, HEADS_PER_GROUP, HEAD_DIM)
    return y, final


def ssd_bidirectional(lat_in, ctx_in, dt_bias, a_log, need_ctx_y):
    xs, bm, cm, dtr = lat_in
    xs_c, bm_c, cm_c, dtr_c = ctx_in
    b = xs.shape[0]
    ys_lat, ys_ctx = [], []
    for d in range(2):
        flip = (lambda t: jnp.flip(t, axis=1)) if d == 1 else (lambda t: t)
        a = -jnp.exp(a_log[d].astype(jnp.float32)).reshape(N_GROUPS, HEADS_PER_GROUP)
        bias = dt_bias[d].astype(jnp.float32).reshape(N_GROUPS, HEADS_PER_GROUP)
        dt_c = jax.nn.softplus(dtr_c[:, :, d].astype(jnp.float32) + bias)
        dt_l = jax.nn.softplus(dtr[:, :, d].astype(jnp.float32) + bias)
        init = jnp.zeros((b, N_GROUPS, HEADS_PER_GROUP, HEAD_DIM, D_STATE), jnp.float32)
        y_c, s_ctx = ssd_chunked(flip(xs_c), flip(dt_c), a, flip(bm_c), flip(cm_c), init, need_ctx_y)
        y_l, _ = ssd_chunked(flip(xs), flip(dt_l), a, flip(bm), flip(cm), s_ctx, True)
        ys_lat.append(flip(y_l))
        if need_ctx_y:
            ys_ctx.append(flip(y_c))
    y_ctx = ys_ctx[0] + ys_ctx[1] if need_ctx_y else None
    return ys_lat[0] + ys_lat[1], y_ctx


def ssd_output(y, xs, z, d_skip, gain):
    b, l = y.shape[:2]
    y = y + xs.astype(jnp.float32) * d_skip.astype(jnp.float32).reshape(N_GROUPS, HEADS_PER_GROUP, 1)
    y = y.reshape(b, l, D_INNER) * jax.nn.silu(z.astype(jnp.float32))
    y = y.reshape(b, l, N_GROUPS, D_INNER // N_GROUPS)
    y = y * lax.rsqrt(jnp.mean(y * y, axis=-1, keepdims=True) + EPS)
    return (y.reshape(b, l, D_INNER) * gain.astype(jnp.float32)).astype(z.dtype)


def fourier_grid(f, rows):
    b, l, _ = f.shape
    fg = f.astype(jnp.float32).reshape(b, rows, GRID_W, FFT_GROUPS, FFT_GROUP_DIM)
    out = jnp.fft.fftn(fg, axes=(1, 2, 4), norm='ortho').real
    return out.reshape(b, l, FFT_WIDTH).astype(f.dtype)


def fourier_seq(f):
    b, l, _ = f.shape
    fg = f.astype(jnp.float32).reshape(b, l, FFT_GROUPS, FFT_GROUP_DIM)
    out = jnp.fft.fftn(fg, axes=(1, 3), norm='ortho').real
    return out.reshape(b, l, FFT_WIDTH).astype(f.dtype)


def merge_branches(u_gate, y_ssd, f_mix, w_ssd_out, w_fft_out, w_o):
    b, l, _ = u_gate.shape
    gates = jax.nn.sigmoid(u_gate.astype(jnp.float32)).astype(y_ssd.dtype)
    gates = gates.reshape(b, l, N_BRANCH, D_MODEL)
    merged = gates[:, :, 0] * (y_ssd @ w_ssd_out) + gates[:, :, 1] * (f_mix @ w_fft_out)
    return merged @ w_o


def swiglu(h, w_in, w_out):
    gu = h @ w_in
    return (jax.nn.silu(gu[..., :D_FF]) * gu[..., D_FF:]) @ w_out


def setup_inputs(seed: int = 0) -> dict:
    key = jax.random.key(seed)
    ks = jax.random.split(key, 22)

    def nrm(k, shape, s):
        return jax.random.normal(k, shape, jnp.float32) * s

    x = nrm(ks[0], (BATCH, SEQ, D_MODEL), 1.0)
    c = nrm(ks[1], (BATCH, D_MODEL), 1.0)
    ctx = nrm(ks[2], (BATCH, CTX_LEN, D_MODEL), 1.0)
    c_ctx = nrm(ks[3], (D_MODEL,), 1.0)
    w_ada = nrm(ks[4], (DEPTH, D_MODEL, 6 * D_MODEL), 0.5 * D_MODEL ** -0.5)
    b_ada = nrm(ks[5], (DEPTH, 6 * D_MODEL), 0.01)
    norm1_g = 1.0 + nrm(ks[6], (DEPTH, D_MODEL), 0.01)
    w_in = nrm(ks[7], (DEPTH, D_MODEL, IN_WIDTH), D_MODEL ** -0.5)
    conv_w = nrm(ks[8], (DEPTH, D_CONV, CONV_DIM), D_CONV ** -0.5)
    conv_b = nrm(ks[9], (DEPTH, CONV_DIM), 0.01)
    dt = jnp.exp(jax.random.uniform(ks[10], (DEPTH, 2, N_HEADS), jnp.float32,
                                    minval=math.log(DT_MIN), maxval=math.log(DT_MAX)))
    dt_bias = dt + jnp.log(-jnp.expm1(-dt))
    a_log = jnp.log(jax.random.uniform(ks[11], (DEPTH, 2, N_HEADS), jnp.float32,
                                       minval=1.0, maxval=16.0))
    d_skip = 1.0 + nrm(ks[12], (DEPTH, N_HEADS), 0.01)
    ssd_norm_g = 1.0 + nrm(ks[13], (DEPTH, D_INNER), 0.01)
    w_ssd_out = nrm(ks[14], (DEPTH, D_INNER, D_MODEL), D_INNER ** -0.5)
    w_fft_out = nrm(ks[15], (DEPTH, FFT_WIDTH, D_MODEL), FFT_WIDTH ** -0.5)
    w_o = nrm(ks[16], (DEPTH, D_MODEL, D_MODEL), D_MODEL ** -0.5)
    norm2_g = 1.0 + nrm(ks[17], (DEPTH, D_MODEL), 0.01)
    w_ffn_in = nrm(ks[18], (DEPTH, D_MODEL, 2 * D_FF), D_MODEL ** -0.5)
    w_ffn_out = nrm(ks[19], (DEPTH, D_FF, D_MODEL), D_FF ** -0.5)
    final_g = 1.0 + nrm(ks[20], (D_MODEL,), 0.01)
    return {'x': x, 'c': c, 'ctx': ctx, 'c_ctx': c_ctx, 'w_ada': w_ada, 'b_ada': b_ada,
            'norm1_g': norm1_g, 'w_in': w_in, 'conv_w': conv_w, 'conv_b': conv_b,
            'dt_bias': dt_bias, 'a_log': a_log, 'd_skip': d_skip, 'ssd_norm_g': ssd_norm_g,
            'w_ssd_out': w_ssd_out, 'w_fft_out': w_fft_out, 'w_o': w_o, 'norm2_g': norm2_g,
            'w_ffn_in': w_ffn_in, 'w_ffn_out': w_ffn_out, 'final_g': final_g}


def reference(x, c, ctx, c_ctx, w_ada, b_ada, norm1_g, w_in, conv_w, conv_b, dt_bias, a_log,
              d_skip, ssd_norm_g, w_ssd_out, w_fft_out, w_o, norm2_g, w_ffn_in, w_ffn_out,
              final_g):
    rows = x.shape[1] // GRID_W
    for i in range(DEPTH):
        last = i == DEPTH - 1
        sh1, sc1, g1, sh2, sc2, g2 = jnp.split((jax.nn.silu(c) @ w_ada[i] + b_ada[i])[:, None, :], 6, axis=-1)
        cmods = jnp.split((jax.nn.silu(c_ctx) @ w_ada[i] + b_ada[i])[None, None, :], 6, axis=-1)

        h = modulate(rmsnorm(x, norm1_g[i]), sh1, sc1)
        hc = modulate(rmsnorm(ctx, norm1_g[i]), cmods[0], cmods[1])
        u = h @ w_in[i]
        uc = hc @ (w_in[i][:, :SSD_IN_WIDTH] if last else w_in[i])

        lat_in = ssd_inputs(u, conv_w[i], conv_b[i])
        ctx_in = ssd_inputs(uc, conv_w[i], conv_b[i])
        y_lat, y_ctx = ssd_bidirectional(lat_in, ctx_in, dt_bias[i], a_log[i], not last)
        y_lat = ssd_output(y_lat, lat_in[0], u[..., Z_START:FFT_START], d_skip[i], ssd_norm_g[i])
        f_lat = fourier_grid(u[..., FFT_START:GATE_START], rows)
        mix = merge_branches(u[..., GATE_START:], y_lat, f_lat, w_ssd_out[i], w_fft_out[i], w_o[i])
        x_new = x + g1 * mix

        x_new = x_new + g2 * swiglu(modulate(rmsnorm(x_new, norm2_g[i]), sh2, sc2),
                                    w_ffn_in[i], w_ffn_out[i])

        if not last:
            y_c = ssd_output(y_ctx, ctx_in[0], uc[..., Z_START:FFT_START], d_skip[i], ssd_norm_g[i])
            f_c = fourier_seq(uc[..., FFT_START:GATE_START])
            mix_c = merge_branches(uc[..., GATE_START:], y_c, f_c, w_ssd_out[i], w_fft_out[i], w_o[i])
            ctx = ctx + cmods[2] * mix_c
            ctx = ctx + cmods[5] * swiglu(modulate(rmsnorm(ctx, norm2_g[i]), cmods[3], cmods[4]),
                                          w_ffn_in[i], w_ffn_out[i])
        x = x_new
    return rmsnorm(x, final_g)
```

```python
import math
import os
from contextlib import ExitStack
import numpy as np
import ml_dtypes
import concourse.bass as bass
import concourse.mybir as mybir
from concourse.bass_utils import run_bass_kernel_spmd

F32 = mybir.dt.float32
BF16 = mybir.dt.bfloat16
AF = mybir.ActivationFunctionType
ALU = mybir.AluOpType
AX = mybir.AxisListType

D = 1024
NCST = 3328
NSMALL = 532
NT = 2048
NTILE = 16
CTX = 256
EPS = 1e-6
HW = NT + CTX + 4
HALO0 = NT + CTX
DT_START = 4096
Z_START = 4160
FFT_START = Z_START + 2048
GATE_START = FFT_START + 1024
D_FF = 2816
UW = 2312
CO = 2308
CTXO = 2052
GROUPS = [[0, 1, 2, 3], [4, 5, 6, 7]]


class _Op:
    pass


class Prog:
    def __init__(self, nc, n_dma=32):
        self.nc = nc
        self.ops = []
        self.lastw = {}
        self.rd_eng = {}
        self.rd_dma = {}
        self.n_dma = n_dma
        self.rr = 0
        self.rr_pool = 0
        self.last_on = [None] * n_dma
        self.last_eng = {}
        self.dma_since = []
        self.fence_dep = None
        self.cc_w = {}

    PSKEY = {"psC": "psA0", "psZ": "psA6", "psY": "psA1", "psS0": "psA5", "psS1": "psA6", ("psD", 0): "psA2",
             ("psD", 1): "psA3", "psO0": "psA4", "psO1": "psA4", "psDa": "psA6", "psDb": "psA6", "psDt": "psA6",
             "psE": "psA5", "psT03": "psT", "psT45": "psT", "psT7": "psT"}

    def add(self, eng, fn, reads=(), writes=(), dma=False, cc=False):
        reads = [self.PSKEY.get(k, k) for k in reads]
        writes = [self.PSKEY.get(k, k) for k in writes]
        op = _Op()
        op.eng, op.fn, op.dma, op.cc = eng, fn, dma, cc
        op.idx = len(self.ops)
        deps = set()
        for k in reads:
            w = self.lastw.get(k)
            if w is not None:
                deps.add(w)
            if isinstance(k, str) and k.startswith("ps"):
                for e2, i2 in self.rd_eng.get(k, {}).items():
                    if e2 != eng:
                        deps.add(i2)
        for k in writes:
            w = self.lastw.get(k)
            if w is not None:
                deps.add(w)
            deps.update(self.rd_eng.get(k, {}).values())
            deps.update(self.rd_dma.get(k, ()))
        if dma:
            if eng == "pool":
                s = self.n_dma - 8 + (self.rr_pool % 8)
                self.rr_pool += 1
            else:
                s = self.rr % (self.n_dma - 8)
                self.rr += 1
            op.dsem = s
            if self.last_on[s] is not None:
                deps.add(self.last_on[s])
            self.last_on[s] = op.idx
        if self.fence_dep is not None:
            deps.add(self.fence_dep)
        for k in list(reads) + list(writes):
            if k in self.cc_w:
                deps.add(self.cc_w[k])
        if cc:
            for k in writes:
                self.cc_w[k] = op.idx
        op.deps = deps
        if dma or cc:
            self.dma_since.append(op.idx)
        else:
            self.last_eng[eng] = op.idx
        for k in reads:
            if dma or cc:
                self.rd_dma.setdefault(k, []).append(op.idx)
            else:
                self.rd_eng.setdefault(k, {})[eng] = op.idx
        for k in writes:
            self.lastw[k] = op.idx
            self.rd_eng[k] = {}
            self.rd_dma[k] = []
        self.ops.append(op)
        return op

    def fence(self):
        deps = set(self.last_eng.values())
        deps.update(i for i in self.dma_since if not self.ops[i].cc)
        op = self.add("sp", lambda e: e.nop())
        op.deps |= deps
        self.fence_dep = op.idx
        self.dma_since = []
        self.lastw = {}
        self.rd_eng = {}
        self.rd_dma = {}
        self.last_on = [None] * self.n_dma

    def emit(self, es):
        nc = self.nc
        ops = self.ops
        engs = ("pe", "act", "dve", "pool", "sp")
        needs = [False] * len(ops)
        for op in ops:
            for d in op.deps:
                dd = ops[d]
                if dd.dma or dd.cc:
                    continue
                if dd.eng == "pe" and op.eng == "pe" and not (op.dma or op.cc):
                    continue
                needs[d] = True
        esem = {e: es.enter_context(nc.semaphore("s_" + e)) for e in engs}
        dsem = [es.enter_context(nc.semaphore("d%d" % i)) for i in range(self.n_dma)]
        ncc = sum(1 for op in ops if op.cc)
        ccsems = [es.enter_context(nc.semaphore("ccs%d" % i)) for i in range(ncc)]
        cnt = {e: 0 for e in engs}
        dcnt = [0] * self.n_dma
        cccnt = 0
        for op in ops:
            if op.cc:
                op.ev = (ccsems[cccnt], 1)
                cccnt += 1
            elif op.dma:
                dcnt[op.dsem] += 16
                op.ev = (dsem[op.dsem], dcnt[op.dsem])
            elif needs[op.idx]:
                cnt[op.eng] += 1
                op.ev = (esem[op.eng], cnt[op.eng])
            else:
                op.ev = None
        per = {e: [] for e in engs}
        for op in ops:
            per[op.eng].append(op)
        self.stats = {e: len(per[e]) for e in engs}

        def run(ename, eng):
            seen = {}
            for op in per[ename]:
                waits = {}
                for d in op.deps:
                    dd = ops[d]
                    if dd.eng == "pe" and ename == "pe" and not (dd.dma or dd.cc) and not (op.dma or op.cc):
                        continue
                    sem, val = dd.ev
                    key = id(sem)
                    if seen.get(key, 0) >= val:
                        continue
                    if key not in waits or waits[key][1] < val:
                        waits[key] = (sem, val)
                for key, (sem, val) in waits.items():
                    eng.wait_ge(sem, val)
                    seen[key] = val
                ins = op.fn(eng)
                if op.cc:
                    ins.then_inc(op.ev[0])
                elif op.dma:
                    ins.then_inc(op.ev[0], 16)
                elif op.ev is not None:
                    ins.then_inc(op.ev[0], 1)
            if ename == "sp":
                for i in range(self.n_dma):
                    if dcnt[i]:
                        eng.wait_ge(dsem[i], dcnt[i])
                for cs in ccsems:
                    eng.wait_ge(cs, 1)

        with nc.Block() as block:
            @block.tensor
            def _(e):
                run("pe", e)

            @block.scalar
            def _(e):
                run("act", e)

            @block.vector
            def _(e):
                run("dve", e)

            @block.gpsimd
            def _(e):
                run("pool", e)

            @block.sync
            def _(e):
                run("sp", e)


ARENA = 150 * 1024


def build(debug=None):
    nc = bass.Bass("TRN2", target_bir_lowering=False)
    es = ExitStack()
    P = Prog(nc)

    in_names = []

    def din(name, shape, dt=F32):
        in_names.append(name)
        return nc.dram_tensor(name, list(shape), dt, kind="ExternalInput").ap()

    x_d = din("x", [NT, D])
    ctx_d = din("ctx", [CTX, D])
    xh_d = din("xhalo", [4, D])
    cvec_d = din("cvec", [128, 16])
    wada_d = din("w_ada", [D, 6 * D])
    bada_row_d = din("b_ada_row", [1, 6 * D])
    win_d = din("w_in", [D, 9280])
    identc_d = din("identc", [128, 128])
    cst_d = din("cst", [128, NCST])
    smallc_d = din("smallc", [128, NSMALL])
    convw_d = din("convw_fm", [128, 160])
    out_d = nc.dram_tensor("out", [NT, D], F32, kind="ExternalOutput").ap()
    cparts = [nc.dram_tensor("contribA", [128, 2048], F32), nc.dram_tensor("contribB", [128, 2048], F32),
              nc.dram_tensor("contribC", [128, 64], F32)]
    gparts = [nc.dram_tensor("gathA", [512, 2048], F32), nc.dram_tensor("gathB", [512, 2048], F32),
              nc.dram_tensor("gathC", [512, 64], F32)]

    def contrib_ap(g, d):
        return cparts[g // 4].ap()[:, ((g % 4) * 2 + d) * 256:((g % 4) * 2 + d + 1) * 256]

    def gath_ap(g, d):
        return gparts[g // 4].ap().rearrange("(r n) c -> n r c", n=128)[:, :, ((g % 4) * 2 + d) * 256:((g % 4) * 2 + d + 1) * 256]
    sctx_d = nc.dram_tensor("sctx", [128, 4096], F32).ap()
    part_d = nc.dram_tensor("partd", [128, 65536], BF16)
    rs_d = nc.dram_tensor("rsd", [32, 65536], BF16)
    yT_d = nc.dram_tensor("yTd", [16, 128, NT], BF16).ap()
    dbg = {}

    def dout(name, shape, dt):
        dbg[name] = nc.dram_tensor("dbg_" + name, list(shape), dt, kind="ExternalOutput").ap()
        return dbg[name]

    def sb(name, shape, dt=F32):
        return es.enter_context(nc.sbuf_tensor(name, list(shape), dt))

    def ps(name, shape, dt=F32):
        return es.enter_context(nc.psum_tensor(name, list(shape), dt))

    arena = sb("arena", [128, ARENA // 2], BF16)

    class Bump:
        def __init__(self, base=0):
            self.off = base

        def __call__(self, shape, dt=BF16):
            n = int(np.prod(shape[1:]))
            esz = 2 if dt == BF16 else 4
            nb = (n * esz + 31) // 32 * 32
            assert self.off + nb <= ARENA, ("arena overflow", self.off + nb)
            v = arena[0:shape[0], self.off // 2: self.off // 2 + n * esz // 2]
            self.off += nb
            if dt != BF16:
                v = v.bitcast(dt)
            if len(shape) == 3:
                v = v.rearrange("p (a b) -> p a b", a=shape[1])
            elif len(shape) == 4:
                v = v.rearrange("p (a b c) -> p a b c", a=shape[1], b=shape[2])
            return v

    def MM(out, lhsT, rhs, start, stop, reads, writes):
        P.add("pe", lambda e: e.matmul(out, lhsT=lhsT, rhs=rhs, start=start, stop=stop), reads, writes)

    def TR(out, in_, ident, reads, writes):
        P.add("pe", lambda e: e.transpose(out=out, in_=in_, identity=ident), reads, writes)

    def ACTF(out, in_, func, reads, writes, bias=None, scale=None, accum=None):
        kw = {}
        if bias is not None:
            kw["bias"] = bias
        if scale is not None:
            kw["scale"] = scale
        if accum is not None:
            kw["accum_out"] = accum
        P.add("act", lambda e: e.activation(out=out, in_=in_, func=func, **kw), reads, writes)

    def CP(eng, out, in_, reads, writes):
        if eng == "act":
            P.add("act", lambda e: e.copy(out=out, in_=in_), reads, writes)
        else:
            P.add(eng, lambda e: e.tensor_copy(out=out, in_=in_), reads, writes)

    def TT(out, in0, in1, op, reads, writes, eng="dve"):
        P.add(eng, lambda e: e.tensor_tensor(out=out, in0=in0, in1=in1, op=op), reads, writes)

    def TS(out, in0, s1, op0, reads, writes, s2=None, op1=None, eng="dve"):
        if op1 is None:
            P.add(eng, lambda e: e.tensor_scalar(out=out, in0=in0, scalar1=s1, scalar2=None, op0=op0), reads, writes)
        else:
            P.add(eng, lambda e: e.tensor_scalar(out=out, in0=in0, scalar1=s1, scalar2=s2, op0=op0, op1=op1), reads, writes)

    def STT(out, in0, scalar, in1, op0, op1, reads, writes, eng="dve"):
        P.add(eng, lambda e: e.scalar_tensor_tensor(out=out, in0=in0, scalar=scalar, in1=in1, op0=op0, op1=op1),
              reads, writes)

    def RCP(out, in_, reads, writes):
        P.add("dve", lambda e: e.reciprocal(out=out, in_=in_), reads, writes)

    def DMA(q, out, in_, reads, writes):
        P.add(q, lambda e: e.dma_start(out=out, in_=in_), reads, writes, dma=True)

    def MEMSET(ap, val, writes):
        P.add("dve", lambda e: e.memset(ap, val), (), writes)

    def bc4(ap4, n=64):
        return ap4.unsqueeze(2).to_broadcast([128, 4, n])

    ident_f = sb("ident_f", [128, 128], F32)
    ones_f = sb("ones_f", [128, 128], F32)
    epsc = sb("epsc", [128, 2])
    cst = sb("cst_sb", [128, NCST], BF16)
    smallc = sb("smallc_sb", [128, NSMALL])
    convw = sb("convw_sb", [128, 160])
    mods = sb("mods", [128, 48, 2])
    a1 = sb("a1", [128, 8, 2])
    a2 = sb("a2", [128, 8, 2])
    gbc = sb("gbc", [128, 2048])
    hT = sb("hT", [128, 8, HW], BF16)
    ssq = sb("ssq", [128, 32])
    rstd = sb("rstd", [128, 32])
    ident_bf = cst[:, 0:128]
    triU = cst[:, 128:256]
    triL = cst[:, 256:384]
    ones_bf = cst[:, 384:512]
    mask4 = cst[:, 512:1536].rearrange("p (d n) -> p d n", d=2)
    W2 = cst[:, 1536:1792]
    Wch = cst[:, 1792:2304]
    CrBlk = cst[:, 2304:3328].rearrange("p (c a k) -> p c a k", c=4, a=2)
    bada_fm = smallc[:, 0:48]
    n1g = smallc[:, 48:56]
    n2g = smallc[:, 56:64]
    convb = smallc[:, 64:96]
    ssdg = smallc[:, 96:112]
    dskip = smallc[:, 112:144]
    hmask = smallc[:, 144:146]
    dtb = smallc[:, 146:147]
    alog = smallc[:, 147:148]
    cmask = smallc[:, 148:404].rearrange("p (r c) -> p r c", r=4)
    dtb_bc = smallc[:, 404:468]
    alog_bc = smallc[:, 468:532]

    psA = [ps("psA%d" % i, [128, 512]) for i in range(7)]
    psT = ps("psT", [128, 8, 128], BF16)

    DMA("sp", ident_f[:], identc_d[:, :], [], ["ident_f"])
    DMA("pool", cst[:], cst_d[:, :], [], ["cst"])
    DMA("sp", smallc[:], smallc_d[:, :], [], ["smallc"])
    DMA("sp", convw[:], convw_d[:, :], [], ["convw"])
    MEMSET(ones_f[:], 1.0, ["ones_f"])
    MEMSET(epsc[:, 0:1], EPS, ["epsc"])
    MEMSET(epsc[:, 1:2], 1.0, ["epsc"])

    win_v = win_d.rearrange("(k p) n -> p k n", p=128)

    B = Bump()
    wada = [B([128, 8, 512], F32) for _ in range(2)]
    cv = B([128, 16], F32)
    scv = B([128, 16], F32)
    screp = B([128, 8, 128], F32)
    bada_row = B([1, 2048], F32)
    xt = [B([128, D], F32) for _ in range(3)]
    xn = [B([128, D], BF16) for _ in range(2)]
    junk = B([128, D], BF16)

    DMA("sp", cv, cvec_d[:, :], [], ["cv"])
    DMA("sp", bada_row[:, 0:1024], bada_row_d[:, 2048:3072], [], ["bada_row"])
    DMA("sp", bada_row[:, 1024:2048], bada_row_d[:, 5120:6144], [], ["bada_row"])
    ACTF(scv, cv, AF.Silu, ["cv"], ["scv"])
    CP("dve", screp, scv[:, 0:8].unsqueeze(2).to_broadcast([128, 8, 128]), ["scv"], ["screp"])
    wada_v = wada_d.rearrange("(k p) n -> p k n", p=128)
    scv3 = scv.rearrange("p (v k) -> p k v", v=2)
    mods_ps = psA[3]
    blk_order = [0, 1, 2, 3, 6, 7, 8, 9, 4, 5, 10, 11]
    for bi, blk in enumerate(blk_order):
        wt = wada[bi % 2]
        wk = "wada%d" % (bi % 2)
        DMA("sp", wt, wada_v[:, :, blk * 512:(blk + 1) * 512], [], [wk])
        if blk in (4, 5, 10, 11):
            gi = {4: 0, 5: 1, 10: 2, 11: 3}[blk]
            pst = psA[gi % 2]
            pk = "psA%d" % (gi % 2)
            for k in range(8):
                MM(pst[:, :], screp[:, k, :], wt[:, k, :], k == 0, False, ["screp", wk], [pk])
            MM(pst[:, :], ones_f[0:1, :], bada_row[0:1, gi * 512:(gi + 1) * 512], False, True, ["ones_f", "bada_row"], [pk])
            CP("act", gbc[:, gi * 512:(gi + 1) * 512], pst[:, :], [pk], ["gbc"])
        else:
            for jj in range(4):
                j = blk * 4 + jj
                for k in range(8):
                    MM(mods_ps[:, 2 * j:2 * j + 2], wt[:, k, jj * 128:(jj + 1) * 128], scv3[:, k, :], k == 0, k == 7,
                       ["scv", wk], ["psA3"])
    mps3 = mods_ps[:, 0:96].rearrange("p (j v) -> p j v", v=2)
    for lo in (0, 24):
        TT(mods[:, lo:lo + 16, :], mps3[:, lo:lo + 16, :], bada_fm[:, lo:lo + 16].unsqueeze(2).to_broadcast([128, 16, 2]),
           ALU.add, ["psA3", "smallc"], ["mods"])
    for (aa, off, ng) in ((a1, 8, n1g), (a2, 32, n2g)):
        TS(aa[:], mods[:, off:off + 8, :], 1.0, ALU.add, ["mods"], ["a12"])
        TT(aa[:], aa[:], ng.unsqueeze(2).to_broadcast([128, 8, 2]), ALU.mult, ["a12", "smallc"], ["a12"])

    def norm_core(ti, xin, xk, nrows, dst_fn, v, acol, shoff, dkeys, inv_n=1.0 / D):
        s2 = ti % 2
        nk = "xn%d" % s2
        xnn = xn[s2]
        ACTF(junk[0:nrows, :], xin, AF.Square, [xk], ["junk", ("ssq", ti)], accum=ssq[0:nrows, ti:ti + 1])
        ACTF(rstd[0:nrows, ti:ti + 1], ssq[0:nrows, ti:ti + 1], AF.Sqrt, [("ssq", ti), "epsc"], [("rstd", ti)],
             bias=epsc[0:nrows, 0:1], scale=inv_n)
        RCP(rstd[0:nrows, ti:ti + 1], rstd[0:nrows, ti:ti + 1], [("rstd", ti)], [("rstd", ti)])
        TS(xnn[0:nrows, :], xin, rstd[0:nrows, ti:ti + 1], ALU.mult, [xk, ("rstd", ti)], [nk])
        for k in range(8):
            TR(psT[:, k, 0:nrows], xnn[0:nrows, k * 128:(k + 1) * 128], ident_bf[0:nrows, 0:nrows], [nk, "cst"], ["psT"])
        for k in range(8):
            ACTF(dst_fn(k), psT[:, k, 0:nrows], AF.Identity, ["psT", "a12", "mods"], dkeys,
                 bias=mods[:, shoff + k, v:v + 1], scale=acol[:, k, v:v + 1])

    def norm_tile(ti, src_ap, nrows, col0, v):
        s3 = ti % 3
        xk = "xt%d" % s3
        DMA("sp", xt[s3][0:nrows, :], src_ap, [], [xk])
        norm_core(ti, xt[s3][0:nrows, :], xk, nrows, lambda k: hT[:, k, col0:col0 + nrows], v, a1, 0,
                  [("hT", col0 // 128)])

    norm_tile(18, xh_d[:, :], 4, HALO0, 0)
    for t in range(NTILE):
        norm_tile(t, x_d[t * 128:(t + 1) * 128, :], 128, t * 128, 0)
    for t in range(2):
        norm_tile(16 + t, ctx_d[t * 128:(t + 1) * 128, :], 128, NT + t * 128, 1)

    def hk(tb):
        return [("hT", 4 * tb + i) for i in range(4)]

    P.fence()
    STOP = [False]

    class WPool:
        def __init__(self, bufs, name):
            self.bufs, self.name, self.i = bufs, name, 0

        def load(self, src_ap, shape_sel=None):
            i = self.i % len(self.bufs)
            self.i += 1
            buf = self.bufs[i]
            key = "%s%d" % (self.name, i)
            dst = buf if shape_sel is None else shape_sel(buf)
            DMA("pool", dst, src_ap, [], [key])
            return buf, key

    if debug not in ("ssd", "pd") and not os.environ.get("KSKIPF"):
        B = Bump()
        wfft = B([128, 8, 1024])
        f_tm = [B([128, 1024]) for _ in range(2)]
        Z = B([128, 8, 2, NT])
        Y = [B([128, 2, 1024]) for _ in range(2)]
        Pq = [B([128, 4, 1024]) for _ in range(2)]
        for hf in range(2):
            DMA("pool", wfft[:, :, hf * 512:(hf + 1) * 512], win_v[:, :, FFT_START + hf * 512:FFT_START + (hf + 1) * 512],
                [], [("wfft", hf)])
        for t in range(NTILE):
            ft = f_tm[t % 2]
            fk = "f_tm%d" % (t % 2)
            for hf in range(2):
                pst, pk = psA[hf], "psA%d" % hf
                for k in range(8):
                    MM(pst[:, :], hT[:, k, t * 128:(t + 1) * 128], wfft[:, k, hf * 512:(hf + 1) * 512], k == 0, k == 7,
                       [("hT", t), ("wfft", hf)], [pk])
                CP("act", ft[:, hf * 512:(hf + 1) * 512], pst[:, :], [pk], [fk])
            for gp in range(4):
                pst, pk = psA[2 + gp % 2], "psA%d" % (2 + gp % 2)
                for gi in range(2):
                    g = 2 * gp + gi
                    MM(pst[:, gi * 256:(gi + 1) * 256], ft[:, g * 128:(g + 1) * 128], W2, True, True, [fk, "cst"], [pk])
                for gi in range(2):
                    for ab in range(2):
                        zo = Z[:, 2 * gp + gi, ab, :].rearrange("p (q c r) -> p r q c", q=16, c=4, r=32)[:, 2 * t:2 * t + 2, :, :]
                        zi = pst[:, gi * 256 + ab * 128:gi * 256 + (ab + 1) * 128].rearrange("p (r q c) -> p r q c", r=2, q=16, c=4)
                        CP("dve" if ab else "act", zo, zi, [pk], [("Z", t)])
        zkeys = [("Z", t) for t in range(NTILE)]
        for quad in range(16):
            Yq, yk = Y[quad % 2], "Y%d" % (quad % 2)
            Yv = Yq.rearrange("p a (g k) -> p g a k", g=8)
            for gp in range(4):
                pst, pk = psA[gp % 2], "psA%d" % (gp % 2)
                for gi in range(2):
                    g = 2 * gp + gi
                    for ab in range(2):
                        zsel = Z[:, g, ab, quad * 128:(quad + 1) * 128]
                        MM(pst[:, gi * 256:(gi + 1) * 256], zsel, Wch[:, ab * 256:(ab + 1) * 256], ab == 0, ab == 1,
                           zkeys + ["cst"], [pk])
                CP("dve" if gp % 2 else "act", Yv[:, 2 * gp:2 * gp + 2, :, :],
                   pst[:, :].rearrange("p (g a k) -> p g a k", g=2, a=2), [pk], [yk])
            Pqq, pqk = Pq[quad % 2], "Pq%d" % (quad % 2)
            for c4 in range(4):
                for hf in range(2):
                    pst, pk = psA[2 + hf], "psA%d" % (2 + hf)
                    MM(pst[:, :], CrBlk[:, c4, 0, :], Yq[:, 0, hf * 512:(hf + 1) * 512], True, False, [yk, "cst"], [pk])
                    MM(pst[:, :], CrBlk[:, c4, 1, :], Yq[:, 1, hf * 512:(hf + 1) * 512], False, True, [yk, "cst"], [pk])
                    CP("dve" if hf else "act", Pqq[:, c4, hf * 512:(hf + 1) * 512], pst[:, :], [pk], [pqk])
            DMA("sp", part_d.ap()[:, quad * 4096:(quad + 1) * 4096], Pqq.rearrange("p c k -> p (c k)"), [pqk], ["part_d"])
        P.add("pool", lambda e: e.collective_compute("ReduceScatter", ALU.add, replica_groups=GROUPS,
                                                     ins=[part_d.ap().opt()], outs=[rs_d.ap().opt()]),
              ["part_d"], ["rs_d"], cc=True)
        if debug == "fft":
            dd = dout("rs", [32, 65536], BF16)
            tmpb = Bump(100 * 1024)
            tb_ = tmpb([32, 16384])
            for i in range(4):
                DMA("sp", tb_, rs_d.ap()[:, i * 16384:(i + 1) * 16384], ["rs_d"], ["tb_"])
                DMA("sp", dd[:, i * 16384:(i + 1) * 16384], tb_, ["tb_"], ["dd"])
            STOP[0] = True
        P.fence()

    if not STOP[0] and debug != "pd":
        B = Bump()
        dt_tm = B([128, 18, 64], F32)
        negA = B([128, 18, 64], F32)
        expA = B([128, 18, 64], F32)
        dec_bc = B([128, 18, 64], F32)
        dtw = B([128, 18, 64], F32)
        dta_bf = B([128, 18, 64])
        Dcore = B([128, 64], F32)
        Dm = B([128, 4, 64], F32)
        wq = WPool([B([128, 8, 256]) for _ in range(4)], "wq")
        u_sb = [B([128, UW]) for _ in range(2)]
        acc = B([128, 2048], F32)
        diagw = [B([128, 5, 128]) for _ in range(2)]
        xbcT = B([128, 4, CO])
        xsB = B([128, 18, 384])
        Eb_all = B([128, 16, 256])
        E = [B([128, 256], F32) for _ in range(2)]
        Ebf0 = B([128, 256])
        Fg = [B([128, 4, 256], F32) for _ in range(2)]
        xdtS = [B([128, 256]) for _ in range(2)]
        xdt = [[B([128, 256]) for _ in range(2)] for _ in range(2)]
        xsD = [B([128, 256]) for _ in range(2)]
        drep = [B([128, 8, 128]) for _ in range(2)]
        Lt = [[B([128, 4, 128]) for _ in range(2)] for _ in range(2)]
        GT = [[B([128, 4, 128]) for _ in range(2)] for _ in range(2)]
        T0 = B([128, 256], F32)
        T1 = B([128, 256], F32)
        YG = B([128, 256], F32)
        SZ = [B([128, 256], F32) for _ in range(2)]
        YN = B([128, 256])
        yTg = B([128, 2, NT])
        ss2 = B([128, 16], F32)
        r2 = B([128, 16], F32)
        SCR = B.off
        B2 = Bump(SCR)
        wdt = B2([128, 8, 64])
        nega_bc = B2([128, 64], F32)
        tmpE = B2([128, 64], F32)
        Asb = B2([128, 128], F32)
        DMA("pool", wdt, win_v[:, :, DT_START:DT_START + 64], [], ["wdt"])
        ACTF(nega_bc, alog_bc, AF.Exp, ["smallc"], ["nega"])
        TS(nega_bc, nega_bc, -1.0, ALU.mult, ["nega"], ["nega"])
        psD = psA[6]
        psE = psA[5]
        KDT = float(os.environ.get("KDT", "9"))
        for t in range(18 if KDT >= 2 else 0):
            c0 = t * 128
            for k in range(8):
                MM(psD[:, 128:192], hT[:, k, c0:c0 + 128], wdt[:, k, :], k == 0, k == 7, ["wdt", ("hT", t)], ["psDt"])
            TT(tmpE, psD[:, 128:192], dtb_bc, ALU.add, ["psDt", "smallc"], ["tmpE"])
            ACTF(tmpE, tmpE, AF.Exp, ["tmpE"], ["tmpE"])
            ACTF(dt_tm[:, t, :], tmpE, AF.Ln, ["tmpE", "epsc"], [("dt_tm", t)], bias=epsc[:, 1:2])
            TT(dta_bf[:, t, :], dt_tm[:, t, :], nega_bc, ALU.mult, [("dt_tm", t), "nega"], [("dta_bf", t)])
            if KDT < 3:
                continue
            MM(psD[:, 0:32], triU, dta_bf[:, t, 0:32], True, True, ["cst", ("dta_bf", t)], ["psDa"])
            MM(psD[:, 32:64], triL, dta_bf[:, t, 32:64], True, True, ["cst", ("dta_bf", t)], ["psDa"])
            MM(psD[:, 64:128], ones_bf, dta_bf[:, t, :], True, True, ["cst", ("dta_bf", t)], ["psDb"])
            if t < 16 and not os.environ.get("KNOPSE"):
                MM(psE[:, 0:64], ones_bf, dta_bf[:, t, :], t == 0, t == 15, ["cst", ("dta_bf", t)], ["psE"])
            if KDT < 3.2:
                continue
            CP("dve", Asb, psD[:, 0:128], ["psDa"], ["Asb"])
            TS(negA[:, t, :], Asb[:, 0:64], -1.0, ALU.mult, ["Asb"], [("negA", t)])
            if KDT < 3.4:
                continue
            ACTF(expA[:, t, :], Asb[:, 0:64], AF.Exp, ["Asb"], [("expA", t)])
            ACTF(dec_bc[:, t, :], Asb[:, 64:128], AF.Exp, ["Asb"], [("dec", t)])
            if KDT < 3.6:
                continue
            TT(dtw[:, t, :], Asb[:, 64:128], negA[:, t, :], ALU.add, ["Asb", ("negA", t)], [("dtw", t)])
            ACTF(dtw[:, t, :], dtw[:, t, :], AF.Exp, [("dtw", t)], [("dtw", t)])
            TT(dtw[:, t, :], dtw[:, t, :], dt_tm[:, t, :], ALU.mult, [("dtw", t), ("dt_tm", t)], [("dtw", t)])
        if KDT >= 4:
            ACTF(Dcore, psE[:, 0:64], AF.Exp, ["psE"], ["Dcore"])
            DMA("sp", cparts[2].ap()[:, :], Dcore, ["Dcore"], ["contrib"])
        for ub in (u_sb if KDT >= 5 else []):
            MEMSET(ub[:, 2052:2054], 0.0, ["u_sb0", "u_sb1"])
            MEMSET(ub[:, 2310:2312], 0.0, ["u_sb0", "u_sb1"])
        P.fence()

        uctr = [0]
        dctr = [0]

        def ssd_prep(g, p2):
            chunks = [(2 * g, 256 * g), (2 * g + 1, 256 * g + 128), (16 + g, 2048 + 128 * g)]
            if p2:
                chunks.append((24 + g, 3072 + 128 * g))
            CW = NT if p2 else CO
            for ci, (cch, col) in enumerate(chunks):
                wb, wk = wq.load(win_v[:, :, col:col + 128], lambda b: b[:, :, 0:128])
                ui = uctr[0] % 2
                uctr[0] += 1
                ub, uk = u_sb[ui], "u_sb%d" % ui
                for tb in range(4):
                    pst, pk = psA[tb % 2], "psA%d" % (tb % 2)
                    for k in range(8):
                        MM(pst[:, :], wb[:, k, 0:128], hT[:, k, tb * 512:(tb + 1) * 512], k == 0, k == 7, [wk] + hk(tb), [pk])
                    CP("act", ub[:, 2 + tb * 512:2 + (tb + 1) * 512], pst[:, :], [pk], [uk])
                if not p2:
                    pst, pk = psA[0], "psA0"
                    for k in range(8):
                        MM(pst[:, 0:256], wb[:, k, 0:128], hT[:, k, NT:NT + 256], k == 0, k == 7,
                           [wk, ("hT", 16), ("hT", 17)], [pk])
                    CP("act", ub[:, 2054:2310], pst[:, 0:256], [pk], [uk])
                pst, pk = psA[1], "psA1"
                for k in range(8):
                    MM(pst[:, 0:4], wb[:, k, 0:128], hT[:, k, HALO0:HALO0 + 4], k == 0, k == 7, [wk, ("hT", 18)], [pk])
                TS(ub[:, 0:2], pst[:, 0:2], hmask[:, 0:1], ALU.mult, [pk, "smallc"], [uk])
                TS(ub[:, 2050:2052], pst[:, 2:4], hmask[:, 1:2], ALU.mult, [pk, "smallc"], [uk])
                di = dctr[0] % 2
                dctr[0] += 1
                dg, dk = diagw[di], "diagw%d" % di
                for k in range(5):
                    TS(dg[:, k, :], ident_bf, convw[:, cch * 5 + k:cch * 5 + k + 1], ALU.mult, ["cst", "convw"], [dk])
                blocks = [(tb * 512, 512) for tb in range(4)] + ([] if p2 else [(CTXO, 256)])
                for bi, (o0, n) in enumerate(blocks):
                    pst, pk = psA[2 + bi % 2], "psA%d" % (2 + bi % 2)
                    for k in range(5):
                        MM(pst[:, 0:n], dg[:, k, :], ub[:, o0 + k:o0 + k + n], k == 0, k == 4, [dk, uk], [pk])
                    ACTF(xbcT[:, ci, o0:o0 + n], pst[:, 0:n], AF.Silu, [pk, "smallc"], [("xbcT", ci)], bias=convb[:, cch:cch + 1])
            for t in (range(16) if p2 else range(18)):
                c0 = t * 128 if t < 16 else CTXO + (t - 16) * 128
                for ci in range(3):
                    TR(psT[:, ci, :], xbcT[:, ci, c0:c0 + 128], ident_bf, [("xbcT", ci), "cst"], ["psT03"])
                CP("act", xsB[:, t, :].rearrange("p (c k) -> p c k", c=3), psT[:, 0:3, :], ["psT03"], [("xsB", t)])

        EL = os.environ.get("KEL", "dve")

        def ss_a(g, t, d, si):
            c0 = d * 32 + 4 * g
            xs4 = xsB[:, t, 0:256].rearrange("p (j q) -> p j q", j=4)
            psS = psA[5][:, 0:256] if si == 0 else psA[6][:, 0:256]
            TT(xdtS[si].rearrange("p (j q) -> p j q", j=4), xs4, bc4(dtw[:, t, c0:c0 + 4]), ALU.mult,
               [("xsB", t), ("dtw", t)], ["xdtS%d" % si], eng=EL)
            MM(psS, xsB[:, t, 256:384], xdtS[si], True, True, [("xsB", t), "xdtS%d" % si], ["psS%d" % si])

        def ss_b(g, t, d, si, first):
            c0 = d * 32 + 4 * g
            psS = psA[5][:, 0:256] if si == 0 else psA[6][:, 0:256]
            pk = "psS%d" % si
            if first:
                CP("act", E[d], psS, [pk], [("E", d)])
            else:
                E4 = E[d].rearrange("p (j q) -> p j q", j=4)
                TT(E4, E4, bc4(dec_bc[:, t, c0:c0 + 4]), ALU.mult, [("E", d), ("dec", t)], [("E", d)])
                TT(E[d], E[d], psS, ALU.add, [("E", d), pk], [("E", d)])

        def run_states(g, d, tiles, first0, pre=None):
            tiles = list(tiles)
            ss_a(g, tiles[0], d, 0)
            for i, t in enumerate(tiles):
                if i + 1 < len(tiles):
                    ss_a(g, tiles[i + 1], d, (i + 1) % 2)
                if pre is not None:
                    pre(t)
                ss_b(g, t, d, i % 2, first0 and i == 0)

        def ssd_pass1(g):
            ssd_prep(g, False)
            for d, ctx_order, lat_order in ((1, (17, 16), range(15, -1, -1)), (0, (16, 17), range(16))):
                run_states(g, d, ctx_order, True)
                DMA("sp", sctx_d[:, (g * 2 + d) * 256:(g * 2 + d + 1) * 256], E[d], [("E", d)], ["sctx_d"])
                run_states(g, d, lat_order, True)
                DMA("sp", contrib_ap(g, d), E[d], [("E", d)], ["contrib"])

        NG = int(os.environ.get("KNG", "8"))
        if debug == "ssd_dt":
            dd = dout("dt", [128, 5, 18 * 64], F32)
            for i, arr in enumerate((dt_tm, negA, expA, dec_bc, dtw)):
                DMA("sp", dd[:, i, :], arr.rearrange("p t c -> p (t c)"), [(nm, t) for nm in ("dt_tm", "negA", "expA", "dec", "dtw") for t in range(18)], ["dd"])
            P.emit(es)
            es.close()
            P.in_names = in_names
            return nc, P
        for g in range(NG):
            ssd_pass1(g)
        if NG < 8:
            MEMSET(acc[:, 0:2048], 0.0, ["acc"])
            for gz in range(NG, 8):
                for dz in range(2):
                    DMA("sp", contrib_ap(gz, dz), acc[:, 0:256], ["acc"], ["contrib"])
        for ci in range(3):
            def _ag(e, ci=ci):
                return e.collective_compute("AllGather", ALU.bypass, replica_groups=GROUPS,
                                            ins=[cparts[ci].ap().opt()], outs=[gparts[ci].ap().opt()])
            P.add("pool", _ag, ["contrib", "rs_d", "ccchain"], ["gath", "ccchain"], cc=True)
        P.fence()
        DMA("sp", Dm, gparts[2].ap().rearrange("(r n) c -> n r c", n=128), ["gath"], ["Dm"])
        TS(Dm, Dm, -1.0, ALU.add, ["Dm"], ["Dm"])
        TT(Dm, Dm, cmask, ALU.mult, ["Dm", "smallc"], ["Dm"])
        TS(Dm, Dm, 1.0, ALU.add, ["Dm"], ["Dm"])

        def ssd_pass2(g):
            ssd_prep(g, True)
            wz, wzk = wq.load(win_v[:, :, Z_START + 256 * g:Z_START + 256 * (g + 1)])
            for d in range(2):
                c0 = d * 32 + 4 * g
                DMA("sp", Fg[d], gath_ap(g, d), ["gath"], [("Fg", d)])
                DMA("sp", E[d], sctx_d[:, (g * 2 + d) * 256:(g * 2 + d + 1) * 256], [], [("E", d)])
                E4 = E[d].rearrange("p (j q) -> p j q", j=4)
                for r in (range(4) if d == 0 else range(3, -1, -1)):
                    TT(E4, E4, bc4(Dm[:, r, c0:c0 + 4]), ALU.mult, [("E", d), "Dm"], [("E", d)])
                    STT(E[d], Fg[d][:, r, :], cmask[:, r, c0:c0 + 1], E[d], ALU.mult, ALU.add,
                        [("E", d), ("Fg", d), "smallc"], [("E", d)])
            run_states(g, 1, range(15, -1, -1), False,
                       pre=lambda t: CP("act", Eb_all[:, t, :], E[1], [("E", 1)], [("Eb", t)]))
            CP("act", Ebf0, E[0], [("E", 0)], ["Ebf0"])
            psC = psA[0][:, 0:128]
            psZ = psA[6][:, 0:256]
            psY = psA[1][:, 0:256]
            psO = psA[4]
            def h1a(t):
                p = t % 2
                cols = slice(t * 128, (t + 1) * 128)
                MM(psC, xbcT[:, 2, cols], xbcT[:, 3, cols], True, True, [("xbcT", 2), ("xbcT", 3)], ["psC"])
                for d in range(2):
                    CP(EL, drep[p][:, 4 * d:4 * d + 4, :],
                       dta_bf[:, t, d * 32 + 4 * g:d * 32 + 4 * g + 4].unsqueeze(2).to_broadcast([128, 4, 128]),
                       [("dta_bf", t)], [("drep", d, p)])
                for d in range(2):
                    c0 = d * 32 + 4 * g
                    pD = psA[2 + d]
                    for j in range(4):
                        MM(pD[:, 128 * j:128 * (j + 1)], ident_bf, mask4[:, d, 0:128], True, False, ["cst"], [("psD", d)])
                        MM(pD[:, 128 * j:128 * (j + 1)], drep[p][:, 4 * d + j, :], triU if d == 0 else triL, False, True,
                           [("drep", d, p), "cst"], [("psD", d)])
                    for j in range(4):
                        ACTF(Lt[d][p][:, j, :], pD[:, 128 * j:128 * (j + 1)], AF.Exp, [("psD", d), ("negA", t)], [("Lt", d, p)],
                             bias=negA[:, t, c0 + j:c0 + j + 1])
                for k in range(8):
                    MM(psZ, hT[:, k, cols], wz[:, k, :], k == 0, k == 7, [("hT", t), wzk], ["psZ"])
                ACTF(SZ[p], psZ, AF.Silu, ["psZ"], [("SZ", p)])

            def h1b(t):
                p = t % 2
                xs4 = xsB[:, t, 0:256].rearrange("p (j q) -> p j q", j=4)
                for d in range(2):
                    c0 = d * 32 + 4 * g
                    TT(GT[d][p], Lt[d][p], psC.unsqueeze(1).to_broadcast([128, 4, 128]), ALU.mult, [("Lt", d, p), "psC"],
                       [("GT", d, p)])
                    TT(xdt[d][p].rearrange("p (j q) -> p j q", j=4), xs4, bc4(dt_tm[:, t, c0:c0 + 4]), ALU.mult,
                       [("xsB", t), ("dt_tm", t)], [("xdt", d, p)], eng=EL)
                TT(xsD[p].rearrange("p (j q) -> p j q", j=4), xs4, bc4(dskip[:, 4 * g:4 * g + 4]), ALU.mult,
                   [("xsB", t), "smallc"], [("xsD", p)], eng=EL)

            def h2a(t):
                p = t % 2
                cols = slice(t * 128, (t + 1) * 128)
                for j in range(4):
                    js = slice(64 * j, 64 * (j + 1))
                    MM(psY[:, js], ident_bf, xsD[p][:, js], True, False, ["cst", ("xsD", p)], ["psY"])
                    MM(psY[:, js], GT[0][p][:, j, :], xdt[0][p][:, js], False, False, [("GT", 0, p), ("xdt", 0, p)], ["psY"])
                    MM(psY[:, js], GT[1][p][:, j, :], xdt[1][p][:, js], False, True, [("GT", 1, p), ("xdt", 1, p)], ["psY"])
                MM(psO[:, 0:256], xbcT[:, 3, cols], Ebf0, True, True, [("xbcT", 3), "Ebf0"], ["psO0"])
                MM(psO[:, 256:512], xbcT[:, 3, cols], Eb_all[:, t, :], True, True, [("xbcT", 3), ("Eb", t)], ["psO1"])
                ss_a(g, t, 0, 0)
                ss_b(g, t, 0, 0, False)
                CP("act", Ebf0, E[0], [("E", 0)], ["Ebf0"])
                TT(T0.rearrange("p (j q) -> p j q", j=4), psO[:, 0:256].rearrange("p (j q) -> p j q", j=4),
                   bc4(expA[:, t, 4 * g:4 * g + 4]), ALU.mult, ["psO0", ("expA", t)], ["T0"])
                TT(T1.rearrange("p (j q) -> p j q", j=4), psO[:, 256:512].rearrange("p (j q) -> p j q", j=4),
                   bc4(expA[:, t, 32 + 4 * g:32 + 4 * g + 4]), ALU.mult, ["psO1", ("expA", t)], ["T1"])
                TT(T0, T0, T1, ALU.add, ["T0", "T1"], ["T0"])
                TT(T0, T0, psY, ALU.add, ["T0", "psY"], ["T0"])
                TT(YG, T0, SZ[p], ALU.mult, ["T0", ("SZ", p)], ["YG"])
                ACTF(T1, YG, AF.Square, ["YG"], ["T1", ("ss2", t)], accum=ss2[:, t:t + 1])
                ACTF(r2[:, t:t + 1], ss2[:, t:t + 1], AF.Sqrt, [("ss2", t), "epsc"], [("r2", t)], bias=epsc[:, 0:1], scale=1.0 / 256)

            def h2b(t):
                cols = slice(t * 128, (t + 1) * 128)
                RCP(r2[:, t:t + 1], r2[:, t:t + 1], [("r2", t)], [("r2", t)])
                TS(YN, YG, r2[:, t:t + 1], ALU.mult, ["YG", ("r2", t)], ["YN"])
                for i in range(2):
                    TR(psT[:, 4 + i, :], YN[:, i * 128:(i + 1) * 128], ident_bf, ["YN", "cst"], ["psT45"])
                for i in range(2):
                    ACTF(yTg[:, i, cols], psT[:, 4 + i, :], AF.Identity, ["psT45", "smallc"], ["yTg"],
                         scale=ssdg[:, 2 * g + i:2 * g + i + 1])

            h1a(0)
            h1b(0)
            for t in range(16):
                if t + 1 < 16:
                    h1a(t + 1)
                h2a(t)
                if t + 1 < 16:
                    h1b(t + 1)
                h2b(t)
            for i in range(2):
                DMA("sp", yT_d[2 * g + i, :, :], yTg[:, i, :], ["yTg"], ["yT_d"])

        for g in range(NG):
            ssd_pass2(g)
        if debug == "ssd":
            dd = dout("yT", [16, 128, NT], BF16)
            P.fence()
            tb_ = Bump(100 * 1024)([128, NT])
            for i in range(2 * NG):
                DMA("sp", tb_, yT_d[i, :, :], [], ["tb_"])
                DMA("sp", dd[i, :, :], tb_, ["tb_"], ["dd"])
            STOP[0] = True
        P.fence()

    if not STOP[0]:
        B = Bump()
        xt = [B([128, D], F32) for _ in range(2)]
        xn = [B([128, D], BF16) for _ in range(2)]
        junk = B([128, D], BF16)
        fgb = B([128, D], F32)
        xnew = [B([128, D], F32) for _ in range(4)]
        h2T = B([128, 8, 512])
        BASE = B.off
        B1 = Bump(BASE)
        wo = B1([128, 8, 1024])
        w8 = WPool([B1([128, 8, 128]) for _ in range(6)], "w8")
        w16 = WPool([B1([128, 16, 128]) for _ in range(2)], "w16")
        yTb = B1([128, 16, 512])
        fmT = B1([128, 8, 512])
        ftl = [B1([128, 1024]) for _ in range(2)]
        mergedT = B1([128, 8, 512])
        S0 = B1([128, 512], F32)
        S1 = B1([128, 512], F32)
        M0 = B1([128, 512], F32)
        M1 = B1([128, 512], F32)
        TMP = B1([128, 512], F32)
        B2 = Bump(BASE)
        wfi = WPool([B2([128, 8, 128]) for _ in range(4)], "wfi")
        wfo = [B2([128, 22, 512]) for _ in range(2)]
        actT = B2([128, 22, 512])
        SA = [B2([128, 512], F32) for _ in range(2)]
        xo = [B2([128, D], F32) for _ in range(4)]
        TMP2 = B2([128, 512], F32)
        ot = [B2([128, D], F32) for _ in range(2)]

        fg_d = din("fg_bc", [128, D])
        wssd_t = din("w_ssd_t", [8, 128, 16 * 128])
        wfft_t = din("w_fft_t", [8, 128, 8 * 128])
        wgate_t = din("w_gate_t", [16, 128, 8 * 128])
        wo_d = din("w_o", [D, D])
        wfi_t = din("w_ffn_in_t", [44, 128, 8 * 128])
        wfo_d = din("w_ffn_out", [D_FF, D])
        DMA("sp", fgb, fg_d[:, :], [], ["fgb"])
        if debug == "pd":
            yin = din("dbg_yT_in", [16, 128, NT], BF16)
            rin = din("dbg_rs_in", [32, 65536], BF16)
            tb_ = Bump(100 * 1024)([128, 16384])
            for i in range(16):
                DMA("sp", tb_[:, 0:NT], yin[i, :, :], [], ["tb_"])
                DMA("sp", yT_d[i, :, :], tb_[:, 0:NT], ["tb_"], ["yT_d"])
            for i in range(4):
                DMA("sp", tb_[0:32, :], rin[:, i * 16384:(i + 1) * 16384], [], ["tb_"])
                DMA("sp", rs_d.ap()[:, i * 16384:(i + 1) * 16384], tb_[0:32, :], ["tb_"], ["rs_d"])
            P.fence()
        wo_v = wo_d.rearrange("(k p) n -> p k n", p=128)
        wfo_v = wfo_d.rearrange("(j p) n -> p j n", p=128)
        yT_v = yT_d.rearrange("k p t -> p k t")
        wc16 = nc.dram_tensor("wc16", [8, 128, 2048], BF16).ap()
        wc8 = nc.dram_tensor("wc8", [24, 128, 1024], BF16).ap()
        wcfi = nc.dram_tensor("wcfi", [44, 128, 1024], BF16).ap()
        wco = nc.dram_tensor("wco", [128, 8192], BF16).ap().rearrange("p (k c) -> p k c", k=8)
        wcfo = nc.dram_tensor("wcfo", [2, 128, 22 * 512], BF16).ap()

        def cload(pool, src, cache, tb):
            if tb == 0:
                buf, key = pool.load(src)
                DMA("sp", cache, buf.rearrange("p k c -> p (k c)"), [key], ["wcache"])
                return buf, key
            i = pool.i % len(pool.bufs)
            pool.i += 1
            buf, key = pool.bufs[i], "%s%d" % (pool.name, i)
            DMA("sp", buf.rearrange("p k c -> p (k c)"), cache, [], [key])
            return buf, key

        for tb in range(int(os.environ.get("KTB", "4"))):
            tcols = slice(tb * 512, (tb + 1) * 512)
            DMA("sp", yTb, yT_v[:, :, tcols], [], ["yTb"])
            for hf in range(2):
                hs_ = slice(hf * 512, (hf + 1) * 512)
                if tb == 0:
                    DMA("pool", wo[:, :, hs_], wo_v[:, :, hs_], [], [("wo", hf)])
                    DMA("sp", wco[:, :, hs_], wo[:, :, hs_], [("wo", hf)], ["wcache"])
                else:
                    DMA("sp", wo[:, :, hs_], wco[:, :, hs_], [], [("wo", hf)])
            for tt in range(4):
                t = 4 * tb + tt
                ft, fk = ftl[tt % 2], "ftl%d" % (tt % 2)
                DMA("sp", ft, rs_d.ap()[2 * t:2 * t + 2, :].rearrange("r (c k) -> (r c) k", k=1024), ["rs_d"], [fk])
                for k in range(8):
                    TR(psT[:, k, :], ft[:, k * 128:(k + 1) * 128], ident_bf, [fk, "cst"], ["psT"])
                CP("act", fmT[:, :, tt * 128:(tt + 1) * 128], psT[:, :, :], ["psT"], ["fmT"])
            for fc in range(8):
                wss, wssk = cload(w16, wssd_t[fc].rearrange("p (k c) -> p k c", k=16), wc16[fc], tb)
                wff, wffk = cload(w8, wfft_t[fc].rearrange("p (k c) -> p k c", k=8), wc8[fc], tb)
                wg0, wg0k = cload(w8, wgate_t[fc].rearrange("p (k c) -> p k c", k=8), wc8[8 + fc], tb)
                wg1, wg1k = cload(w8, wgate_t[8 + fc].rearrange("p (k c) -> p k c", k=8), wc8[16 + fc], tb)
                for k in range(16):
                    MM(psA[0][:, :], wss[:, k, :], yTb[:, k, :], k == 0, k == 15, [wssk, "yTb"], ["psA0"])
                for k in range(8):
                    MM(psA[1][:, :], wff[:, k, :], fmT[:, k, :], k == 0, k == 7, [wffk, "fmT"], ["psA1"])
                for k in range(8):
                    MM(psA[2][:, :], wg0[:, k, :], hT[:, k, tcols], k == 0, k == 7, [wg0k] + hk(tb), ["psA2"])
                for k in range(8):
                    MM(psA[3][:, :], wg1[:, k, :], hT[:, k, tcols], k == 0, k == 7, [wg1k] + hk(tb), ["psA3"])
                ACTF(S0, psA[2][:, :], AF.Sigmoid, ["psA2"], ["S0"])
                ACTF(S1, psA[3][:, :], AF.Sigmoid, ["psA3"], ["S1"])
                TT(M0, S0, psA[0][:, :], ALU.mult, ["S0", "psA0"], ["M0"])
                TT(M1, S1, psA[1][:, :], ALU.mult, ["S1", "psA1"], ["M1"])
                TT(mergedT[:, fc, :], M0, M1, ALU.add, ["M0", "M1"], ["mergedT"])
            for tt in range(4):
                t = 4 * tb + tt
                xi = tt % 2
                xk = "xt%d" % xi
                DMA("sp", xt[xi], x_d[t * 128:(t + 1) * 128, :], [], [xk])
                for hf in range(2):
                    pst, pk = psA[4 + hf], "psA%d" % (4 + hf)
                    hs = slice(hf * 512, (hf + 1) * 512)
                    for k in range(8):
                        MM(pst[:, :], mergedT[:, k, tt * 128:(tt + 1) * 128], wo[:, k, hs], k == 0, k == 7,
                           ["mergedT", ("wo", hf)], [pk])
                    TT(TMP, pst[:, :], gbc[:, hs], ALU.mult, [pk, "gbc"], ["TMP"])
                    TT(xnew[tt][:, hs], TMP, xt[xi][:, hs], ALU.add, ["TMP", xk], [("xnew", tt)])
                norm_core(20 + tt, xnew[tt], ("xnew", tt), 128, lambda k, tt=tt: h2T[:, k, tt * 128:(tt + 1) * 128], 0, a2, 24,
                          ["h2T"])
            P.fence()
            def wfo_load(hf):
                if tb == 0:
                    DMA("pool", wfo[hf], wfo_v[:, :, hf * 512:(hf + 1) * 512], [], [("wfo", hf)])
                    DMA("sp", wcfo[hf], wfo[hf].rearrange("p j c -> p (j c)"), [("wfo", hf)], ["wcache"])
                else:
                    DMA("sp", wfo[hf].rearrange("p j c -> p (j c)"), wcfo[hf], [], [("wfo", hf)])
            wfo_load(0)
            for j in range(22):
                if j == 8:
                    wfo_load(1)
                wa, wak = cload(wfi, wfi_t[j].rearrange("p (k c) -> p k c", k=8), wcfi[j], tb)
                wb, wbk = cload(wfi, wfi_t[22 + j].rearrange("p (k c) -> p k c", k=8), wcfi[22 + j], tb)
                pa, pak = psA[2 * (j % 2)], "psA%d" % (2 * (j % 2))
                pb, pbk = psA[2 * (j % 2) + 1], "psA%d" % (2 * (j % 2) + 1)
                for k in range(8):
                    MM(pa[:, :], wa[:, k, :], h2T[:, k, :], k == 0, k == 7, [wak, "h2T"], [pak])
                for k in range(8):
                    MM(pb[:, :], wb[:, k, :], h2T[:, k, :], k == 0, k == 7, [wbk, "h2T"], [pbk])
                sa, sak = SA[j % 2], "SA%d" % (j % 2)
                ACTF(sa, pa[:, :], AF.Silu, [pak], [sak])
                TT(actT[:, j, :], sa, pb[:, :], ALU.mult, [sak, pbk], ["actT"])
            for hf in range(2):
                hs = slice(hf * 512, (hf + 1) * 512)
                for tt in range(4):
                    pst, pk = psA[4 + tt % 2], "psA%d" % (4 + tt % 2)
                    for j in range(22):
                        MM(pst[:, :], actT[:, j, tt * 128:(tt + 1) * 128], wfo[hf][:, j, :], j == 0, j == 21,
                           ["actT", ("wfo", hf)], [pk])
                    TT(TMP2, pst[:, :], gbc[:, 1024 + hf * 512:1024 + (hf + 1) * 512], ALU.mult, [pk, "gbc"], ["TMP2"])
                    TT(xo[tt][:, hs], TMP2, xnew[tt][:, hs], ALU.add, ["TMP2", ("xnew", tt)], [("xo", tt)])
            for tt in range(4):
                t = 4 * tb + tt
                ti = 24 + tt
                oi = tt % 2
                ACTF(junk, xo[tt], AF.Square, [("xo", tt)], ["junk", ("ssq", ti)], accum=ssq[:, ti:ti + 1])
                ACTF(rstd[:, ti:ti + 1], ssq[:, ti:ti + 1], AF.Sqrt, [("ssq", ti), "epsc"], [("rstd", ti)],
                     bias=epsc[:, 0:1], scale=1.0 / D)
                RCP(rstd[:, ti:ti + 1], rstd[:, ti:ti + 1], [("rstd", ti)], [("rstd", ti)])
                STT(ot[oi], xo[tt], rstd[:, ti:ti + 1], fgb, ALU.mult, ALU.mult, [("xo", tt), ("rstd", ti), "fgb"], [("ot", oi)])
                DMA("sp", out_d[t * 128:(t + 1) * 128, :], ot[oi], [("ot", oi)], ["out"])
            P.fence()

    P.emit(es)
    es.close()
    P.in_names = in_names
    return nc, P


def _consts(q):
    c = np.zeros((128, NCST), np.float32)
    i = np.arange(128)
    c[:, 0:128] = np.eye(128)
    c[:, 128:256] = (i[:, None] <= i[None, :])
    c[:, 256:384] = (i[:, None] >= i[None, :])
    c[:, 384:512] = 1.0
    mf = np.where(i[:, None] <= i[None, :], 0.0, -30000.0)
    mb = np.where(i[:, None] >= i[None, :], 0.0, -30000.0)
    c[:, 512:1024] = np.tile(mf, (1, 4))
    c[:, 1024:1536] = np.tile(mb, (1, 4))
    cc = np.arange(64)
    ang = 2 * np.pi * np.outer(cc, cc) / 64.0
    w2c = np.zeros((128, 128))
    w2s = np.zeros((128, 128))
    for r2 in range(2):
        w2c[r2 * 64:(r2 + 1) * 64, r2 * 64:(r2 + 1) * 64] = np.cos(ang) / 8.0
        w2s[r2 * 64:(r2 + 1) * 64, r2 * 64:(r2 + 1) * 64] = np.sin(ang) / 8.0
    c[:, 1536:1664] = w2c
    c[:, 1664:1792] = w2s
    angc = 2 * np.pi * np.outer(i, i) / 128.0
    Cc = np.cos(angc) / math.sqrt(128.0)
    Sc = np.sin(angc) / math.sqrt(128.0)
    c[:, 1792:1920] = Cc
    c[:, 1920:2048] = Sc
    c[:, 2048:2176] = -Sc
    c[:, 2176:2304] = Cc
    rl = np.arange(32)
    angr = 2 * np.pi * np.outer(32 * q + rl, i) / 128.0
    blk = np.zeros((128, 4, 2, 128))
    for c4 in range(4):
        blk[c4 * 32:(c4 + 1) * 32, c4, 0, :] = np.cos(angr) / math.sqrt(128.0)
        blk[c4 * 32:(c4 + 1) * 32, c4, 1, :] = -np.sin(angr) / math.sqrt(128.0)
    c[:, 2304:3328] = blk.reshape(128, 1024)
    return c


def _tiles(w):
    kk, nb = w.shape[0] // 128, w.shape[1] // 128
    return np.ascontiguousarray(w.reshape(kk, 128, nb, 128).transpose(2, 1, 0, 3).reshape(nb, 128, kk * 128))


def _prep_inputs(inp):
    f32 = np.float32
    x = np.asarray(inp["x"], f32)
    ctx = np.asarray(inp["ctx"], f32)
    c = np.asarray(inp["c"], f32)
    c_ctx = np.asarray(inp["c_ctx"], f32)
    b_ada = np.asarray(inp["b_ada"], f32)[0]
    conv_w = np.asarray(inp["conv_w"], f32)[0]
    shared = {
        "w_ada": np.ascontiguousarray(inp["w_ada"][0], f32),
        "b_ada_row": np.ascontiguousarray(b_ada.reshape(1, -1)),
        "w_in": np.ascontiguousarray(inp["w_in"][0], f32),
        "identc": np.eye(128, dtype=f32),
        "convw_fm": np.ascontiguousarray(conv_w.reshape(5, 32, 128).transpose(2, 1, 0).reshape(128, 160)),
        "fg_bc": np.ascontiguousarray(np.broadcast_to(np.asarray(inp["final_g"], f32)[None, :], (128, D))),
        "w_ssd_t": _tiles(np.asarray(inp["w_ssd_out"][0], f32)),
        "w_fft_t": _tiles(np.asarray(inp["w_fft_out"][0], f32)),
        "w_gate_t": _tiles(np.asarray(inp["w_in"][0], f32)[:, GATE_START:GATE_START + 2048]),
        "w_o": np.ascontiguousarray(inp["w_o"][0], f32),
        "w_ffn_in_t": _tiles(np.asarray(inp["w_ffn_in"][0], f32)),
        "w_ffn_out": np.ascontiguousarray(inp["w_ffn_out"][0], f32),
    }
    sm = np.zeros((128, NSMALL), f32)
    sm[:, 0:48] = b_ada.reshape(48, 128).T
    sm[:, 48:56] = np.asarray(inp["norm1_g"], f32)[0].reshape(8, 128).T
    sm[:, 56:64] = np.asarray(inp["norm2_g"], f32)[0].reshape(8, 128).T
    sm[:, 64:96] = np.asarray(inp["conv_b"], f32)[0].reshape(32, 128).T
    sm[:, 96:112] = np.asarray(inp["ssd_norm_g"], f32)[0].reshape(16, 128).T
    sm[:, 112:144] = np.asarray(inp["d_skip"], f32)[0][None, :]
    sm[:, 404:468] = np.asarray(inp["dt_bias"], f32)[0].reshape(1, 64)
    sm[:, 468:532] = np.asarray(inp["a_log"], f32)[0].reshape(1, 64)
    maps = []
    for core in range(8):
        b, q = core // 4, core % 4
        t0 = q * NT
        m = dict(shared)
        m["x"] = np.ascontiguousarray(x[b, t0:t0 + NT])
        m["ctx"] = np.ascontiguousarray(ctx[b])
        xh = np.zeros((4, D), f32)
        for i, tt in enumerate((t0 - 2, t0 - 1, t0 + NT, t0 + NT + 1)):
            if 0 <= tt < 8192:
                xh[i] = x[b, tt]
        m["xhalo"] = xh
        cv = np.zeros((128, 16), f32)
        cv[:, 0:8] = c[b].reshape(8, 128).T
        cv[:, 8:16] = c_ctx.reshape(8, 128).T
        m["cvec"] = cv
        s = sm.copy()
        s[:, 144] = 0.0 if q == 0 else 1.0
        s[:, 145] = 0.0 if q == 3 else 1.0
        cm = np.zeros((4, 64), f32)
        for r in range(4):
            cm[r, 0:32] = 1.0 if r < q else 0.0
            cm[r, 32:64] = 1.0 if r > q else 0.0
        s[:, 148:404] = cm.reshape(1, 256)
        m["smallc"] = s
        m["cst"] = _consts(q)
        maps.append(m)
    return maps


_CACHE = {}


def kernel(**inp):
    if "nc" not in _CACHE:
        _CACHE["nc"] = build()[0]
    nc = _CACHE["nc"]
    maps = _prep_inputs(inp)
    res = run_bass_kernel_spmd(nc, maps, core_ids=list(range(8)))
    out = np.zeros((2, 8192, D), np.float32)
    for core in range(8):
        b, q = core // 4, core % 4
        out[b, q * NT:(q + 1) * NT] = np.asarray(res.results[core]["out"], np.float32)
    return out
```

```python
import math
import os
from contextlib import ExitStack
import numpy as np
import ml_dtypes
import concourse.bass as bass
import concourse.mybir as mybir
from concourse.bass_utils import run_bass_kernel_spmd

F32 = mybir.dt.float32
BF16 = mybir.dt.bfloat16
AF = mybir.ActivationFunctionType
ALU = mybir.AluOpType
AX = mybir.AxisListType

D = 1024
NCST = 3328
NSMALL = 532
NT = 2048
NTILE = 16
CTX = 256
EPS = 1e-6
HW = NT + CTX + 4
HALO0 = NT + CTX
DT_START = 4096
Z_START = 4160
FFT_START = Z_START + 2048
GATE_START = FFT_START + 1024
D_FF = 2816
UW = 2312
CO = 2308
CTXO = 2052
GROUPS = [[0, 1, 2, 3], [4, 5, 6, 7]]


class _Op:
    pass


class Prog:
    def __init__(self, nc, n_dma=32):
        self.nc = nc
        self.ops = []
        self.lastw = {}
        self.rd_eng = {}
        self.rd_dma = {}
        self.n_dma = n_dma
        self.rr = 0
        self.rr_pool = 0
        self.last_on = [None] * n_dma
        self.last_eng = {}
        self.dma_since = []
        self.fence_dep = None
        self.cc_w = {}

    PSKEY = {"psC": "psA0", "psZ": "psA6", "psY": "psA1", "psS0": "psA5", "psS1": "psA6", ("psD", 0): "psA2",
             ("psD", 1): "psA3", "psO0": "psA4", "psO1": "psA4", "psDa": "psA6", "psDb": "psA6", "psDt": "psA6",
             "psE": "psA5", "psT03": "psT", "psT45": "psT", "psT7": "psT"}

    def add(self, eng, fn, reads=(), writes=(), dma=False, cc=False):
        reads = [self.PSKEY.get(k, k) for k in reads]
        writes = [self.PSKEY.get(k, k) for k in writes]
        op = _Op()
        op.eng, op.fn, op.dma, op.cc = eng, fn, dma, cc
        op.idx = len(self.ops)
        deps = set()
        for k in reads:
            w = self.lastw.get(k)
            if w is not None:
                deps.add(w)
            if isinstance(k, str) and k.startswith("ps"):
                for e2, i2 in self.rd_eng.get(k, {}).items():
                    if e2 != eng:
                        deps.add(i2)
        for k in writes:
            w = self.lastw.get(k)
            if w is not None:
                deps.add(w)
            deps.update(self.rd_eng.get(k, {}).values())
            deps.update(self.rd_dma.get(k, ()))
        if dma:
            if eng == "pool":
                s = self.n_dma - 8 + (self.rr_pool % 8)
                self.rr_pool += 1
            else:
                s = self.rr % (self.n_dma - 8)
                self.rr += 1
            op.dsem = s
            if self.last_on[s] is not None:
                deps.add(self.last_on[s])
            self.last_on[s] = op.idx
        if self.fence_dep is not None:
            deps.add(self.fence_dep)
        for k in list(reads) + list(writes):
            if k in self.cc_w:
                deps.add(self.cc_w[k])
        if cc:
            for k in writes:
                self.cc_w[k] = op.idx
        op.deps = deps
        if dma or cc:
            self.dma_since.append(op.idx)
        else:
            self.last_eng[eng] = op.idx
        for k in reads:
            if dma or cc:
                self.rd_dma.setdefault(k, []).append(op.idx)
            else:
                self.rd_eng.setdefault(k, {})[eng] = op.idx
        for k in writes:
            self.lastw[k] = op.idx
            self.rd_eng[k] = {}
            self.rd_dma[k] = []
        self.ops.append(op)
        return op

    def fence(self):
        deps = set(self.last_eng.values())
        deps.update(i for i in self.dma_since if not self.ops[i].cc)
        op = self.add("sp", lambda e: e.nop())
        op.deps |= deps
        self.fence_dep = op.idx
        self.dma_since = []
        self.lastw = {}
        self.rd_eng = {}
        self.rd_dma = {}
        self.last_on = [None] * self.n_dma

    def emit(self, es):
        nc = self.nc
        ops = self.ops
        engs = ("pe", "act", "dve", "pool", "sp")
        needs = [False] * len(ops)
        for op in ops:
            for d in op.deps:
                dd = ops[d]
                if dd.dma or dd.cc:
                    continue
                if dd.eng == "pe" and op.eng == "pe" and not (op.dma or op.cc):
                    continue
                needs[d] = True
        esem = {e: es.enter_context(nc.semaphore("s_" + e)) for e in engs}
        dsem = [es.enter_context(nc.semaphore("d%d" % i)) for i in range(self.n_dma)]
        ncc = sum(1 for op in ops if op.cc)
        ccsems = [es.enter_context(nc.semaphore("ccs%d" % i)) for i in range(ncc)]
        cnt = {e: 0 for e in engs}
        dcnt = [0] * self.n_dma
        cccnt = 0
        for op in ops:
            if op.cc:
                op.ev = (ccsems[cccnt], 1)
                cccnt += 1
            elif op.dma:
                dcnt[op.dsem] += 16
                op.ev = (dsem[op.dsem], dcnt[op.dsem])
            elif needs[op.idx]:
                cnt[op.eng] += 1
                op.ev = (esem[op.eng], cnt[op.eng])
            else:
                op.ev = None
        per = {e: [] for e in engs}
        for op in ops:
            per[op.eng].append(op)
        self.stats = {e: len(per[e]) for e in engs}

        def run(ename, eng):
            seen = {}
            for op in per[ename]:
                waits = {}
                for d in op.deps:
                    dd = ops[d]
                    if dd.eng == "pe" and ename == "pe" and not (dd.dma or dd.cc) and not (op.dma or op.cc):
                        continue
                    sem, val = dd.ev
                    key = id(sem)
                    if seen.get(key, 0) >= val:
                        continue
                    if key not in waits or waits[key][1] < val:
                        waits[key] = (sem, val)
                for key, (sem, val) in waits.items():
                    eng.wait_ge(sem, val)
                    seen[key] = val
                ins = op.fn(eng)
                if op.cc:
                    ins.then_inc(op.ev[0])
                elif op.dma:
                    ins.then_inc(op.ev[0], 16)
                elif op.ev is not None:
                    ins.then_inc(op.ev[0], 1)
            if ename == "sp":
                for i in range(self.n_dma):
                    if dcnt[i]:
                        eng.wait_ge(dsem[i], dcnt[i])
                for cs in ccsems:
                    eng.wait_ge(cs, 1)

        with nc.Block() as block:
            @block.tensor
            def _(e):
                run("pe", e)

            @block.scalar
            def _(e):
                run("act", e)

            @block.vector
            def _(e):
                run("dve", e)

            @block.gpsimd
            def _(e):
                run("pool", e)

            @block.sync
            def _(e):
                run("sp", e)


ARENA = 150 * 1024


def build(debug=None):
    nc = bass.Bass("TRN2", target_bir_lowering=False)
    es = ExitStack()
    P = Prog(nc)

    in_names = []

    def din(name, shape, dt=F32):
        in_names.append(name)
        return nc.dram_tensor(name, list(shape), dt, kind="ExternalInput").ap()

    x_d = din("x", [NT, D])
    ctx_d = din("ctx", [CTX, D])
    xh_d = din("xhalo", [4, D])
    cvec_d = din("cvec", [128, 16])
    wada_d = din("w_ada", [D, 6 * D])
    bada_row_d = din("b_ada_row", [1, 6 * D])
    win_d = din("w_in", [D, 9280])
    identc_d = din("identc", [128, 128])
    cst_d = din("cst", [128, NCST])
    smallc_d = din("smallc", [128, NSMALL])
    convw_d = din("convw_fm", [128, 160])
    out_d = nc.dram_tensor("out", [NT, D], F32, kind="ExternalOutput").ap()
    cparts = [nc.dram_tensor("contribA", [128, 2048], F32), nc.dram_tensor("contribB", [128, 2048], F32),
              nc.dram_tensor("contribC", [128, 64], F32)]
    gparts = [nc.dram_tensor("gathA", [512, 2048], F32), nc.dram_tensor("gathB", [512, 2048], F32),
              nc.dram_tensor("gathC", [512, 64], F32)]

    def contrib_ap(g, d):
        return cparts[g // 4].ap()[:, ((g % 4) * 2 + d) * 256:((g % 4) * 2 + d + 1) * 256]

    def gath_ap(g, d):
        return gparts[g // 4].ap().rearrange("(r n) c -> n r c", n=128)[:, :, ((g % 4) * 2 + d) * 256:((g % 4) * 2 + d + 1) * 256]
    sctx_d = nc.dram_tensor("sctx", [128, 4096], F32).ap()
    part_d = nc.dram_tensor("partd", [128, 65536], BF16)
    rs_d = nc.dram_tensor("rsd", [32, 65536], BF16)
    yT_d = nc.dram_tensor("yTd", [16, 128, NT], BF16).ap()
    dbg = {}

    def dout(name, shape, dt):
        dbg[name] = nc.dram_tensor("dbg_" + name, list(shape), dt, kind="ExternalOutput").ap()
        return dbg[name]

    def sb(name, shape, dt=F32):
        return es.enter_context(nc.sbuf_tensor(name, list(shape), dt))

    def ps(name, shape, dt=F32):
        return es.enter_context(nc.psum_tensor(name, list(shape), dt))

    arena = sb("arena", [128, ARENA // 2], BF16)

    class Bump:
        def __init__(self, base=0):
            self.off = base

        def __call__(self, shape, dt=BF16):
            n = int(np.prod(shape[1:]))
            esz = 2 if dt == BF16 else 4
            nb = (n * esz + 31) // 32 * 32
            assert self.off + nb <= ARENA, ("arena overflow", self.off + nb)
            v = arena[0:shape[0], self.off // 2: self.off // 2 + n * esz // 2]
            self.off += nb
            if dt != BF16:
                v = v.bitcast(dt)
            if len(shape) == 3:
                v = v.rearrange("p (a b) -> p a b", a=shape[1])
            elif len(shape) == 4:
                v = v.rearrange("p (a b c) -> p a b c", a=shape[1], b=shape[2])
            return v

    def MM(out, lhsT, rhs, start, stop, reads, writes):
        P.add("pe", lambda e: e.matmul(out, lhsT=lhsT, rhs=rhs, start=start, stop=stop), reads, writes)

    def TR(out, in_, ident, reads, writes):
        P.add("pe", lambda e: e.transpose(out=out, in_=in_, identity=ident), reads, writes)

    def ACTF(out, in_, func, reads, writes, bias=None, scale=None, accum=None):
        kw = {}
        if bias is not None:
            kw["bias"] = bias
        if scale is not None:
            kw["scale"] = scale
        if accum is not None:
            kw["accum_out"] = accum
        P.add("act", lambda e: e.activation(out=out, in_=in_, func=func, **kw), reads, writes)

    def CP(eng, out, in_, reads, writes):
        if eng == "act":
            P.add("act", lambda e: e.copy(out=out, in_=in_), reads, writes)
        else:
            P.add(eng, lambda e: e.tensor_copy(out=out, in_=in_), reads, writes)

    def TT(out, in0, in1, op, reads, writes, eng="dve"):
        P.add(eng, lambda e: e.tensor_tensor(out=out, in0=in0, in1=in1, op=op), reads, writes)

    def TS(out, in0, s1, op0, reads, writes, s2=None, op1=None, eng="dve"):
        if op1 is None:
            P.add(eng, lambda e: e.tensor_scalar(out=out, in0=in0, scalar1=s1, scalar2=None, op0=op0), reads, writes)
        else:
            P.add(eng, lambda e: e.tensor_scalar(out=out, in0=in0, scalar1=s1, scalar2=s2, op0=op0, op1=op1), reads, writes)

    def STT(out, in0, scalar, in1, op0, op1, reads, writes, eng="dve"):
        P.add(eng, lambda e: e.scalar_tensor_tensor(out=out, in0=in0, scalar=scalar, in1=in1, op0=op0, op1=op1),
              reads, writes)

    def RCP(out, in_, reads, writes):
        P.add("dve", lambda e: e.reciprocal(out=out, in_=in_), reads, writes)

    def DMA(q, out, in_, reads, writes):
        P.add(q, lambda e: e.dma_start(out=out, in_=in_), reads, writes, dma=True)

    def MEMSET(ap, val, writes):
        P.add("dve", lambda e: e.memset(ap, val), (), writes)

    def bc4(ap4, n=64):
        return ap4.unsqueeze(2).to_broadcast([128, 4, n])

    ident_f = sb("ident_f", [128, 128], F32)
    ones_f = sb("ones_f", [128, 128], F32)
    epsc = sb("epsc", [128, 2])
    cst = sb("cst_sb", [128, NCST], BF16)
    smallc = sb("smallc_sb", [128, NSMALL])
    convw = sb("convw_sb", [128, 160])
    mods = sb("mods", [128, 48, 2])
    a1 = sb("a1", [128, 8, 2])
    a2 = sb("a2", [128, 8, 2])
    gbc = sb("gbc", [128, 2048])
    hT = sb("hT", [128, 8, HW], BF16)
    ssq = sb("ssq", [128, 32])
    rstd = sb("rstd", [128, 32])
    ident_bf = cst[:, 0:128]
    triU = cst[:, 128:256]
    triL = cst[:, 256:384]
    ones_bf = cst[:, 384:512]
    mask4 = cst[:, 512:1536].rearrange("p (d n) -> p d n", d=2)
    W2 = cst[:, 1536:1792]
    Wch = cst[:, 1792:2304]
    CrBlk = cst[:, 2304:3328].rearrange("p (c a k) -> p c a k", c=4, a=2)
    bada_fm = smallc[:, 0:48]
    n1g = smallc[:, 48:56]
    n2g = smallc[:, 56:64]
    convb = smallc[:, 64:96]
    ssdg = smallc[:, 96:112]
    dskip = smallc[:, 112:144]
    hmask = smallc[:, 144:146]
    dtb = smallc[:, 146:147]
    alog = smallc[:, 147:148]
    cmask = smallc[:, 148:404].rearrange("p (r c) -> p r c", r=4)
    dtb_bc = smallc[:, 404:468]
    alog_bc = smallc[:, 468:532]

    psA = [ps("psA%d" % i, [128, 512]) for i in range(7)]
    psT = ps("psT", [128, 8, 128], BF16)

    DMA("sp", ident_f[:], identc_d[:, :], [], ["ident_f"])
    DMA("pool", cst[:], cst_d[:, :], [], ["cst"])
    DMA("sp", smallc[:], smallc_d[:, :], [], ["smallc"])
    DMA("sp", convw[:], convw_d[:, :], [], ["convw"])
    MEMSET(ones_f[:], 1.0, ["ones_f"])
    MEMSET(epsc[:, 0:1], EPS, ["epsc"])
    MEMSET(epsc[:, 1:2], 1.0, ["epsc"])

    win_v = win_d.rearrange("(k p) n -> p k n", p=128)

    B = Bump()
    wada = [B([128, 8, 512], F32) for _ in range(2)]
    cv = B([128, 16], F32)
    scv = B([128, 16], F32)
    screp = B([128, 8, 128], F32)
    bada_row = B([1, 2048], F32)
    xt = [B([128, D], F32) for _ in range(3)]
    xn = [B([128, D], BF16) for _ in range(2)]
    junk = B([128, D], BF16)

    DMA("sp", cv, cvec_d[:, :], [], ["cv"])
    DMA("sp", bada_row[:, 0:1024], bada_row_d[:, 2048:3072], [], ["bada_row"])
    DMA("sp", bada_row[:, 1024:2048], bada_row_d[:, 5120:6144], [], ["bada_row"])
    ACTF(scv, cv, AF.Silu, ["cv"], ["scv"])
    CP("dve", screp, scv[:, 0:8].unsqueeze(2).to_broadcast([128, 8, 128]), ["scv"], ["screp"])
    wada_v = wada_d.rearrange("(k p) n -> p k n", p=128)
    scv3 = scv.rearrange("p (v k) -> p k v", v=2)
    mods_ps = psA[3]
    blk_order = [0, 1, 2, 3, 6, 7, 8, 9, 4, 5, 10, 11]
    for bi, blk in enumerate(blk_order):
        wt = wada[bi % 2]
        wk = "wada%d" % (bi % 2)
        DMA("sp", wt, wada_v[:, :, blk * 512:(blk + 1) * 512], [], [wk])
        if blk in (4, 5, 10, 11):
            gi = {4: 0, 5: 1, 10: 2, 11: 3}[blk]
            pst = psA[gi % 2]
            pk = "psA%d" % (gi % 2)
            for k in range(8):
                MM(pst[:, :], screp[:, k, :], wt[:, k, :], k == 0, False, ["screp", wk], [pk])
            MM(pst[:, :], ones_f[0:1, :], bada_row[0:1, gi * 512:(gi + 1) * 512], False, True, ["ones_f", "bada_row"], [pk])
            CP("act", gbc[:, gi * 512:(gi + 1) * 512], pst[:, :], [pk], ["gbc"])
        else:
            for jj in range(4):
                j = blk * 4 + jj
                for k in range(8):
                    MM(mods_ps[:, 2 * j:2 * j + 2], wt[:, k, jj * 128:(jj + 1) * 128], scv3[:, k, :], k == 0, k == 7,
                       ["scv", wk], ["psA3"])
    mps3 = mods_ps[:, 0:96].rearrange("p (j v) -> p j v", v=2)
    for lo in (0, 24):
        TT(mods[:, lo:lo + 16, :], mps3[:, lo:lo + 16, :], bada_fm[:, lo:lo + 16].unsqueeze(2).to_broadcast([128, 16, 2]),
           ALU.add, ["psA3", "smallc"], ["mods"])
    for (aa, off, ng) in ((a1, 8, n1g), (a2, 32, n2g)):
        TS(aa[:], mods[:, off:off + 8, :], 1.0, ALU.add, ["mods"], ["a12"])
        TT(aa[:], aa[:], ng.unsqueeze(2).to_broadcast([128, 8, 2]), ALU.mult, ["a12", "smallc"], ["a12"])

    def norm_core(ti, xin, xk, nrows, dst_fn, v, acol, shoff, dkeys, inv_n=1.0 / D):
        s2 = ti % 2
        nk = "xn%d" % s2
        xnn = xn[s2]
        ACTF(junk[0:nrows, :], xin, AF.Square, [xk], ["junk", ("ssq", ti)], accum=ssq[0:nrows, ti:ti + 1])
        ACTF(rstd[0:nrows, ti:ti + 1], ssq[0:nrows, ti:ti + 1], AF.Sqrt, [("ssq", ti), "epsc"], [("rstd", ti)],
             bias=epsc[0:nrows, 0:1], scale=inv_n)
        RCP(rstd[0:nrows, ti:ti + 1], rstd[0:nrows, ti:ti + 1], [("rstd", ti)], [("rstd", ti)])
        TS(xnn[0:nrows, :], xin, rstd[0:nrows, ti:ti + 1], ALU.mult, [xk, ("rstd", ti)], [nk])
        for k in range(8):
            TR(psT[:, k, 0:nrows], xnn[0:nrows, k * 128:(k + 1) * 128], ident_bf[0:nrows, 0:nrows], [nk, "cst"], ["psT"])
        for k in range(8):
            ACTF(dst_fn(k), psT[:, k, 0:nrows], AF.Identity, ["psT", "a12", "mods"], dkeys,
                 bias=mods[:, shoff + k, v:v + 1], scale=acol[:, k, v:v + 1])

    def norm_tile(ti, src_ap, nrows, col0, v):
        s3 = ti % 3
        xk = "xt%d" % s3
        DMA("sp", xt[s3][0:nrows, :], src_ap, [], [xk])
        norm_core(ti, xt[s3][0:nrows, :], xk, nrows, lambda k: hT[:, k, col0:col0 + nrows], v, a1, 0,
                  [("hT", col0 // 128)])

    norm_tile(18, xh_d[:, :], 4, HALO0, 0)
    for t in range(NTILE):
        norm_tile(t, x_d[t * 128:(t + 1) * 128, :], 128, t * 128, 0)
    for t in range(2):
        norm_tile(16 + t, ctx_d[t * 128:(t + 1) * 128, :], 128, NT + t * 128, 1)

    def hk(tb):
        return [("hT", 4 * tb + i) for i in range(4)]

    P.fence()
    STOP = [False]

    class WPool:
        def __init__(self, bufs, name):
            self.bufs, self.name, self.i = bufs, name, 0

        def load(self, src_ap, shape_sel=None):
            i = self.i % len(self.bufs)
            self.i += 1
            buf = self.bufs[i]
            key = "%s%d" % (self.name, i)
            dst = buf if shape_sel is None else shape_sel(buf)
            DMA("pool", dst, src_ap, [], [key])
            return buf, key

    if debug not in ("ssd", "pd") and not os.environ.get("KSKIPF"):
        B = Bump()
        wfft = B([128, 8, 1024])
        f_tm = [B([128, 1024]) for _ in range(2)]
        Z = B([128, 8, 2, NT])
        Y = [B([128, 2, 1024]) for _ in range(2)]
        Pq = [B([128, 4, 1024]) for _ in range(2)]
        for hf in range(2):
            DMA("pool", wfft[:, :, hf * 512:(hf + 1) * 512], win_v[:, :, FFT_START + hf * 512:FFT_START + (hf + 1) * 512],
                [], [("wfft", hf)])
        for t in range(NTILE):
            ft = f_tm[t % 2]
            fk = "f_tm%d" % (t % 2)
            for hf in range(2):
                pst, pk = psA[hf], "psA%d" % hf
                for k in range(8):
                    MM(pst[:, :], hT[:, k, t * 128:(t + 1) * 128], wfft[:, k, hf * 512:(hf + 1) * 512], k == 0, k == 7,
                       [("hT", t), ("wfft", hf)], [pk])
                CP("act", ft[:, hf * 512:(hf + 1) * 512], pst[:, :], [pk], [fk])
            for gp in range(4):
                pst, pk = psA[2 + gp % 2], "psA%d" % (2 + gp % 2)
                for gi in range(2):
                    g = 2 * gp + gi
                    MM(pst[:, gi * 256:(gi + 1) * 256], ft[:, g * 128:(g + 1) * 128], W2, True, True, [fk, "cst"], [pk])
                for gi in range(2):
                    for ab in range(2):
                        zo = Z[:, 2 * gp + gi, ab, :].rearrange("p (q c r) -> p r q c", q=16, c=4, r=32)[:, 2 * t:2 * t + 2, :, :]
                        zi = pst[:, gi * 256 + ab * 128:gi * 256 + (ab + 1) * 128].rearrange("p (r q c) -> p r q c", r=2, q=16, c=4)
                        CP("dve" if gp % 2 else "act", zo, zi, [pk], [("Z", t)])
        zkeys = [("Z", t) for t in range(NTILE)]
        for quad in range(16):
            Yq, yk = Y[quad % 2], "Y%d" % (quad % 2)
            Yv = Yq.rearrange("p a (g k) -> p g a k", g=8)
            for gp in range(4):
                pst, pk = psA[gp % 2], "psA%d" % (gp % 2)
                for gi in range(2):
                    g = 2 * gp + gi
                    for ab in range(2):
                        zsel = Z[:, g, ab, quad * 128:(quad + 1) * 128]
                        MM(pst[:, gi * 256:(gi + 1) * 256], zsel, Wch[:, ab * 256:(ab + 1) * 256], ab == 0, ab == 1,
                           zkeys + ["cst"], [pk])
                CP("dve" if gp % 2 else "act", Yv[:, 2 * gp:2 * gp + 2, :, :],
                   pst[:, :].rearrange("p (g a k) -> p g a k", g=2, a=2), [pk], [yk])
            Pqq, pqk = Pq[quad % 2], "Pq%d" % (quad % 2)
            for c4 in range(4):
                for hf in range(2):
                    pst, pk = psA[2 + hf], "psA%d" % (2 + hf)
                    MM(pst[:, :], CrBlk[:, c4, 0, :], Yq[:, 0, hf * 512:(hf + 1) * 512], True, False, [yk, "cst"], [pk])
                    MM(pst[:, :], CrBlk[:, c4, 1, :], Yq[:, 1, hf * 512:(hf + 1) * 512], False, True, [yk, "cst"], [pk])
                    CP("dve" if hf else "act", Pqq[:, c4, hf * 512:(hf + 1) * 512], pst[:, :], [pk], [pqk])
            DMA("sp", part_d.ap()[:, quad * 4096:(quad + 1) * 4096], Pqq.rearrange("p c k -> p (c k)"), [pqk], ["part_d"])
        P.add("pool", lambda e: e.collective_compute("ReduceScatter", ALU.add, replica_groups=GROUPS,
                                                     ins=[part_d.ap().opt()], outs=[rs_d.ap().opt()]),
              ["part_d"], ["rs_d"], cc=True)
        if debug == "fft":
            dd = dout("rs", [32, 65536], BF16)
            tmpb = Bump(100 * 1024)
            tb_ = tmpb([32, 16384])
            for i in range(4):
                DMA("sp", tb_, rs_d.ap()[:, i * 16384:(i + 1) * 16384], ["rs_d"], ["tb_"])
                DMA("sp", dd[:, i * 16384:(i + 1) * 16384], tb_, ["tb_"], ["dd"])
            STOP[0] = True
        P.fence()

    if not STOP[0] and debug != "pd":
        B = Bump()
        dt_tm = B([128, 18, 64], F32)
        negA = B([128, 18, 64], F32)
        expA = B([128, 18, 64], F32)
        dec_bc = B([128, 18, 64], F32)
        dtw = B([128, 18, 64], F32)
        dta_bf = B([128, 18, 64])
        Dcore = B([128, 64], F32)
        Dm = B([128, 4, 64], F32)
        wq = WPool([B([128, 8, 256]) for _ in range(4)], "wq")
        u_sb = [B([128, UW]) for _ in range(2)]
        acc = B([128, 2048], F32)
        diagw = [B([128, 5, 128]) for _ in range(2)]
        xbcT = B([128, 4, CO])
        xsB = B([128, 18, 384])
        Eb_all = B([128, 16, 256])
        E = [B([128, 256], F32) for _ in range(2)]
        Ebf0 = B([128, 256])
        Fg = [B([128, 4, 256], F32) for _ in range(2)]
        xdtS = [B([128, 256]) for _ in range(2)]
        xdt = [[B([128, 256]) for _ in range(2)] for _ in range(2)]
        xsD = [B([128, 256]) for _ in range(2)]
        drep = [B([128, 8, 128]) for _ in range(2)]
        Lt = [[B([128, 4, 128]) for _ in range(2)] for _ in range(2)]
        GT = [[B([128, 4, 128]) for _ in range(2)] for _ in range(2)]
        T0 = B([128, 256], F32)
        T1 = B([128, 256], F32)
        YG = B([128, 256], F32)
        SZ = [B([128, 256], F32) for _ in range(2)]
        YN = B([128, 256])
        yTg = B([128, 2, NT])
        ss2 = B([128, 16], F32)
        r2 = B([128, 16], F32)
        SCR = B.off
        B2 = Bump(SCR)
        wdt = B2([128, 8, 64])
        nega_bc = B2([128, 64], F32)
        tmpE = B2([128, 64], F32)
        Asb = B2([128, 128], F32)
        DMA("pool", wdt, win_v[:, :, DT_START:DT_START + 64], [], ["wdt"])
        ACTF(nega_bc, alog_bc, AF.Exp, ["smallc"], ["nega"])
        TS(nega_bc, nega_bc, -1.0, ALU.mult, ["nega"], ["nega"])
        psD = psA[6]
        psE = psA[5]
        KDT = float(os.environ.get("KDT", "9"))
        for t in range(18 if KDT >= 2 else 0):
            c0 = t * 128
            for k in range(8):
                MM(psD[:, 128:192], hT[:, k, c0:c0 + 128], wdt[:, k, :], k == 0, k == 7, ["wdt", ("hT", t)], ["psDt"])
            TT(tmpE, psD[:, 128:192], dtb_bc, ALU.add, ["psDt", "smallc"], ["tmpE"])
            ACTF(tmpE, tmpE, AF.Exp, ["tmpE"], ["tmpE"])
            ACTF(dt_tm[:, t, :], tmpE, AF.Ln, ["tmpE", "epsc"], [("dt_tm", t)], bias=epsc[:, 1:2])
            TT(dta_bf[:, t, :], dt_tm[:, t, :], nega_bc, ALU.mult, [("dt_tm", t), "nega"], [("dta_bf", t)])
            if KDT < 3:
                continue
            MM(psD[:, 0:32], triU, dta_bf[:, t, 0:32], True, True, ["cst", ("dta_bf", t)], ["psDa"])
            MM(psD[:, 32:64], triL, dta_bf[:, t, 32:64], True, True, ["cst", ("dta_bf", t)], ["psDa"])
            MM(psD[:, 64:128], ones_bf, dta_bf[:, t, :], True, True, ["cst", ("dta_bf", t)], ["psDb"])
            if t < 16 and not os.environ.get("KNOPSE"):
                MM(psE[:, 0:64], ones_bf, dta_bf[:, t, :], t == 0, t == 15, ["cst", ("dta_bf", t)], ["psE"])
            if KDT < 3.2:
                continue
            CP("dve", Asb, psD[:, 0:128], ["psDa"], ["Asb"])
            TS(negA[:, t, :], Asb[:, 0:64], -1.0, ALU.mult, ["Asb"], [("negA", t)])
            if KDT < 3.4:
                continue
            ACTF(expA[:, t, :], Asb[:, 0:64], AF.Exp, ["Asb"], [("expA", t)])
            ACTF(dec_bc[:, t, :], Asb[:, 64:128], AF.Exp, ["Asb"], [("dec", t)])
            if KDT < 3.6:
                continue
            TT(dtw[:, t, :], Asb[:, 64:128], negA[:, t, :], ALU.add, ["Asb", ("negA", t)], [("dtw", t)])
            ACTF(dtw[:, t, :], dtw[:, t, :], AF.Exp, [("dtw", t)], [("dtw", t)])
            TT(dtw[:, t, :], dtw[:, t, :], dt_tm[:, t, :], ALU.mult, [("dtw", t), ("dt_tm", t)], [("dtw", t)])
        if KDT >= 4:
            ACTF(Dcore, psE[:, 0:64], AF.Exp, ["psE"], ["Dcore"])
            DMA("sp", cparts[2].ap()[:, :], Dcore, ["Dcore"], ["contrib"])
        for ub in (u_sb if KDT >= 5 else []):
            MEMSET(ub[:, 2052:2054], 0.0, ["u_sb0", "u_sb1"])
            MEMSET(ub[:, 2310:2312], 0.0, ["u_sb0", "u_sb1"])
        P.fence()

        uctr = [0]
        dctr = [0]

        def ssd_prep(g, p2):
            chunks = [(2 * g, 256 * g), (2 * g + 1, 256 * g + 128), (16 + g, 2048 + 128 * g)]
            if p2:
                chunks.append((24 + g, 3072 + 128 * g))
            CW = NT if p2 else CO
            for ci, (cch, col) in enumerate(chunks):
                wb, wk = wq.load(win_v[:, :, col:col + 128], lambda b: b[:, :, 0:128])
                ui = uctr[0] % 2
                uctr[0] += 1
                ub, uk = u_sb[ui], "u_sb%d" % ui
                for tb in range(4):
                    pst, pk = psA[tb % 2], "psA%d" % (tb % 2)
                    for k in range(8):
                        MM(pst[:, :], wb[:, k, 0:128], hT[:, k, tb * 512:(tb + 1) * 512], k == 0, k == 7, [wk] + hk(tb), [pk])
                    CP("act", ub[:, 2 + tb * 512:2 + (tb + 1) * 512], pst[:, :], [pk], [uk])
                if not p2:
                    pst, pk = psA[0], "psA0"
                    for k in range(8):
                        MM(pst[:, 0:256], wb[:, k, 0:128], hT[:, k, NT:NT + 256], k == 0, k == 7,
                           [wk, ("hT", 16), ("hT", 17)], [pk])
                    CP("act", ub[:, 2054:2310], pst[:, 0:256], [pk], [uk])
                pst, pk = psA[1], "psA1"
                for k in range(8):
                    MM(pst[:, 0:4], wb[:, k, 0:128], hT[:, k, HALO0:HALO0 + 4], k == 0, k == 7, [wk, ("hT", 18)], [pk])
                TS(ub[:, 0:2], pst[:, 0:2], hmask[:, 0:1], ALU.mult, [pk, "smallc"], [uk])
                TS(ub[:, 2050:2052], pst[:, 2:4], hmask[:, 1:2], ALU.mult, [pk, "smallc"], [uk])
                di = dctr[0] % 2
                dctr[0] += 1
                dg, dk = diagw[di], "diagw%d" % di
                for k in range(5):
                    TS(dg[:, k, :], ident_bf, convw[:, cch * 5 + k:cch * 5 + k + 1], ALU.mult, ["cst", "convw"], [dk])
                blocks = [(tb * 512, 512) for tb in range(4)] + ([] if p2 else [(CTXO, 256)])
                for bi, (o0, n) in enumerate(blocks):
                    pst, pk = psA[2 + bi % 2], "psA%d" % (2 + bi % 2)
                    for k in range(5):
                        MM(pst[:, 0:n], dg[:, k, :], ub[:, o0 + k:o0 + k + n], k == 0, k == 4, [dk, uk], [pk])
                    ACTF(xbcT[:, ci, o0:o0 + n], pst[:, 0:n], AF.Silu, [pk, "smallc"], [("xbcT", ci)], bias=convb[:, cch:cch + 1])
            for t in (range(16) if p2 else range(18)):
                c0 = t * 128 if t < 16 else CTXO + (t - 16) * 128
                for ci in range(3):
                    TR(psT[:, ci, :], xbcT[:, ci, c0:c0 + 128], ident_bf, [("xbcT", ci), "cst"], ["psT03"])
                CP("act", xsB[:, t, :].rearrange("p (c k) -> p c k", c=3), psT[:, 0:3, :], ["psT03"], [("xsB", t)])

        EL = os.environ.get("KEL", "dve")

        def ss_a(g, t, d, si):
            c0 = d * 32 + 4 * g
            xs4 = xsB[:, t, 0:256].rearrange("p (j q) -> p j q", j=4)
            psS = psA[5][:, 0:256] if si == 0 else psA[6][:, 0:256]
            TT(xdtS[si].rearrange("p (j q) -> p j q", j=4), xs4, bc4(dtw[:, t, c0:c0 + 4]), ALU.mult,
               [("xsB", t), ("dtw", t)], ["xdtS%d" % si], eng=EL)
            MM(psS, xsB[:, t, 256:384], xdtS[si], True, True, [("xsB", t), "xdtS%d" % si], ["psS%d" % si])

        def ss_b(g, t, d, si, first):
            c0 = d * 32 + 4 * g
            psS = psA[5][:, 0:256] if si == 0 else psA[6][:, 0:256]
            pk = "psS%d" % si
            if first:
                CP("act", E[d], psS, [pk], [("E", d)])
            else:
                E4 = E[d].rearrange("p (j q) -> p j q", j=4)
                TT(E4, E4, bc4(dec_bc[:, t, c0:c0 + 4]), ALU.mult, [("E", d), ("dec", t)], [("E", d)])
                TT(E[d], E[d], psS, ALU.add, [("E", d), pk], [("E", d)])

        def run_states(g, d, tiles, first0, pre=None):
            tiles = list(tiles)
            ss_a(g, tiles[0], d, 0)
            for i, t in enumerate(tiles):
                if i + 1 < len(tiles):
                    ss_a(g, tiles[i + 1], d, (i + 1) % 2)
                if pre is not None:
                    pre(t)
                ss_b(g, t, d, i % 2, first0 and i == 0)

        def ssd_pass1(g):
            ssd_prep(g, False)
            for d, ctx_order, lat_order in ((1, (17, 16), range(15, -1, -1)), (0, (16, 17), range(16))):
                run_states(g, d, ctx_order, True)
                DMA("sp", sctx_d[:, (g * 2 + d) * 256:(g * 2 + d + 1) * 256], E[d], [("E", d)], ["sctx_d"])
                run_states(g, d, lat_order, True)
                DMA("sp", contrib_ap(g, d), E[d], [("E", d)], ["contrib"])

        NG = int(os.environ.get("KNG", "8"))
        if debug == "ssd_dt":
            dd = dout("dt", [128, 5, 18 * 64], F32)
            for i, arr in enumerate((dt_tm, negA, expA, dec_bc, dtw)):
                DMA("sp", dd[:, i, :], arr.rearrange("p t c -> p (t c)"), [(nm, t) for nm in ("dt_tm", "negA", "expA", "dec", "dtw") for t in range(18)], ["dd"])
            P.emit(es)
            es.close()
            P.in_names = in_names
            return nc, P
        for g in range(NG):
            ssd_pass1(g)
        if NG < 8:
            MEMSET(acc[:, 0:2048], 0.0, ["acc"])
            for gz in range(NG, 8):
                for dz in range(2):
                    DMA("sp", contrib_ap(gz, dz), acc[:, 0:256], ["acc"], ["contrib"])
        for ci in range(3):
            def _ag(e, ci=ci):
                return e.collective_compute("AllGather", ALU.bypass, replica_groups=GROUPS,
                                            ins=[cparts[ci].ap().opt()], outs=[gparts[ci].ap().opt()])
            P.add("pool", _ag, ["contrib", "rs_d", "ccchain"], ["gath", "ccchain"], cc=True)
        P.fence()
        DMA("sp", Dm, gparts[2].ap().rearrange("(r n) c -> n r c", n=128), ["gath"], ["Dm"])
        TS(Dm, Dm, -1.0, ALU.add, ["Dm"], ["Dm"])
        TT(Dm, Dm, cmask, ALU.mult, ["Dm", "smallc"], ["Dm"])
        TS(Dm, Dm, 1.0, ALU.add, ["Dm"], ["Dm"])

        def ssd_pass2(g):
            ssd_prep(g, True)
            wz, wzk = wq.load(win_v[:, :, Z_START + 256 * g:Z_START + 256 * (g + 1)])
            for d in range(2):
                c0 = d * 32 + 4 * g
                DMA("sp", Fg[d], gath_ap(g, d), ["gath"], [("Fg", d)])
                DMA("sp", E[d], sctx_d[:, (g * 2 + d) * 256:(g * 2 + d + 1) * 256], [], [("E", d)])
                E4 = E[d].rearrange("p (j q) -> p j q", j=4)
                for r in (range(4) if d == 0 else range(3, -1, -1)):
                    TT(E4, E4, bc4(Dm[:, r, c0:c0 + 4]), ALU.mult, [("E", d), "Dm"], [("E", d)])
                    STT(E[d], Fg[d][:, r, :], cmask[:, r, c0:c0 + 1], E[d], ALU.mult, ALU.add,
                        [("E", d), ("Fg", d), "smallc"], [("E", d)])
            run_states(g, 1, range(15, -1, -1), False,
                       pre=lambda t: CP("act", Eb_all[:, t, :], E[1], [("E", 1)], [("Eb", t)]))
            CP("act", Ebf0, E[0], [("E", 0)], ["Ebf0"])
            psC = psA[0][:, 0:128]
            psZ = psA[6][:, 0:256]
            psY = psA[1][:, 0:256]
            psO = psA[4]
            def h1a(t):
                p = t % 2
                cols = slice(t * 128, (t + 1) * 128)
                MM(psC, xbcT[:, 2, cols], xbcT[:, 3, cols], True, True, [("xbcT", 2), ("xbcT", 3)], ["psC"])
                for d in range(2):
                    CP(EL, drep[p][:, 4 * d:4 * d + 4, :],
                       dta_bf[:, t, d * 32 + 4 * g:d * 32 + 4 * g + 4].unsqueeze(2).to_broadcast([128, 4, 128]),
                       [("dta_bf", t)], [("drep", d, p)])
                for d in range(2):
                    c0 = d * 32 + 4 * g
                    pD = psA[2 + d]
                    for j in range(4):
                        MM(pD[:, 128 * j:128 * (j + 1)], ident_bf, mask4[:, d, 0:128], True, False, ["cst"], [("psD", d)])
                        MM(pD[:, 128 * j:128 * (j + 1)], drep[p][:, 4 * d + j, :], triU if d == 0 else triL, False, True,
                           [("drep", d, p), "cst"], [("psD", d)])
                    for j in range(4):
                        ACTF(Lt[d][p][:, j, :], pD[:, 128 * j:128 * (j + 1)], AF.Exp, [("psD", d), ("negA", t)], [("Lt", d, p)],
                             bias=negA[:, t, c0 + j:c0 + j + 1])
                for k in range(8):
                    MM(psZ, hT[:, k, cols], wz[:, k, :], k == 0, k == 7, [("hT", t), wzk], ["psZ"])
                ACTF(SZ[p], psZ, AF.Silu, ["psZ"], [("SZ", p)])

            def h1b(t):
                p = t % 2
                xs4 = xsB[:, t, 0:256].rearrange("p (j q) -> p j q", j=4)
                for d in range(2):
                    c0 = d * 32 + 4 * g
                    TT(GT[d][p], Lt[d][p], psC.unsqueeze(1).to_broadcast([128, 4, 128]), ALU.mult, [("Lt", d, p), "psC"],
                       [("GT", d, p)])
                    TT(xdt[d][p].rearrange("p (j q) -> p j q", j=4), xs4, bc4(dt_tm[:, t, c0:c0 + 4]), ALU.mult,
                       [("xsB", t), ("dt_tm", t)], [("xdt", d, p)], eng=EL)
                TT(xsD[p].rearrange("p (j q) -> p j q", j=4), xs4, bc4(dskip[:, 4 * g:4 * g + 4]), ALU.mult,
                   [("xsB", t), "smallc"], [("xsD", p)], eng=EL)

            def h2a(t):
                p = t % 2
                cols = slice(t * 128, (t + 1) * 128)
                for j in range(4):
                    js = slice(64 * j, 64 * (j + 1))
                    MM(psY[:, js], ident_bf, xsD[p][:, js], True, False, ["cst", ("xsD", p)], ["psY"])
                    MM(psY[:, js], GT[0][p][:, j, :], xdt[0][p][:, js], False, False, [("GT", 0, p), ("xdt", 0, p)], ["psY"])
                    MM(psY[:, js], GT[1][p][:, j, :], xdt[1][p][:, js], False, True, [("GT", 1, p), ("xdt", 1, p)], ["psY"])
                MM(psO[:, 0:256], xbcT[:, 3, cols], Ebf0, True, True, [("xbcT", 3), "Ebf0"], ["psO0"])
                MM(psO[:, 256:512], xbcT[:, 3, cols], Eb_all[:, t, :], True, True, [("xbcT", 3), ("Eb", t)], ["psO1"])
                ss_a(g, t, 0, 0)
                ss_b(g, t, 0, 0, False)
                CP("act", Ebf0, E[0], [("E", 0)], ["Ebf0"])
                TT(T0.rearrange("p (j q) -> p j q", j=4), psO[:, 0:256].rearrange("p (j q) -> p j q", j=4),
                   bc4(expA[:, t, 4 * g:4 * g + 4]), ALU.mult, ["psO0", ("expA", t)], ["T0"])
                TT(T1.rearrange("p (j q) -> p j q", j=4), psO[:, 256:512].rearrange("p (j q) -> p j q", j=4),
                   bc4(expA[:, t, 32 + 4 * g:32 + 4 * g + 4]), ALU.mult, ["psO1", ("expA", t)], ["T1"])
                TT(T0, T0, T1, ALU.add, ["T0", "T1"], ["T0"])
                TT(T0, T0, psY, ALU.add, ["T0", "psY"], ["T0"])
                TT(YG, T0, SZ[p], ALU.mult, ["T0", ("SZ", p)], ["YG"])
                ACTF(T1, YG, AF.Square, ["YG"], ["T1", ("ss2", t)], accum=ss2[:, t:t + 1])
                ACTF(r2[:, t:t + 1], ss2[:, t:t + 1], AF.Sqrt, [("ss2", t), "epsc"], [("r2", t)], bias=epsc[:, 0:1], scale=1.0 / 256)

            def h2b(t):
                cols = slice(t * 128, (t + 1) * 128)
                RCP(r2[:, t:t + 1], r2[:, t:t + 1], [("r2", t)], [("r2", t)])
                TS(YN, YG, r2[:, t:t + 1], ALU.mult, ["YG", ("r2", t)], ["YN"])
                for i in range(2):
                    TR(psT[:, 4 + i, :], YN[:, i * 128:(i + 1) * 128], ident_bf, ["YN", "cst"], ["psT45"])
                for i in range(2):
                    ACTF(yTg[:, i, cols], psT[:, 4 + i, :], AF.Identity, ["psT45", "smallc"], ["yTg"],
                         scale=ssdg[:, 2 * g + i:2 * g + i + 1])

            h1a(0)
            h1b(0)
            for t in range(16):
                if t + 1 < 16:
                    h1a(t + 1)
                h2a(t)
                if t + 1 < 16:
                    h1b(t + 1)
                h2b(t)
            for i in range(2):
                DMA("sp", yT_d[2 * g + i, :, :], yTg[:, i, :], ["yTg"], ["yT_d"])

        for g in range(NG):
            ssd_pass2(g)
        if debug == "ssd":
            dd = dout("yT", [16, 128, NT], BF16)
            P.fence()
            tb_ = Bump(100 * 1024)([128, NT])
            for i in range(2 * NG):
                DMA("sp", tb_, yT_d[i, :, :], [], ["tb_"])
                DMA("sp", dd[i, :, :], tb_, ["tb_"], ["dd"])
            STOP[0] = True
        P.fence()

    if not STOP[0]:
        B = Bump()
        xt = [B([128, D], F32) for _ in range(2)]
        xn = [B([128, D], BF16) for _ in range(2)]
        junk = B([128, D], BF16)
        fgb = B([128, D], F32)
        xnew = [B([128, D], F32) for _ in range(4)]
        h2T = B([128, 8, 512])
        BASE = B.off
        B1 = Bump(BASE)
        wo = B1([128, 8, 1024])
        w8 = WPool([B1([128, 8, 128]) for _ in range(6)], "w8")
        w16 = WPool([B1([128, 16, 128]) for _ in range(2)], "w16")
        yTb = B1([128, 16, 512])
        fmT = B1([128, 8, 512])
        ftl = [B1([128, 1024]) for _ in range(2)]
        mergedT = B1([128, 8, 512])
        S0 = B1([128, 512], F32)
        S1 = B1([128, 512], F32)
        M0 = B1([128, 512], F32)
        M1 = B1([128, 512], F32)
        TMP = B1([128, 512], F32)
        B2 = Bump(BASE)
        wfi = WPool([B2([128, 8, 128]) for _ in range(4)], "wfi")
        wfo = [B2([128, 22, 512]) for _ in range(2)]
        actT = B2([128, 22, 512])
        SA = [B2([128, 512], F32) for _ in range(2)]
        xo = [B2([128, D], F32) for _ in range(4)]
        TMP2 = B2([128, 512], F32)
        ot = [B2([128, D], F32) for _ in range(2)]

        fg_d = din("fg_bc", [128, D])
        wssd_t = din("w_ssd_t", [8, 128, 16 * 128])
        wfft_t = din("w_fft_t", [8, 128, 8 * 128])
        wgate_t = din("w_gate_t", [16, 128, 8 * 128])
        wo_d = din("w_o", [D, D])
        wfi_t = din("w_ffn_in_t", [44, 128, 8 * 128])
        wfo_d = din("w_ffn_out", [D_FF, D])
        DMA("sp", fgb, fg_d[:, :], [], ["fgb"])
        if debug == "pd":
            yin = din("dbg_yT_in", [16, 128, NT], BF16)
            rin = din("dbg_rs_in", [32, 65536], BF16)
            tb_ = Bump(100 * 1024)([128, 16384])
            for i in range(16):
                DMA("sp", tb_[:, 0:NT], yin[i, :, :], [], ["tb_"])
                DMA("sp", yT_d[i, :, :], tb_[:, 0:NT], ["tb_"], ["yT_d"])
            for i in range(4):
                DMA("sp", tb_[0:32, :], rin[:, i * 16384:(i + 1) * 16384], [], ["tb_"])
                DMA("sp", rs_d.ap()[:, i * 16384:(i + 1) * 16384], tb_[0:32, :], ["tb_"], ["rs_d"])
            P.fence()
        wo_v = wo_d.rearrange("(k p) n -> p k n", p=128)
        wfo_v = wfo_d.rearrange("(j p) n -> p j n", p=128)
        yT_v = yT_d.rearrange("k p t -> p k t")

        for tb in range(int(os.environ.get("KTB", "4"))):
            tcols = slice(tb * 512, (tb + 1) * 512)
            DMA("sp", yTb, yT_v[:, :, tcols], [], ["yTb"])
            for hf in range(2):
                DMA("pool", wo[:, :, hf * 512:(hf + 1) * 512], wo_v[:, :, hf * 512:(hf + 1) * 512], [], [("wo", hf)])
            for tt in range(4):
                t = 4 * tb + tt
                ft, fk = ftl[tt % 2], "ftl%d" % (tt % 2)
                DMA("sp", ft, rs_d.ap()[2 * t:2 * t + 2, :].rearrange("r (c k) -> (r c) k", k=1024), ["rs_d"], [fk])
                for k in range(8):
                    TR(psT[:, k, :], ft[:, k * 128:(k + 1) * 128], ident_bf, [fk, "cst"], ["psT"])
                CP("act", fmT[:, :, tt * 128:(tt + 1) * 128], psT[:, :, :], ["psT"], ["fmT"])
            for fc in range(8):
                wss, wssk = w16.load(wssd_t[fc].rearrange("p (k c) -> p k c", k=16))
                wff, wffk = w8.load(wfft_t[fc].rearrange("p (k c) -> p k c", k=8))
                wg0, wg0k = w8.load(wgate_t[fc].rearrange("p (k c) -> p k c", k=8))
                wg1, wg1k = w8.load(wgate_t[8 + fc].rearrange("p (k c) -> p k c", k=8))
                for k in range(16):
                    MM(psA[0][:, :], wss[:, k, :], yTb[:, k, :], k == 0, k == 15, [wssk, "yTb"], ["psA0"])
                for k in range(8):
                    MM(psA[1][:, :], wff[:, k, :], fmT[:, k, :], k == 0, k == 7, [wffk, "fmT"], ["psA1"])
                for k in range(8):
                    MM(psA[2][:, :], wg0[:, k, :], hT[:, k, tcols], k == 0, k == 7, [wg0k] + hk(tb), ["psA2"])
                for k in range(8):
                    MM(psA[3][:, :], wg1[:, k, :], hT[:, k, tcols], k == 0, k == 7, [wg1k] + hk(tb), ["psA3"])
                ACTF(S0, psA[2][:, :], AF.Sigmoid, ["psA2"], ["S0"])
                ACTF(S1, psA[3][:, :], AF.Sigmoid, ["psA3"], ["S1"])
                TT(M0, S0, psA[0][:, :], ALU.mult, ["S0", "psA0"], ["M0"])
                TT(M1, S1, psA[1][:, :], ALU.mult, ["S1", "psA1"], ["M1"])
                TT(mergedT[:, fc, :], M0, M1, ALU.add, ["M0", "M1"], ["mergedT"])
            for tt in range(4):
                t = 4 * tb + tt
                xi = tt % 2
                xk = "xt%d" % xi
                DMA("sp", xt[xi], x_d[t * 128:(t + 1) * 128, :], [], [xk])
                for hf in range(2):
                    pst, pk = psA[4 + hf], "psA%d" % (4 + hf)
                    hs = slice(hf * 512, (hf + 1) * 512)
                    for k in range(8):
                        MM(pst[:, :], mergedT[:, k, tt * 128:(tt + 1) * 128], wo[:, k, hs], k == 0, k == 7,
                           ["mergedT", ("wo", hf)], [pk])
                    TT(TMP, pst[:, :], gbc[:, hs], ALU.mult, [pk, "gbc"], ["TMP"])
                    TT(xnew[tt][:, hs], TMP, xt[xi][:, hs], ALU.add, ["TMP", xk], [("xnew", tt)])
                norm_core(20 + tt, xnew[tt], ("xnew", tt), 128, lambda k, tt=tt: h2T[:, k, tt * 128:(tt + 1) * 128], 0, a2, 24,
                          ["h2T"])
            P.fence()
            DMA("pool", wfo[0], wfo_v[:, :, 0:512], [], [("wfo", 0)])
            for j in range(22):
                if j == 8:
                    DMA("pool", wfo[1], wfo_v[:, :, 512:1024], [], [("wfo", 1)])
                wa, wak = wfi.load(wfi_t[j].rearrange("p (k c) -> p k c", k=8))
                wb, wbk = wfi.load(wfi_t[22 + j].rearrange("p (k c) -> p k c", k=8))
                pa, pak = psA[2 * (j % 2)], "psA%d" % (2 * (j % 2))
                pb, pbk = psA[2 * (j % 2) + 1], "psA%d" % (2 * (j % 2) + 1)
                for k in range(8):
                    MM(pa[:, :], wa[:, k, :], h2T[:, k, :], k == 0, k == 7, [wak, "h2T"], [pak])
                for k in range(8):
                    MM(pb[:, :], wb[:, k, :], h2T[:, k, :], k == 0, k == 7, [wbk, "h2T"], [pbk])
                sa, sak = SA[j % 2], "SA%d" % (j % 2)
                ACTF(sa, pa[:, :], AF.Silu, [pak], [sak])
                TT(actT[:, j, :], sa, pb[:, :], ALU.mult, [sak, pbk], ["actT"])
            for hf in range(2):
                hs = slice(hf * 512, (hf + 1) * 512)
                for tt in range(4):
                    pst, pk = psA[4 + tt % 2], "psA%d" % (4 + tt % 2)
                    for j in range(22):
                        MM(pst[:, :], actT[:, j, tt * 128:(tt + 1) * 128], wfo[hf][:, j, :], j == 0, j == 21,
                           ["actT", ("wfo", hf)], [pk])
                    TT(TMP2, pst[:, :], gbc[:, 1024 + hf * 512:1024 + (hf + 1) * 512], ALU.mult, [pk, "gbc"], ["TMP2"])
                    TT(xo[tt][:, hs], TMP2, xnew[tt][:, hs], ALU.add, ["TMP2", ("xnew", tt)], [("xo", tt)])
            for tt in range(4):
                t = 4 * tb + tt
                ti = 24 + tt
                oi = tt % 2
                ACTF(junk, xo[tt], AF.Square, [("xo", tt)], ["junk", ("ssq", ti)], accum=ssq[:, ti:ti + 1])
                ACTF(rstd[:, ti:ti + 1], ssq[:, ti:ti + 1], AF.Sqrt, [("ssq", ti), "epsc"], [("rstd", ti)],
                     bias=epsc[:, 0:1], scale=1.0 / D)
                RCP(rstd[:, ti:ti + 1], rstd[:, ti:ti + 1], [("rstd", ti)], [("rstd", ti)])
                STT(ot[oi], xo[tt], rstd[:, ti:ti + 1], fgb, ALU.mult, ALU.mult, [("xo", tt), ("rstd", ti), "fgb"], [("ot", oi)])
                DMA("sp", out_d[t * 128:(t + 1) * 128, :], ot[oi], [("ot", oi)], ["out"])
            P.fence()

    P.emit(es)
    es.close()
    P.in_names = in_names
    return nc, P


def _consts(q):
    c = np.zeros((128, NCST), np.float32)
    i = np.arange(128)
    c[:, 0:128] = np.eye(128)
    c[:, 128:256] = (i[:, None] <= i[None, :])
    c[:, 256:384] = (i[:, None] >= i[None, :])
    c[:, 384:512] = 1.0
    mf = np.where(i[:, None] <= i[None, :], 0.0, -30000.0)
    mb = np.where(i[:, None] >= i[None, :], 0.0, -30000.0)
    c[:, 512:1024] = np.tile(mf, (1, 4))
    c[:, 1024:1536] = np.tile(mb, (1, 4))
    cc = np.arange(64)
    ang = 2 * np.pi * np.outer(cc, cc) / 64.0
    w2c = np.zeros((128, 128))
    w2s = np.zeros((128, 128))
    for r2 in range(2):
        w2c[r2 * 64:(r2 + 1) * 64, r2 * 64:(r2 + 1) * 64] = np.cos(ang) / 8.0
        w2s[r2 * 64:(r2 + 1) * 64, r2 * 64:(r2 + 1) * 64] = np.sin(ang) / 8.0
    c[:, 1536:1664] = w2c
    c[:, 1664:1792] = w2s
    angc = 2 * np.pi * np.outer(i, i) / 128.0
    Cc = np.cos(angc) / math.sqrt(128.0)
    Sc = np.sin(angc) / math.sqrt(128.0)
    c[:, 1792:1920] = Cc
    c[:, 1920:2048] = Sc
    c[:, 2048:2176] = -Sc
    c[:, 2176:2304] = Cc
    rl = np.arange(32)
    angr = 2 * np.pi * np.outer(32 * q + rl, i) / 128.0
    blk = np.zeros((128, 4, 2, 128))
    for c4 in range(4):
        blk[c4 * 32:(c4 + 1) * 32, c4, 0, :] = np.cos(angr) / math.sqrt(128.0)
        blk[c4 * 32:(c4 + 1) * 32, c4, 1, :] = -np.sin(angr) / math.sqrt(128.0)
    c[:, 2304:3328] = blk.reshape(128, 1024)
    return c


def _tiles(w):
    kk, nb = w.shape[0] // 128, w.shape[1] // 128
    return np.ascontiguousarray(w.reshape(kk, 128, nb, 128).transpose(2, 1, 0, 3).reshape(nb, 128, kk * 128))


def _prep_inputs(inp):
    f32 = np.float32
    x = np.asarray(inp["x"], f32)
    ctx = np.asarray(inp["ctx"], f32)
    c = np.asarray(inp["c"], f32)
    c_ctx = np.asarray(inp["c_ctx"], f32)
    b_ada = np.asarray(inp["b_ada"], f32)[0]
    conv_w = np.asarray(inp["conv_w"], f32)[0]
    shared = {
        "w_ada": np.ascontiguousarray(inp["w_ada"][0], f32),
        "b_ada_row": np.ascontiguousarray(b_ada.reshape(1, -1)),
        "w_in": np.ascontiguousarray(inp["w_in"][0], f32),
        "identc": np.eye(128, dtype=f32),
        "convw_fm": np.ascontiguousarray(conv_w.reshape(5, 32, 128).transpose(2, 1, 0).reshape(128, 160)),
        "fg_bc": np.ascontiguousarray(np.broadcast_to(np.asarray(inp["final_g"], f32)[None, :], (128, D))),
        "w_ssd_t": _tiles(np.asarray(inp["w_ssd_out"][0], f32)),
        "w_fft_t": _tiles(np.asarray(inp["w_fft_out"][0], f32)),
        "w_gate_t": _tiles(np.asarray(inp["w_in"][0], f32)[:, GATE_START:GATE_START + 2048]),
        "w_o": np.ascontiguousarray(inp["w_o"][0], f32),
        "w_ffn_in_t": _tiles(np.asarray(inp["w_ffn_in"][0], f32)),
        "w_ffn_out": np.ascontiguousarray(inp["w_ffn_out"][0], f32),
    }
    sm = np.zeros((128, NSMALL), f32)
    sm[:, 0:48] = b_ada.reshape(48, 128).T
    sm[:, 48:56] = np.asarray(inp["norm1_g"], f32)[0].reshape(8, 128).T
    sm[:, 56:64] = np.asarray(inp["norm2_g"], f32)[0].reshape(8, 128).T
    sm[:, 64:96] = np.asarray(inp["conv_b"], f32)[0].reshape(32, 128).T
    sm[:, 96:112] = np.asarray(inp["ssd_norm_g"], f32)[0].reshape(16, 128).T
    sm[:, 112:144] = np.asarray(inp["d_skip"], f32)[0][None, :]
    sm[:, 404:468] = np.asarray(inp["dt_bias"], f32)[0].reshape(1, 64)
    sm[:, 468:532] = np.asarray(inp["a_log"], f32)[0].reshape(1, 64)
    maps = []
    for core in range(8):
        b, q = core // 4, core % 4
        t0 = q * NT
        m = dict(shared)
        m["x"] = np.ascontiguousarray(x[b, t0:t0 + NT])
        m["ctx"] = np.ascontiguousarray(ctx[b])
        xh = np.zeros((4, D), f32)
        for i, tt in enumerate((t0 - 2, t0 - 1, t0 + NT, t0 + NT + 1)):
            if 0 <= tt < 8192:
                xh[i] = x[b, tt]
        m["xhalo"] = xh
        cv = np.zeros((128, 16), f32)
        cv[:, 0:8] = c[b].reshape(8, 128).T
        cv[:, 8:16] = c_ctx.reshape(8, 128).T
        m["cvec"] = cv
        s = sm.copy()
        s[:, 144] = 0.0 if q == 0 else 1.0
        s[:, 145] = 0.0 if q == 3 else 1.0
        cm = np.zeros((4, 64), f32)
        for r in range(4):
            cm[r, 0:32] = 1.0 if r < q else 0.0
            cm[r, 32:64] = 1.0 if r > q else 0.0
        s[:, 148:404] = cm.reshape(1, 256)
        m["smallc"] = s
        m["cst"] = _consts(q)
        maps.append(m)
    return maps


_CACHE = {}


def kernel(**inp):
    if "nc" not in _CACHE:
        _CACHE["nc"] = build()[0]
    nc = _CACHE["nc"]
    maps = _prep_inputs(inp)
    res = run_bass_kernel_spmd(nc, maps, core_ids=list(range(8)))
    out = np.zeros((2, 8192, D), np.float32)
    for core in range(8):
        b, q = core // 4, core % 4
        out[b, q * NT:(q + 1) * NT] = np.asarray(res.results[core]["out"], np.float32)
    return out
```

```python
import math
import os
from contextlib import ExitStack
import numpy as np
import ml_dtypes
import concourse.bass as bass
import concourse.mybir as mybir
from concourse.bass_utils import run_bass_kernel_spmd

F32 = mybir.dt.float32
BF16 = mybir.dt.bfloat16
AF = mybir.ActivationFunctionType
ALU = mybir.AluOpType
AX = mybir.AxisListType

D = 1024
NCST = 3328
NSMALL = 532
NT = 2048
NTILE = 16
CTX = 256
EPS = 1e-6
HW = NT + CTX + 4
HALO0 = NT + CTX
DT_START = 4096
Z_START = 4160
FFT_START = Z_START + 2048
GATE_START = FFT_START + 1024
D_FF = 2816
UW = 2312
CO = 2308
CTXO = 2052
GROUPS = [[0, 1, 2, 3], [4, 5, 6, 7]]


class _Op:
    pass


class Prog:
    def __init__(self, nc, n_dma=32):
        self.nc = nc
        self.ops = []
        self.lastw = {}
        self.rd_eng = {}
        self.rd_dma = {}
        self.n_dma = n_dma
        self.rr = 0
        self.rr_pool = 0
        self.last_on = [None] * n_dma
        self.last_eng = {}
        self.dma_since = []
        self.fence_dep = None
        self.cc_w = {}

    PSKEY = {"psC": "psA0", "psZ": "psA6", "psY": "psA1", "psS0": "psA5", "psS1": "psA6", ("psD", 0): "psA2",
             ("psD", 1): "psA3", "psO0": "psA4", "psO1": "psA4", "psDa": "psA6", "psDb": "psA6", "psDt": "psA6",
             "psE": "psA5", "psT03": "psT", "psT45": "psT", "psT7": "psT"}

    def add(self, eng, fn, reads=(), writes=(), dma=False, cc=False):
        reads = [self.PSKEY.get(k, k) for k in reads]
        writes = [self.PSKEY.get(k, k) for k in writes]
        op = _Op()
        op.eng, op.fn, op.dma, op.cc = eng, fn, dma, cc
        op.idx = len(self.ops)
        deps = set()
        for k in reads:
            w = self.lastw.get(k)
            if w is not None:
                deps.add(w)
            if isinstance(k, str) and k.startswith("ps"):
                for e2, i2 in self.rd_eng.get(k, {}).items():
                    if e2 != eng:
                        deps.add(i2)
        for k in writes:
            w = self.lastw.get(k)
            if w is not None:
                deps.add(w)
            deps.update(self.rd_eng.get(k, {}).values())
            deps.update(self.rd_dma.get(k, ()))
        if dma:
            if eng == "pool":
                s = self.n_dma - 8 + (self.rr_pool % 8)
                self.rr_pool += 1
            else:
                s = self.rr % (self.n_dma - 8)
                self.rr += 1
            op.dsem = s
            if self.last_on[s] is not None:
                deps.add(self.last_on[s])
            self.last_on[s] = op.idx
        if self.fence_dep is not None:
            deps.add(self.fence_dep)
        for k in list(reads) + list(writes):
            if k in self.cc_w:
                deps.add(self.cc_w[k])
        if cc:
            for k in writes:
                self.cc_w[k] = op.idx
        op.deps = deps
        if dma or cc:
            self.dma_since.append(op.idx)
        else:
            self.last_eng[eng] = op.idx
        for k in reads:
            if dma or cc:
                self.rd_dma.setdefault(k, []).append(op.idx)
            else:
                self.rd_eng.setdefault(k, {})[eng] = op.idx
        for k in writes:
            self.lastw[k] = op.idx
            self.rd_eng[k] = {}
            self.rd_dma[k] = []
        self.ops.append(op)
        return op

    def fence(self):
        deps = set(self.last_eng.values())
        deps.update(i for i in self.dma_since if not self.ops[i].cc)
        op = self.add("sp", lambda e: e.nop())
        op.deps |= deps
        self.fence_dep = op.idx
        self.dma_since = []
        self.lastw = {}
        self.rd_eng = {}
        self.rd_dma = {}
        self.last_on = [None] * self.n_dma

    def emit(self, es):
        nc = self.nc
        ops = self.ops
        engs = ("pe", "act", "dve", "pool", "sp")
        needs = [False] * len(ops)
        for op in ops:
            for d in op.deps:
                dd = ops[d]
                if dd.dma or dd.cc:
                    continue
                if dd.eng == "pe" and op.eng == "pe" and not (op.dma or op.cc):
                    continue
                needs[d] = True
        esem = {e: es.enter_context(nc.semaphore("s_" + e)) for e in engs}
        dsem = [es.enter_context(nc.semaphore("d%d" % i)) for i in range(self.n_dma)]
        ncc = sum(1 for op in ops if op.cc)
        ccsems = [es.enter_context(nc.semaphore("ccs%d" % i)) for i in range(ncc)]
        cnt = {e: 0 for e in engs}
        dcnt = [0] * self.n_dma
        cccnt = 0
        for op in ops:
            if op.cc:
                op.ev = (ccsems[cccnt], 1)
                cccnt += 1
            elif op.dma:
                dcnt[op.dsem] += 16
                op.ev = (dsem[op.dsem], dcnt[op.dsem])
            elif needs[op.idx]:
                cnt[op.eng] += 1
                op.ev = (esem[op.eng], cnt[op.eng])
            else:
                op.ev = None
        per = {e: [] for e in engs}
        for op in ops:
            per[op.eng].append(op)
        self.stats = {e: len(per[e]) for e in engs}

        def run(ename, eng):
            seen = {}
            for op in per[ename]:
                waits = {}
                for d in op.deps:
                    dd = ops[d]
                    if dd.eng == "pe" and ename == "pe" and not (dd.dma or dd.cc) and not (op.dma or op.cc):
                        continue
                    sem, val = dd.ev
                    key = id(sem)
                    if seen.get(key, 0) >= val:
                        continue
                    if key not in waits or waits[key][1] < val:
                        waits[key] = (sem, val)
                for key, (sem, val) in waits.items():
                    eng.wait_ge(sem, val)
                    seen[key] = val
                ins = op.fn(eng)
                if op.cc:
                    ins.then_inc(op.ev[0])
                elif op.dma:
                    ins.then_inc(op.ev[0], 16)
                elif op.ev is not None:
                    ins.then_inc(op.ev[0], 1)
            if ename == "sp":
                for i in range(self.n_dma):
                    if dcnt[i]:
                        eng.wait_ge(dsem[i], dcnt[i])
                for cs in ccsems:
                    eng.wait_ge(cs, 1)

        with nc.Block() as block:
            @block.tensor
            def _(e):
                run("pe", e)

            @block.scalar
            def _(e):
                run("act", e)

            @block.vector
            def _(e):
                run("dve", e)

            @block.gpsimd
            def _(e):
                run("pool", e)

            @block.sync
            def _(e):
                run("sp", e)


ARENA = 150 * 1024


def build(debug=None):
    nc = bass.Bass("TRN2", target_bir_lowering=False)
    es = ExitStack()
    P = Prog(nc)

    in_names = []

    def din(name, shape, dt=F32):
        in_names.append(name)
        return nc.dram_tensor(name, list(shape), dt, kind="ExternalInput").ap()

    x_d = din("x", [NT, D])
    ctx_d = din("ctx", [CTX, D])
    xh_d = din("xhalo", [4, D])
    cvec_d = din("cvec", [128, 16])
    wada_d = din("w_ada", [D, 6 * D])
    bada_row_d = din("b_ada_row", [1, 6 * D])
    win_d = din("w_in", [D, 9280])
    identc_d = din("identc", [128, 128])
    cst_d = din("cst", [128, NCST])
    smallc_d = din("smallc", [128, NSMALL])
    convw_d = din("convw_fm", [128, 160])
    out_d = nc.dram_tensor("out", [NT, D], F32, kind="ExternalOutput").ap()
    cparts = [nc.dram_tensor("contribA", [128, 2048], F32), nc.dram_tensor("contribB", [128, 2048], F32),
              nc.dram_tensor("contribC", [128, 64], F32)]
    gparts = [nc.dram_tensor("gathA", [512, 2048], F32), nc.dram_tensor("gathB", [512, 2048], F32),
              nc.dram_tensor("gathC", [512, 64], F32)]

    def contrib_ap(g, d):
        return cparts[g // 4].ap()[:, ((g % 4) * 2 + d) * 256:((g % 4) * 2 + d + 1) * 256]

    def gath_ap(g, d):
        return gparts[g // 4].ap().rearrange("(r n) c -> n r c", n=128)[:, :, ((g % 4) * 2 + d) * 256:((g % 4) * 2 + d + 1) * 256]
    sctx_d = nc.dram_tensor("sctx", [128, 4096], F32).ap()
    part_d = nc.dram_tensor("partd", [128, 65536], BF16)
    rs_d = nc.dram_tensor("rsd", [32, 65536], BF16)
    yT_d = nc.dram_tensor("yTd", [16, 128, NT], BF16).ap()
    dbg = {}

    def dout(name, shape, dt):
        dbg[name] = nc.dram_tensor("dbg_" + name, list(shape), dt, kind="ExternalOutput").ap()
        return dbg[name]

    def sb(name, shape, dt=F32):
        return es.enter_context(nc.sbuf_tensor(name, list(shape), dt))

    def ps(name, shape, dt=F32):
        return es.enter_context(nc.psum_tensor(name, list(shape), dt))

    arena = sb("arena", [128, ARENA // 2], BF16)

    class Bump:
        def __init__(self, base=0):
            self.off = base

        def __call__(self, shape, dt=BF16):
            n = int(np.prod(shape[1:]))
            esz = 2 if dt == BF16 else 4
            nb = (n * esz + 31) // 32 * 32
            assert self.off + nb <= ARENA, ("arena overflow", self.off + nb)
            v = arena[0:shape[0], self.off // 2: self.off // 2 + n * esz // 2]
            self.off += nb
            if dt != BF16:
                v = v.bitcast(dt)
            if len(shape) == 3:
                v = v.rearrange("p (a b) -> p a b", a=shape[1])
            elif len(shape) == 4:
                v = v.rearrange("p (a b c) -> p a b c", a=shape[1], b=shape[2])
            return v

    def MM(out, lhsT, rhs, start, stop, reads, writes):
        P.add("pe", lambda e: e.matmul(out, lhsT=lhsT, rhs=rhs, start=start, stop=stop), reads, writes)

    def TR(out, in_, ident, reads, writes):
        P.add("pe", lambda e: e.transpose(out=out, in_=in_, identity=ident), reads, writes)

    def ACTF(out, in_, func, reads, writes, bias=None, scale=None, accum=None):
        kw = {}
        if bias is not None:
            kw["bias"] = bias
        if scale is not None:
            kw["scale"] = scale
        if accum is not None:
            kw["accum_out"] = accum
        P.add("act", lambda e: e.activation(out=out, in_=in_, func=func, **kw), reads, writes)

    def CP(eng, out, in_, reads, writes):
        if eng == "act":
            P.add("act", lambda e: e.copy(out=out, in_=in_), reads, writes)
        else:
            P.add(eng, lambda e: e.tensor_copy(out=out, in_=in_), reads, writes)

    def TT(out, in0, in1, op, reads, writes, eng="dve"):
        P.add(eng, lambda e: e.tensor_tensor(out=out, in0=in0, in1=in1, op=op), reads, writes)

    def TS(out, in0, s1, op0, reads, writes, s2=None, op1=None, eng="dve"):
        if op1 is None:
            P.add(eng, lambda e: e.tensor_scalar(out=out, in0=in0, scalar1=s1, scalar2=None, op0=op0), reads, writes)
        else:
            P.add(eng, lambda e: e.tensor_scalar(out=out, in0=in0, scalar1=s1, scalar2=s2, op0=op0, op1=op1), reads, writes)

    def STT(out, in0, scalar, in1, op0, op1, reads, writes, eng="dve"):
        P.add(eng, lambda e: e.scalar_tensor_tensor(out=out, in0=in0, scalar=scalar, in1=in1, op0=op0, op1=op1),
              reads, writes)

    def RCP(out, in_, reads, writes):
        P.add("dve", lambda e: e.reciprocal(out=out, in_=in_), reads, writes)

    def DMA(q, out, in_, reads, writes):
        P.add(q, lambda e: e.dma_start(out=out, in_=in_), reads, writes, dma=True)

    def MEMSET(ap, val, writes):
        P.add("dve", lambda e: e.memset(ap, val), (), writes)

    def bc4(ap4, n=64):
        return ap4.unsqueeze(2).to_broadcast([128, 4, n])

    ident_f = sb("ident_f", [128, 128], F32)
    ones_f = sb("ones_f", [128, 128], F32)
    epsc = sb("epsc", [128, 2])
    cst = sb("cst_sb", [128, NCST], BF16)
    smallc = sb("smallc_sb", [128, NSMALL])
    convw = sb("convw_sb", [128, 160])
    mods = sb("mods", [128, 48, 2])
    a1 = sb("a1", [128, 8, 2])
    a2 = sb("a2", [128, 8, 2])
    gbc = sb("gbc", [128, 2048])
    hT = sb("hT", [128, 8, HW], BF16)
    ssq = sb("ssq", [128, 32])
    rstd = sb("rstd", [128, 32])
    ident_bf = cst[:, 0:128]
    triU = cst[:, 128:256]
    triL = cst[:, 256:384]
    ones_bf = cst[:, 384:512]
    mask4 = cst[:, 512:1536].rearrange("p (d n) -> p d n", d=2)
    W2 = cst[:, 1536:1792]
    Wch = cst[:, 1792:2304]
    CrBlk = cst[:, 2304:3328].rearrange("p (c a k) -> p c a k", c=4, a=2)
    bada_fm = smallc[:, 0:48]
    n1g = smallc[:, 48:56]
    n2g = smallc[:, 56:64]
    convb = smallc[:, 64:96]
    ssdg = smallc[:, 96:112]
    dskip = smallc[:, 112:144]
    hmask = smallc[:, 144:146]
    dtb = smallc[:, 146:147]
    alog = smallc[:, 147:148]
    cmask = smallc[:, 148:404].rearrange("p (r c) -> p r c", r=4)
    dtb_bc = smallc[:, 404:468]
    alog_bc = smallc[:, 468:532]

    psA = [ps("psA%d" % i, [128, 512]) for i in range(7)]
    psT = ps("psT", [128, 8, 128], BF16)

    DMA("sp", ident_f[:], identc_d[:, :], [], ["ident_f"])
    DMA("pool", cst[:], cst_d[:, :], [], ["cst"])
    DMA("sp", smallc[:], smallc_d[:, :], [], ["smallc"])
    DMA("sp", convw[:], convw_d[:, :], [], ["convw"])
    MEMSET(ones_f[:], 1.0, ["ones_f"])
    MEMSET(epsc[:, 0:1], EPS, ["epsc"])
    MEMSET(epsc[:, 1:2], 1.0, ["epsc"])

    win_v = win_d.rearrange("(k p) n -> p k n", p=128)

    B = Bump()
    wada = [B([128, 8, 512], F32) for _ in range(2)]
    cv = B([128, 16], F32)
    scv = B([128, 16], F32)
    screp = B([128, 8, 128], F32)
    bada_row = B([1, 2048], F32)
    xt = [B([128, D], F32) for _ in range(3)]
    xn = [B([128, D], BF16) for _ in range(2)]
    junk = B([128, D], BF16)

    DMA("sp", cv, cvec_d[:, :], [], ["cv"])
    DMA("sp", bada_row[:, 0:1024], bada_row_d[:, 2048:3072], [], ["bada_row"])
    DMA("sp", bada_row[:, 1024:2048], bada_row_d[:, 5120:6144], [], ["bada_row"])
    ACTF(scv, cv, AF.Silu, ["cv"], ["scv"])
    CP("dve", screp, scv[:, 0:8].unsqueeze(2).to_broadcast([128, 8, 128]), ["scv"], ["screp"])
    wada_v = wada_d.rearrange("(k p) n -> p k n", p=128)
    scv3 = scv.rearrange("p (v k) -> p k v", v=2)
    mods_ps = psA[3]
    blk_order = [0, 1, 2, 3, 6, 7, 8, 9, 4, 5, 10, 11]
    for bi, blk in enumerate(blk_order):
        wt = wada[bi % 2]
        wk = "wada%d" % (bi % 2)
        DMA("sp", wt, wada_v[:, :, blk * 512:(blk + 1) * 512], [], [wk])
        if blk in (4, 5, 10, 11):
            gi = {4: 0, 5: 1, 10: 2, 11: 3}[blk]
            pst = psA[gi % 2]
            pk = "psA%d" % (gi % 2)
            for k in range(8):
                MM(pst[:, :], screp[:, k, :], wt[:, k, :], k == 0, False, ["screp", wk], [pk])
            MM(pst[:, :], ones_f[0:1, :], bada_row[0:1, gi * 512:(gi + 1) * 512], False, True, ["ones_f", "bada_row"], [pk])
            CP("act", gbc[:, gi * 512:(gi + 1) * 512], pst[:, :], [pk], ["gbc"])
        else:
            for jj in range(4):
                j = blk * 4 + jj
                for k in range(8):
                    MM(mods_ps[:, 2 * j:2 * j + 2], wt[:, k, jj * 128:(jj + 1) * 128], scv3[:, k, :], k == 0, k == 7,
                       ["scv", wk], ["psA3"])
    mps3 = mods_ps[:, 0:96].rearrange("p (j v) -> p j v", v=2)
    for lo in (0, 24):
        TT(mods[:, lo:lo + 16, :], mps3[:, lo:lo + 16, :], bada_fm[:, lo:lo + 16].unsqueeze(2).to_broadcast([128, 16, 2]),
           ALU.add, ["psA3", "smallc"], ["mods"])
    for (aa, off, ng) in ((a1, 8, n1g), (a2, 32, n2g)):
        TS(aa[:], mods[:, off:off + 8, :], 1.0, ALU.add, ["mods"], ["a12"])
        TT(aa[:], aa[:], ng.unsqueeze(2).to_broadcast([128, 8, 2]), ALU.mult, ["a12", "smallc"], ["a12"])

    def norm_core(ti, xin, xk, nrows, dst_fn, v, acol, shoff, dkeys, inv_n=1.0 / D):
        s2 = ti % 2
        nk = "xn%d" % s2
        xnn = xn[s2]
        ACTF(junk[0:nrows, :], xin, AF.Square, [xk], ["junk", ("ssq", ti)], accum=ssq[0:nrows, ti:ti + 1])
        ACTF(rstd[0:nrows, ti:ti + 1], ssq[0:nrows, ti:ti + 1], AF.Sqrt, [("ssq", ti), "epsc"], [("rstd", ti)],
             bias=epsc[0:nrows, 0:1], scale=inv_n)
        RCP(rstd[0:nrows, ti:ti + 1], rstd[0:nrows, ti:ti + 1], [("rstd", ti)], [("rstd", ti)])
        TS(xnn[0:nrows, :], xin, rstd[0:nrows, ti:ti + 1], ALU.mult, [xk, ("rstd", ti)], [nk])
        for k in range(8):
            TR(psT[:, k, 0:nrows], xnn[0:nrows, k * 128:(k + 1) * 128], ident_bf[0:nrows, 0:nrows], [nk, "cst"], ["psT"])
        for k in range(8):
            ACTF(dst_fn(k), psT[:, k, 0:nrows], AF.Identity, ["psT", "a12", "mods"], dkeys,
                 bias=mods[:, shoff + k, v:v + 1], scale=acol[:, k, v:v + 1])

    def norm_tile(ti, src_ap, nrows, col0, v):
        s3 = ti % 3
        xk = "xt%d" % s3
        DMA("sp", xt[s3][0:nrows, :], src_ap, [], [xk])
        norm_core(ti, xt[s3][0:nrows, :], xk, nrows, lambda k: hT[:, k, col0:col0 + nrows], v, a1, 0,
                  [("hT", col0 // 128)])

    norm_tile(18, xh_d[:, :], 4, HALO0, 0)
    for t in range(NTILE):
        norm_tile(t, x_d[t * 128:(t + 1) * 128, :], 128, t * 128, 0)
    for t in range(2):
        norm_tile(16 + t, ctx_d[t * 128:(t + 1) * 128, :], 128, NT + t * 128, 1)

    def hk(tb):
        return [("hT", 4 * tb + i) for i in range(4)]

    P.fence()
    STOP = [False]

    class WPool:
        def __init__(self, bufs, name):
            self.bufs, self.name, self.i = bufs, name, 0

        def load(self, src_ap, shape_sel=None):
            i = self.i % len(self.bufs)
            self.i += 1
            buf = self.bufs[i]
            key = "%s%d" % (self.name, i)
            dst = buf if shape_sel is None else shape_sel(buf)
            DMA("pool", dst, src_ap, [], [key])
            return buf, key

    if debug not in ("ssd", "pd") and not os.environ.get("KSKIPF"):
        B = Bump()
        wfft = B([128, 8, 1024])
        f_tm = [B([128, 1024]) for _ in range(2)]
        Z = B([128, 8, 2, NT])
        Y = [B([128, 2, 1024]) for _ in range(2)]
        Pq = [B([128, 4, 1024]) for _ in range(2)]
        for hf in range(2):
            DMA("pool", wfft[:, :, hf * 512:(hf + 1) * 512], win_v[:, :, FFT_START + hf * 512:FFT_START + (hf + 1) * 512],
                [], [("wfft", hf)])
        for t in range(NTILE):
            ft = f_tm[t % 2]
            fk = "f_tm%d" % (t % 2)
            for hf in range(2):
                pst, pk = psA[hf], "psA%d" % hf
                for k in range(8):
                    MM(pst[:, :], hT[:, k, t * 128:(t + 1) * 128], wfft[:, k, hf * 512:(hf + 1) * 512], k == 0, k == 7,
                       [("hT", t), ("wfft", hf)], [pk])
                CP("act", ft[:, hf * 512:(hf + 1) * 512], pst[:, :], [pk], [fk])
            for gp in range(4):
                pst, pk = psA[2 + gp % 2], "psA%d" % (2 + gp % 2)
                for gi in range(2):
                    g = 2 * gp + gi
                    MM(pst[:, gi * 256:(gi + 1) * 256], ft[:, g * 128:(g + 1) * 128], W2, True, True, [fk, "cst"], [pk])
                for gi in range(2):
                    for ab in range(2):
                        zo = Z[:, 2 * gp + gi, ab, :].rearrange("p (q c r) -> p r q c", q=16, c=4, r=32)[:, 2 * t:2 * t + 2, :, :]
                        zi = pst[:, gi * 256 + ab * 128:gi * 256 + (ab + 1) * 128].rearrange("p (r q c) -> p r q c", r=2, q=16, c=4)
                        CP("dve" if gp % 2 else "act", zo, zi, [pk], [("Z", t)])
        zkeys = [("Z", t) for t in range(NTILE)]
        for quad in range(16):
            Yq, yk = Y[quad % 2], "Y%d" % (quad % 2)
            Yv = Yq.rearrange("p a (g k) -> p g a k", g=8)
            for gp in range(4):
                pst, pk = psA[gp % 2], "psA%d" % (gp % 2)
                for gi in range(2):
                    g = 2 * gp + gi
                    for ab in range(2):
                        zsel = Z[:, g, ab, quad * 128:(quad + 1) * 128]
                        MM(pst[:, gi * 256:(gi + 1) * 256], zsel, Wch[:, ab * 256:(ab + 1) * 256], ab == 0, ab == 1,
                           zkeys + ["cst"], [pk])
                CP("dve" if gp % 2 else "act", Yv[:, 2 * gp:2 * gp + 2, :, :],
                   pst[:, :].rearrange("p (g a k) -> p g a k", g=2, a=2), [pk], [yk])
            Pqq, pqk = Pq[quad % 2], "Pq%d" % (quad % 2)
            for c4 in range(4):
                for hf in range(2):
                    pst, pk = psA[2 + hf], "psA%d" % (2 + hf)
                    MM(pst[:, :], CrBlk[:, c4, 0, :], Yq[:, 0, hf * 512:(hf + 1) * 512], True, False, [yk, "cst"], [pk])
                    MM(pst[:, :], CrBlk[:, c4, 1, :], Yq[:, 1, hf * 512:(hf + 1) * 512], False, True, [yk, "cst"], [pk])
                    CP("dve" if hf else "act", Pqq[:, c4, hf * 512:(hf + 1) * 512], pst[:, :], [pk], [pqk])
            DMA("sp", part_d.ap()[:, quad * 4096:(quad + 1) * 4096], Pqq.rearrange("p c k -> p (c k)"), [pqk], ["part_d"])
        P.add("pool", lambda e: e.collective_compute("ReduceScatter", ALU.add, replica_groups=GROUPS,
                                                     ins=[part_d.ap().opt()], outs=[rs_d.ap().opt()]),
              ["part_d"], ["rs_d"], cc=True)
        if debug == "fft":
            dd = dout("rs", [32, 65536], BF16)
            tmpb = Bump(100 * 1024)
            tb_ = tmpb([32, 16384])
            for i in range(4):
                DMA("sp", tb_, rs_d.ap()[:, i * 16384:(i + 1) * 16384], ["rs_d"], ["tb_"])
                DMA("sp", dd[:, i * 16384:(i + 1) * 16384], tb_, ["tb_"], ["dd"])
            STOP[0] = True
        P.fence()

    if not STOP[0] and debug != "pd":
        B = Bump()
        dt_tm = B([128, 18, 64], F32)
        negA = B([128, 18, 64], F32)
        expA = B([128, 18, 64], F32)
        dec_bc = B([128, 18, 64], F32)
        dtw = B([128, 18, 64], F32)
        dta_bf = B([128, 18, 64])
        Dcore = B([128, 64], F32)
        Dm = B([128, 4, 64], F32)
        wq = WPool([B([128, 8, 256]) for _ in range(4)], "wq")
        u_sb = [B([128, UW]) for _ in range(2)]
        acc = B([128, 2048], F32)
        diagw = [B([128, 5, 128]) for _ in range(2)]
        xbcT = B([128, 4, CO])
        xsB = B([128, 18, 384])
        Eb_all = B([128, 16, 256])
        E = [B([128, 256], F32) for _ in range(2)]
        Ebf0 = B([128, 256])
        Fg = [B([128, 4, 256], F32) for _ in range(2)]
        xdtS = [B([128, 256]) for _ in range(2)]
        xdt = [[B([128, 256]) for _ in range(2)] for _ in range(2)]
        xsD = [B([128, 256]) for _ in range(2)]
        drep = [B([128, 8, 128]) for _ in range(2)]
        Lt = [[B([128, 4, 128]) for _ in range(2)] for _ in range(2)]
        GT = [[B([128, 4, 128]) for _ in range(2)] for _ in range(2)]
        T0 = B([128, 256], F32)
        T1 = B([128, 256], F32)
        YG = B([128, 256], F32)
        SZ = [B([128, 256], F32) for _ in range(2)]
        YN = B([128, 256])
        yTg = B([128, 2, NT])
        ss2 = B([128, 16], F32)
        r2 = B([128, 16], F32)
        SCR = B.off
        B2 = Bump(SCR)
        wdt = B2([128, 8, 64])
        nega_bc = B2([128, 64], F32)
        tmpE = B2([128, 64], F32)
        Asb = B2([128, 128], F32)
        DMA("pool", wdt, win_v[:, :, DT_START:DT_START + 64], [], ["wdt"])
        ACTF(nega_bc, alog_bc, AF.Exp, ["smallc"], ["nega"])
        TS(nega_bc, nega_bc, -1.0, ALU.mult, ["nega"], ["nega"])
        psD = psA[6]
        psE = psA[5]
        KDT = float(os.environ.get("KDT", "9"))
        for t in range(18 if KDT >= 2 else 0):
            c0 = t * 128
            for k in range(8):
                MM(psD[:, 128:192], hT[:, k, c0:c0 + 128], wdt[:, k, :], k == 0, k == 7, ["wdt", ("hT", t)], ["psDt"])
            TT(tmpE, psD[:, 128:192], dtb_bc, ALU.add, ["psDt", "smallc"], ["tmpE"])
            ACTF(tmpE, tmpE, AF.Exp, ["tmpE"], ["tmpE"])
            ACTF(dt_tm[:, t, :], tmpE, AF.Ln, ["tmpE", "epsc"], [("dt_tm", t)], bias=epsc[:, 1:2])
            TT(dta_bf[:, t, :], dt_tm[:, t, :], nega_bc, ALU.mult, [("dt_tm", t), "nega"], [("dta_bf", t)])
            if KDT < 3:
                continue
            MM(psD[:, 0:32], triU, dta_bf[:, t, 0:32], True, True, ["cst", ("dta_bf", t)], ["psDa"])
            MM(psD[:, 32:64], triL, dta_bf[:, t, 32:64], True, True, ["cst", ("dta_bf", t)], ["psDa"])
            MM(psD[:, 64:128], ones_bf, dta_bf[:, t, :], True, True, ["cst", ("dta_bf", t)], ["psDb"])
            if t < 16 and not os.environ.get("KNOPSE"):
                MM(psE[:, 0:64], ones_bf, dta_bf[:, t, :], t == 0, t == 15, ["cst", ("dta_bf", t)], ["psE"])
            if KDT < 3.2:
                continue
            CP("dve", Asb, psD[:, 0:128], ["psDa"], ["Asb"])
            TS(negA[:, t, :], Asb[:, 0:64], -1.0, ALU.mult, ["Asb"], [("negA", t)])
            if KDT < 3.4:
                continue
            ACTF(expA[:, t, :], Asb[:, 0:64], AF.Exp, ["Asb"], [("expA", t)])
            ACTF(dec_bc[:, t, :], Asb[:, 64:128], AF.Exp, ["Asb"], [("dec", t)])
            if KDT < 3.6:
                continue
            TT(dtw[:, t, :], Asb[:, 64:128], negA[:, t, :], ALU.add, ["Asb", ("negA", t)], [("dtw", t)])
            ACTF(dtw[:, t, :], dtw[:, t, :], AF.Exp, [("dtw", t)], [("dtw", t)])
            TT(dtw[:, t, :], dtw[:, t, :], dt_tm[:, t, :], ALU.mult, [("dtw", t), ("dt_tm", t)], [("dtw", t)])
        if KDT >= 4:
            ACTF(Dcore, psE[:, 0:64], AF.Exp, ["psE"], ["Dcore"])
            DMA("sp", cparts[2].ap()[:, :], Dcore, ["Dcore"], ["contrib"])
        for ub in (u_sb if KDT >= 5 else []):
            MEMSET(ub[:, 2052:2054], 0.0, ["u_sb0", "u_sb1"])
            MEMSET(ub[:, 2310:2312], 0.0, ["u_sb0", "u_sb1"])
        P.fence()

        uctr = [0]
        dctr = [0]

        def ssd_prep(g, p2):
            chunks = [(2 * g, 256 * g), (2 * g + 1, 256 * g + 128), (16 + g, 2048 + 128 * g)]
            if p2:
                chunks.append((24 + g, 3072 + 128 * g))
            CW = NT if p2 else CO
            for ci, (cch, col) in enumerate(chunks):
                wb, wk = wq.load(win_v[:, :, col:col + 128], lambda b: b[:, :, 0:128])
                ui = uctr[0] % 2
                uctr[0] += 1
                ub, uk = u_sb[ui], "u_sb%d" % ui
                for tb in range(4):
                    pst, pk = psA[tb % 2], "psA%d" % (tb % 2)
                    for k in range(8):
                        MM(pst[:, :], wb[:, k, 0:128], hT[:, k, tb * 512:(tb + 1) * 512], k == 0, k == 7, [wk] + hk(tb), [pk])
                    CP("act", ub[:, 2 + tb * 512:2 + (tb + 1) * 512], pst[:, :], [pk], [uk])
                if not p2:
                    pst, pk = psA[0], "psA0"
                    for k in range(8):
                        MM(pst[:, 0:256], wb[:, k, 0:128], hT[:, k, NT:NT + 256], k == 0, k == 7,
                           [wk, ("hT", 16), ("hT", 17)], [pk])
                    CP("act", ub[:, 2054:2310], pst[:, 0:256], [pk], [uk])
                pst, pk = psA[1], "psA1"
                for k in range(8):
                    MM(pst[:, 0:4], wb[:, k, 0:128], hT[:, k, HALO0:HALO0 + 4], k == 0, k == 7, [wk, ("hT", 18)], [pk])
                TS(ub[:, 0:2], pst[:, 0:2], hmask[:, 0:1], ALU.mult, [pk, "smallc"], [uk])
                TS(ub[:, 2050:2052], pst[:, 2:4], hmask[:, 1:2], ALU.mult, [pk, "smallc"], [uk])
                di = dctr[0] % 2
                dctr[0] += 1
                dg, dk = diagw[di], "diagw%d" % di
                for k in range(5):
                    TS(dg[:, k, :], ident_bf, convw[:, cch * 5 + k:cch * 5 + k + 1], ALU.mult, ["cst", "convw"], [dk])
                blocks = [(tb * 512, 512) for tb in range(4)] + ([] if p2 else [(CTXO, 256)])
                for bi, (o0, n) in enumerate(blocks):
                    pst, pk = psA[2 + bi % 2], "psA%d" % (2 + bi % 2)
                    for k in range(5):
                        MM(pst[:, 0:n], dg[:, k, :], ub[:, o0 + k:o0 + k + n], k == 0, k == 4, [dk, uk], [pk])
                    ACTF(xbcT[:, ci, o0:o0 + n], pst[:, 0:n], AF.Silu, [pk, "smallc"], [("xbcT", ci)], bias=convb[:, cch:cch + 1])
            for t in (range(16) if p2 else range(18)):
                c0 = t * 128 if t < 16 else CTXO + (t - 16) * 128
                for ci in range(3):
                    TR(psT[:, ci, :], xbcT[:, ci, c0:c0 + 128], ident_bf, [("xbcT", ci), "cst"], ["psT03"])
                CP("act", xsB[:, t, :].rearrange("p (c k) -> p c k", c=3), psT[:, 0:3, :], ["psT03"], [("xsB", t)])

        EL = os.environ.get("KEL", "dve")

        def ss_a(g, t, d, si):
            c0 = d * 32 + 4 * g
            xs4 = xsB[:, t, 0:256].rearrange("p (j q) -> p j q", j=4)
            psS = psA[5][:, 0:256] if si == 0 else psA[6][:, 0:256]
            TT(xdtS[si].rearrange("p (j q) -> p j q", j=4), xs4, bc4(dtw[:, t, c0:c0 + 4]), ALU.mult,
               [("xsB", t), ("dtw", t)], ["xdtS%d" % si], eng=EL)
            MM(psS, xsB[:, t, 256:384], xdtS[si], True, True, [("xsB", t), "xdtS%d" % si], ["psS%d" % si])

        def ss_b(g, t, d, si, first):
            c0 = d * 32 + 4 * g
            psS = psA[5][:, 0:256] if si == 0 else psA[6][:, 0:256]
            pk = "psS%d" % si
            if first:
                CP("act", E[d], psS, [pk], [("E", d)])
            else:
                E4 = E[d].rearrange("p (j q) -> p j q", j=4)
                TT(E4, E4, bc4(dec_bc[:, t, c0:c0 + 4]), ALU.mult, [("E", d), ("dec", t)], [("E", d)])
                TT(E[d], E[d], psS, ALU.add, [("E", d), pk], [("E", d)])

        def run_states(g, d, tiles, first0, pre=None):
            tiles = list(tiles)
            ss_a(g, tiles[0], d, 0)
            for i, t in enumerate(tiles):
                if i + 1 < len(tiles):
                    ss_a(g, tiles[i + 1], d, (i + 1) % 2)
                if pre is not None:
                    pre(t)
                ss_b(g, t, d, i % 2, first0 and i == 0)

        def ssd_pass1(g):
            ssd_prep(g, False)
            for d, ctx_order, lat_order in ((1, (17, 16), range(15, -1, -1)), (0, (16, 17), range(16))):
                run_states(g, d, ctx_order, True)
                DMA("sp", sctx_d[:, (g * 2 + d) * 256:(g * 2 + d + 1) * 256], E[d], [("E", d)], ["sctx_d"])
                run_states(g, d, lat_order, True)
                DMA("sp", contrib_ap(g, d), E[d], [("E", d)], ["contrib"])

        NG = int(os.environ.get("KNG", "8"))
        if debug == "ssd_dt":
            dd = dout("dt", [128, 5, 18 * 64], F32)
            for i, arr in enumerate((dt_tm, negA, expA, dec_bc, dtw)):
                DMA("sp", dd[:, i, :], arr.rearrange("p t c -> p (t c)"), [(nm, t) for nm in ("dt_tm", "negA", "expA", "dec", "dtw") for t in range(18)], ["dd"])
            P.emit(es)
            es.close()
            P.in_names = in_names
            return nc, P
        for g in range(NG):
            ssd_pass1(g)
        if NG < 8:
            MEMSET(acc[:, 0:2048], 0.0, ["acc"])
            for gz in range(NG, 8):
                for dz in range(2):
                    DMA("sp", contrib_ap(gz, dz), acc[:, 0:256], ["acc"], ["contrib"])
        for ci in range(3):
            def _ag(e, ci=ci):
                return e.collective_compute("AllGather", ALU.bypass, replica_groups=GROUPS,
                                            ins=[cparts[ci].ap().opt()], outs=[gparts[ci].ap().opt()])
            P.add("pool", _ag, ["contrib", "rs_d", "ccchain"], ["gath", "ccchain"], cc=True)
        P.fence()
        DMA("sp", Dm, gparts[2].ap().rearrange("(r n) c -> n r c", n=128), ["gath"], ["Dm"])
        TS(Dm, Dm, -1.0, ALU.add, ["Dm"], ["Dm"])
        TT(Dm, Dm, cmask, ALU.mult, ["Dm", "smallc"], ["Dm"])
        TS(Dm, Dm, 1.0, ALU.add, ["Dm"], ["Dm"])

        def ssd_pass2(g):
            ssd_prep(g, True)
            wz, wzk = wq.load(win_v[:, :, Z_START + 256 * g:Z_START + 256 * (g + 1)])
            for d in range(2):
                c0 = d * 32 + 4 * g
                DMA("sp", Fg[d], gath_ap(g, d), ["gath"], [("Fg", d)])
                DMA("sp", E[d], sctx_d[:, (g * 2 + d) * 256:(g * 2 + d + 1) * 256], [], [("E", d)])
                E4 = E[d].rearrange("p (j q) -> p j q", j=4)
                for r in (range(4) if d == 0 else range(3, -1, -1)):
                    TT(E4, E4, bc4(Dm[:, r, c0:c0 + 4]), ALU.mult, [("E", d), "Dm"], [("E", d)])
                    STT(E[d], Fg[d][:, r, :], cmask[:, r, c0:c0 + 1], E[d], ALU.mult, ALU.add,
                        [("E", d), ("Fg", d), "smallc"], [("E", d)])
            run_states(g, 1, range(15, -1, -1), False,
                       pre=lambda t: CP("act", Eb_all[:, t, :], E[1], [("E", 1)], [("Eb", t)]))
            CP("act", Ebf0, E[0], [("E", 0)], ["Ebf0"])
            psC = psA[0][:, 0:128]
            psZ = psA[6][:, 0:256]
            psY = psA[1][:, 0:256]
            psO = psA[4]
            def h1a(t):
                p = t % 2
                cols = slice(t * 128, (t + 1) * 128)
                MM(psC, xbcT[:, 2, cols], xbcT[:, 3, cols], True, True, [("xbcT", 2), ("xbcT", 3)], ["psC"])
                for d in range(2):
                    CP(EL, drep[p][:, 4 * d:4 * d + 4, :],
                       dta_bf[:, t, d * 32 + 4 * g:d * 32 + 4 * g + 4].unsqueeze(2).to_broadcast([128, 4, 128]),
                       [("dta_bf", t)], [("drep", d, p)])
                for d in range(2):
                    c0 = d * 32 + 4 * g
                    pD = psA[2 + d]
                    for j in range(4):
                        MM(pD[:, 128 * j:128 * (j + 1)], ident_bf, mask4[:, d, 0:128], True, False, ["cst"], [("psD", d)])
                        MM(pD[:, 128 * j:128 * (j + 1)], drep[p][:, 4 * d + j, :], triU if d == 0 else triL, False, True,
                           [("drep", d, p), "cst"], [("psD", d)])
                    for j in range(4):
                        ACTF(Lt[d][p][:, j, :], pD[:, 128 * j:128 * (j + 1)], AF.Exp, [("psD", d), ("negA", t)], [("Lt", d, p)],
                             bias=negA[:, t, c0 + j:c0 + j + 1])
                for k in range(8):
                    MM(psZ, hT[:, k, cols], wz[:, k, :], k == 0, k == 7, [("hT", t), wzk], ["psZ"])
                ACTF(SZ[p], psZ, AF.Silu, ["psZ"], [("SZ", p)])

            def h1b(t):
                p = t % 2
                xs4 = xsB[:, t, 0:256].rearrange("p (j q) -> p j q", j=4)
                for d in range(2):
                    c0 = d * 32 + 4 * g
                    TT(GT[d][p], Lt[d][p], psC.unsqueeze(1).to_broadcast([128, 4, 128]), ALU.mult, [("Lt", d, p), "psC"],
                       [("GT", d, p)])
                    TT(xdt[d][p].rearrange("p (j q) -> p j q", j=4), xs4, bc4(dt_tm[:, t, c0:c0 + 4]), ALU.mult,
                       [("xsB", t), ("dt_tm", t)], [("xdt", d, p)], eng=EL)
                TT(xsD[p].rearrange("p (j q) -> p j q", j=4), xs4, bc4(dskip[:, 4 * g:4 * g + 4]), ALU.mult,
                   [("xsB", t), "smallc"], [("xsD", p)], eng=EL)

            def h2a(t):
                p = t % 2
                cols = slice(t * 128, (t + 1) * 128)
                for j in range(4):
                    js = slice(64 * j, 64 * (j + 1))
                    MM(psY[:, js], ident_bf, xsD[p][:, js], True, False, ["cst", ("xsD", p)], ["psY"])
                    MM(psY[:, js], GT[0][p][:, j, :], xdt[0][p][:, js], False, False, [("GT", 0, p), ("xdt", 0, p)], ["psY"])
                    MM(psY[:, js], GT[1][p][:, j, :], xdt[1][p][:, js], False, True, [("GT", 1, p), ("xdt", 1, p)], ["psY"])
                MM(psO[:, 0:256], xbcT[:, 3, cols], Ebf0, True, True, [("xbcT", 3), "Ebf0"], ["psO0"])
                MM(psO[:, 256:512], xbcT[:, 3, cols], Eb_all[:, t, :], True, True, [("xbcT", 3), ("Eb", t)], ["psO1"])
                ss_a(g, t, 0, 0)
                ss_b(g, t, 0, 0, False)
                CP("act", Ebf0, E[0], [("E", 0)], ["Ebf0"])
                TT(T0.rearrange("p (j q) -> p j q", j=4), psO[:, 0:256].rearrange("p (j q) -> p j q", j=4),
                   bc4(expA[:, t, 4 * g:4 * g + 4]), ALU.mult, ["psO0", ("expA", t)], ["T0"])
                TT(T1.rearrange("p (j q) -> p j q", j=4), psO[:, 256:512].rearrange("p (j q) -> p j q", j=4),
                   bc4(expA[:, t, 32 + 4 * g:32 + 4 * g + 4]), ALU.mult, ["psO1", ("expA", t)], ["T1"])
                TT(T0, T0, T1, ALU.add, ["T0", "T1"], ["T0"])
                TT(T0, T0, psY, ALU.add, ["T0", "psY"], ["T0"])
                TT(YG, T0, SZ[p], ALU.mult, ["T0", ("SZ", p)], ["YG"])
                ACTF(T1, YG, AF.Square, ["YG"], ["T1", ("ss2", t)], accum=ss2[:, t:t + 1])
                ACTF(r2[:, t:t + 1], ss2[:, t:t + 1], AF.Sqrt, [("ss2", t), "epsc"], [("r2", t)], bias=epsc[:, 0:1], scale=1.0 / 256)

            def h2b(t):
                cols = slice(t * 128, (t + 1) * 128)
                RCP(r2[:, t:t + 1], r2[:, t:t + 1], [("r2", t)], [("r2", t)])
                TS(YN, YG, r2[:, t:t + 1], ALU.mult, ["YG", ("r2", t)], ["YN"])
                for i in range(2):
                    TR(psT[:, 4 + i, :], YN[:, i * 128:(i + 1) * 128], ident_bf, ["YN", "cst"], ["psT45"])
                for i in range(2):
                    ACTF(yTg[:, i, cols], psT[:, 4 + i, :], AF.Identity, ["psT45", "smallc"], ["yTg"],
                         scale=ssdg[:, 2 * g + i:2 * g + i + 1])

            h1a(0)
            h1b(0)
            for t in range(16):
                if t + 1 < 16:
                    h1a(t + 1)
                h2a(t)
                if t + 1 < 16:
                    h1b(t + 1)
                h2b(t)
            for i in range(2):
                DMA("sp", yT_d[2 * g + i, :, :], yTg[:, i, :], ["yTg"], ["yT_d"])

        for g in range(NG):
            ssd_pass2(g)
        if debug == "ssd":
            dd = dout("yT", [16, 128, NT], BF16)
            P.fence()
            tb_ = Bump(100 * 1024)([128, NT])
            for i in range(2 * NG):
                DMA("sp", tb_, yT_d[i, :, :], [], ["tb_"])
                DMA("sp", dd[i, :, :], tb_, ["tb_"], ["dd"])
            STOP[0] = True
        P.fence()

    if not STOP[0]:
        B = Bump()
        xt = [B([128, D], F32) for _ in range(2)]
        xn = [B([128, D], BF16) for _ in range(2)]
        junk = B([128, D], BF16)
        fgb = B([128, D], F32)
        xnew = [B([128, D], F32) for _ in range(4)]
        h2T = B([128, 8, 512])
        BASE = B.off
        B1 = Bump(BASE)
        wo = B1([128, 8, 1024])
        w8 = WPool([B1([128, 8, 128]) for _ in range(6)], "w8")
        w16 = WPool([B1([128, 16, 128]) for _ in range(2)], "w16")
        yTb = B1([128, 16, 512])
        fmT = B1([128, 8, 512])
        ftl = [B1([128, 1024]) for _ in range(2)]
        mergedT = B1([128, 8, 512])
        S0 = B1([128, 512], F32)
        S1 = B1([128, 512], F32)
        M0 = B1([128, 512], F32)
        M1 = B1([128, 512], F32)
        TMP = B1([128, 512], F32)
        B2 = Bump(BASE)
        wfi = WPool([B2([128, 8, 128]) for _ in range(4)], "wfi")
        wfo = [B2([128, 22, 512]) for _ in range(2)]
        actT = B2([128, 22, 512])
        SA = [B2([128, 512], F32) for _ in range(2)]
        xo = [B2([128, D], F32) for _ in range(4)]
        TMP2 = B2([128, 512], F32)
        ot = [B2([128, D], F32) for _ in range(2)]

        fg_d = din("fg_bc", [128, D])
        wssd_t = din("w_ssd_t", [8, 128, 16 * 128])
        wfft_t = din("w_fft_t", [8, 128, 8 * 128])
        wgate_t = din("w_gate_t", [16, 128, 8 * 128])
        wo_d = din("w_o", [D, D])
        wfi_t = din("w_ffn_in_t", [44, 128, 8 * 128])
        wfo_d = din("w_ffn_out", [D_FF, D])
        DMA("sp", fgb, fg_d[:, :], [], ["fgb"])
        if debug == "pd":
            yin = din("dbg_yT_in", [16, 128, NT], BF16)
            rin = din("dbg_rs_in", [32, 65536], BF16)
            tb_ = Bump(100 * 1024)([128, 16384])
            for i in range(16):
                DMA("sp", tb_[:, 0:NT], yin[i, :, :], [], ["tb_"])
                DMA("sp", yT_d[i, :, :], tb_[:, 0:NT], ["tb_"], ["yT_d"])
            for i in range(4):
                DMA("sp", tb_[0:32, :], rin[:, i * 16384:(i + 1) * 16384], [], ["tb_"])
                DMA("sp", rs_d.ap()[:, i * 16384:(i + 1) * 16384], tb_[0:32, :], ["tb_"], ["rs_d"])
            P.fence()
        wo_v = wo_d.rearrange("(k p) n -> p k n", p=128)
        wfo_v = wfo_d.rearrange("(j p) n -> p j n", p=128)
        yT_v = yT_d.rearrange("k p t -> p k t")

        for tb in range(int(os.environ.get("KTB", "4"))):
            tcols = slice(tb * 512, (tb + 1) * 512)
            DMA("sp", yTb, yT_v[:, :, tcols], [], ["yTb"])
            for hf in range(2):
                DMA("pool", wo[:, :, hf * 512:(hf + 1) * 512], wo_v[:, :, hf * 512:(hf + 1) * 512], [], [("wo", hf)])
            for tt in range(4):
                t = 4 * tb + tt
                ft, fk = ftl[tt % 2], "ftl%d" % (tt % 2)
                DMA("sp", ft, rs_d.ap()[2 * t:2 * t + 2, :].rearrange("r (c k) -> (r c) k", k=1024), ["rs_d"], [fk])
                for k in range(8):
                    TR(psT[:, k, :], ft[:, k * 128:(k + 1) * 128], ident_bf, [fk, "cst"], ["psT"])
                CP("act", fmT[:, :, tt * 128:(tt + 1) * 128], psT[:, :, :], ["psT"], ["fmT"])
            for fc in range(8):
                wss, wssk = w16.load(wssd_t[fc].rearrange("p (k c) -> p k c", k=16))
                wff, wffk = w8.load(wfft_t[fc].rearrange("p (k c) -> p k c", k=8))
                wg0, wg0k = w8.load(wgate_t[fc].rearrange("p (k c) -> p k c", k=8))
                wg1, wg1k = w8.load(wgate_t[8 + fc].rearrange("p (k c) -> p k c", k=8))
                for k in range(8):
                    MM(psA[2][:, :], wg0[:, k, :], hT[:, k, tcols], k == 0, k == 7, [wg0k] + hk(tb), ["psA2"])
                ACTF(S0, psA[2][:, :], AF.Sigmoid, ["psA2"], ["S0"])
                for k in range(8):
                    MM(psA[3][:, :], wg1[:, k, :], hT[:, k, tcols], k == 0, k == 7, [wg1k] + hk(tb), ["psA3"])
                ACTF(S1, psA[3][:, :], AF.Sigmoid, ["psA3"], ["S1"])
                for k in range(16):
                    MM(psA[0][:, :], wss[:, k, :], yTb[:, k, :], k == 0, k == 15, [wssk, "yTb"], ["psA0"])
                for k in range(8):
                    MM(psA[1][:, :], wff[:, k, :], fmT[:, k, :], k == 0, k == 7, [wffk, "fmT"], ["psA1"])
                TT(M0, S0, psA[0][:, :], ALU.mult, ["S0", "psA0"], ["M0"])
                TT(M1, S1, psA[1][:, :], ALU.mult, ["S1", "psA1"], ["M1"])
                TT(mergedT[:, fc, :], M0, M1, ALU.add, ["M0", "M1"], ["mergedT"])
            for tt in range(4):
                t = 4 * tb + tt
                xi = tt % 2
                xk = "xt%d" % xi
                DMA("sp", xt[xi], x_d[t * 128:(t + 1) * 128, :], [], [xk])
                for hf in range(2):
                    pst, pk = psA[4 + hf], "psA%d" % (4 + hf)
                    hs = slice(hf * 512, (hf + 1) * 512)
                    for k in range(8):
                        MM(pst[:, :], mergedT[:, k, tt * 128:(tt + 1) * 128], wo[:, k, hs], k == 0, k == 7,
                           ["mergedT", ("wo", hf)], [pk])
                    TT(TMP, pst[:, :], gbc[:, hs], ALU.mult, [pk, "gbc"], ["TMP"])
                    TT(xnew[tt][:, hs], TMP, xt[xi][:, hs], ALU.add, ["TMP", xk], [("xnew", tt)])
                norm_core(20 + tt, xnew[tt], ("xnew", tt), 128, lambda k, tt=tt: h2T[:, k, tt * 128:(tt + 1) * 128], 0, a2, 24,
                          ["h2T"])
            P.fence()
            DMA("pool", wfo[0], wfo_v[:, :, 0:512], [], [("wfo", 0)])
            for j in range(22):
                if j == 8:
                    DMA("pool", wfo[1], wfo_v[:, :, 512:1024], [], [("wfo", 1)])
                wa, wak = wfi.load(wfi_t[j].rearrange("p (k c) -> p k c", k=8))
                wb, wbk = wfi.load(wfi_t[22 + j].rearrange("p (k c) -> p k c", k=8))
                pa, pak = psA[2 * (j % 2)], "psA%d" % (2 * (j % 2))
                pb, pbk = psA[2 * (j % 2) + 1], "psA%d" % (2 * (j % 2) + 1)
                for k in range(8):
                    MM(pa[:, :], wa[:, k, :], h2T[:, k, :], k == 0, k == 7, [wak, "h2T"], [pak])
                for k in range(8):
                    MM(pb[:, :], wb[:, k, :], h2T[:, k, :], k == 0, k == 7, [wbk, "h2T"], [pbk])
                sa, sak = SA[j % 2], "SA%d" % (j % 2)
                ACTF(sa, pa[:, :], AF.Silu, [pak], [sak])
                TT(actT[:, j, :], sa, pb[:, :], ALU.mult, [sak, pbk], ["actT"])
            for hf in range(2):
                hs = slice(hf * 512, (hf + 1) * 512)
                for tt in range(4):
                    pst, pk = psA[4 + tt % 2], "psA%d" % (4 + tt % 2)
                    for j in range(22):
                        MM(pst[:, :], actT[:, j, tt * 128:(tt + 1) * 128], wfo[hf][:, j, :], j == 0, j == 21,
                           ["actT", ("wfo", hf)], [pk])
                    TT(TMP2, pst[:, :], gbc[:, 1024 + hf * 512:1024 + (hf + 1) * 512], ALU.mult, [pk, "gbc"], ["TMP2"])
                    TT(xo[tt][:, hs], TMP2, xnew[tt][:, hs], ALU.add, ["TMP2", ("xnew", tt)], [("xo", tt)])
            for tt in range(4):
                t = 4 * tb + tt
                ti = 24 + tt
                oi = tt % 2
                ACTF(junk, xo[tt], AF.Square, [("xo", tt)], ["junk", ("ssq", ti)], accum=ssq[:, ti:ti + 1])
                ACTF(rstd[:, ti:ti + 1], ssq[:, ti:ti + 1], AF.Sqrt, [("ssq", ti), "epsc"], [("rstd", ti)],
                     bias=epsc[:, 0:1], scale=1.0 / D)
                RCP(rstd[:, ti:ti + 1], rstd[:, ti:ti + 1], [("rstd", ti)], [("rstd", ti)])
                STT(ot[oi], xo[tt], rstd[:, ti:ti + 1], fgb, ALU.mult, ALU.mult, [("xo", tt), ("rstd", ti), "fgb"], [("ot", oi)])
                DMA("sp", out_d[t * 128:(t + 1) * 128, :], ot[oi], [("ot", oi)], ["out"])
            P.fence()

    P.emit(es)
    es.close()
    P.in_names = in_names
    return nc, P


def _consts(q):
    c = np.zeros((128, NCST), np.float32)
    i = np.arange(128)
    c[:, 0:128] = np.eye(128)
    c[:, 128:256] = (i[:, None] <= i[None, :])
    c[:, 256:384] = (i[:, None] >= i[None, :])
    c[:, 384:512] = 1.0
    mf = np.where(i[:, None] <= i[None, :], 0.0, -30000.0)
    mb = np.where(i[:, None] >= i[None, :], 0.0, -30000.0)
    c[:, 512:1024] = np.tile(mf, (1, 4))
    c[:, 1024:1536] = np.tile(mb, (1, 4))
    cc = np.arange(64)
    ang = 2 * np.pi * np.outer(cc, cc) / 64.0
    w2c = np.zeros((128, 128))
    w2s = np.zeros((128, 128))
    for r2 in range(2):
        w2c[r2 * 64:(r2 + 1) * 64, r2 * 64:(r2 + 1) * 64] = np.cos(ang) / 8.0
        w2s[r2 * 64:(r2 + 1) * 64, r2 * 64:(r2 + 1) * 64] = np.sin(ang) / 8.0
    c[:, 1536:1664] = w2c
    c[:, 1664:1792] = w2s
    angc = 2 * np.pi * np.outer(i, i) / 128.0
    Cc = np.cos(angc) / math.sqrt(128.0)
    Sc = np.sin(angc) / math.sqrt(128.0)
    c[:, 1792:1920] = Cc
    c[:, 1920:2048] = Sc
    c[:, 2048:2176] = -Sc
    c[:, 2176:2304] = Cc
    rl = np.arange(32)
    angr = 2 * np.pi * np.outer(32 * q + rl, i) / 128.0
    blk = np.zeros((128, 4, 2, 128))
    for c4 in range(4):
        blk[c4 * 32:(c4 + 1) * 32, c4, 0, :] = np.cos(angr) / math.sqrt(128.0)
        blk[c4 * 32:(c4 + 1) * 32, c4, 1, :] = -np.sin(angr) / math.sqrt(128.0)
    c[:, 2304:3328] = blk.reshape(128, 1024)
    return c


def _tiles(w):
    kk, nb = w.shape[0] // 128, w.shape[1] // 128
    return np.ascontiguousarray(w.reshape(kk, 128, nb, 128).transpose(2, 1, 0, 3).reshape(nb, 128, kk * 128))


def _prep_inputs(inp):
    f32 = np.float32
    x = np.asarray(inp["x"], f32)
    ctx = np.asarray(inp["ctx"], f32)
    c = np.asarray(inp["c"], f32)
    c_ctx = np.asarray(inp["c_ctx"], f32)
    b_ada = np.asarray(inp["b_ada"], f32)[0]
    conv_w = np.asarray(inp["conv_w"], f32)[0]
    shared = {
        "w_ada": np.ascontiguousarray(inp["w_ada"][0], f32),
        "b_ada_row": np.ascontiguousarray(b_ada.reshape(1, -1)),
        "w_in": np.ascontiguousarray(inp["w_in"][0], f32),
        "identc": np.eye(128, dtype=f32),
        "convw_fm": np.ascontiguousarray(conv_w.reshape(5, 32, 128).transpose(2, 1, 0).reshape(128, 160)),
        "fg_bc": np.ascontiguousarray(np.broadcast_to(np.asarray(inp["final_g"], f32)[None, :], (128, D))),
        "w_ssd_t": _tiles(np.asarray(inp["w_ssd_out"][0], f32)),
        "w_fft_t": _tiles(np.asarray(inp["w_fft_out"][0], f32)),
        "w_gate_t": _tiles(np.asarray(inp["w_in"][0], f32)[:, GATE_START:GATE_START + 2048]),
        "w_o": np.ascontiguousarray(inp["w_o"][0], f32),
        "w_ffn_in_t": _tiles(np.asarray(inp["w_ffn_in"][0], f32)),
        "w_ffn_out": np.ascontiguousarray(inp["w_ffn_out"][0], f32),
    }
    sm = np.zeros((128, NSMALL), f32)
    sm[:, 0:48] = b_ada.reshape(48, 128).T
    sm[:, 48:56] = np.asarray(inp["norm1_g"], f32)[0].reshape(8, 128).T
    sm[:, 56:64] = np.asarray(inp["norm2_g"], f32)[0].reshape(8, 128).T
    sm[:, 64:96] = np.asarray(inp["conv_b"], f32)[0].reshape(32, 128).T
    sm[:, 96:112] = np.asarray(inp["ssd_norm_g"], f32)[0].reshape(16, 128).T
    sm[:, 112:144] = np.asarray(inp["d_skip"], f32)[0][None, :]
    sm[:, 404:468] = np.asarray(inp["dt_bias"], f32)[0].reshape(1, 64)
    sm[:, 468:532] = np.asarray(inp["a_log"], f32)[0].reshape(1, 64)
    maps = []
    for core in range(8):
        b, q = core // 4, core % 4
        t0 = q * NT
        m = dict(shared)
        m["x"] = np.ascontiguousarray(x[b, t0:t0 + NT])
        m["ctx"] = np.ascontiguousarray(ctx[b])
        xh = np.zeros((4, D), f32)
        for i, tt in enumerate((t0 - 2, t0 - 1, t0 + NT, t0 + NT + 1)):
            if 0 <= tt < 8192:
                xh[i] = x[b, tt]
        m["xhalo"] = xh
        cv = np.zeros((128, 16), f32)
        cv[:, 0:8] = c[b].reshape(8, 128).T
        cv[:, 8:16] = c_ctx.reshape(8, 128).T
        m["cvec"] = cv
        s = sm.copy()
        s[:, 144] = 0.0 if q == 0 else 1.0
        s[:, 145] = 0.0 if q == 3 else 1.0
        cm = np.zeros((4, 64), f32)
        for r in range(4):
            cm[r, 0:32] = 1.0 if r < q else 0.0
            cm[r, 32:64] = 1.0 if r > q else 0.0
        s[:, 148:404] = cm.reshape(1, 256)
        m["smallc"] = s
        m["cst"] = _consts(q)
        maps.append(m)
    return maps


_CACHE = {}


def kernel(**inp):
    if "nc" not in _CACHE:
        _CACHE["nc"] = build()[0]
    nc = _CACHE["nc"]
    maps = _prep_inputs(inp)
    res = run_bass_kernel_spmd(nc, maps, core_ids=list(range(8)))
    out = np.zeros((2, 8192, D), np.float32)
    for core in range(8):
        b, q = core // 4, core % 4
        out[b, q * NT:(q + 1) * NT] = np.asarray(res.results[core]["out"], np.float32)
    return out
```

```python
import math
import os
from contextlib import ExitStack
import numpy as np
import ml_dtypes
import concourse.bass as bass
import concourse.mybir as mybir
from concourse.bass_utils import run_bass_kernel_spmd

F32 = mybir.dt.float32
BF16 = mybir.dt.bfloat16
AF = mybir.ActivationFunctionType
ALU = mybir.AluOpType
AX = mybir.AxisListType

D = 1024
NCST = 3328
NSMALL = 532
NT = 2048
NTILE = 16
CTX = 256
EPS = 1e-6
HW = NT + CTX + 4
HALO0 = NT + CTX
DT_START = 4096
Z_START = 4160
FFT_START = Z_START + 2048
GATE_START = FFT_START + 1024
D_FF = 2816
UW = 2312
CO = 2308
CTXO = 2052
GROUPS = [[0, 1, 2, 3], [4, 5, 6, 7]]


class _Op:
    pass


class Prog:
    def __init__(self, nc, n_dma=32):
        self.nc = nc
        self.ops = []
        self.lastw = {}
        self.rd_eng = {}
        self.rd_dma = {}
        self.n_dma = n_dma
        self.rr = 0
        self.rr_pool = 0
        self.last_on = [None] * n_dma
        self.last_eng = {}
        self.dma_since = []
        self.fence_dep = None
        self.cc_w = {}

    PSKEY = {"psC": "psA0", "psZ": "psA6", "psY": "psA1", "psS0": "psA5", "psS1": "psA6", ("psD", 0): "psA2",
             ("psD", 1): "psA3", "psO0": "psA4", "psO1": "psA4", "psDa": "psA6", "psDb": "psA6", "psDt": "psA6",
             "psE": "psA5", "psT03": "psT", "psT45": "psT", "psT7": "psT"}

    def add(self, eng, fn, reads=(), writes=(), dma=False, cc=False):
        reads = [self.PSKEY.get(k, k) for k in reads]
        writes = [self.PSKEY.get(k, k) for k in writes]
        op = _Op()
        op.eng, op.fn, op.dma, op.cc = eng, fn, dma, cc
        op.idx = len(self.ops)
        deps = set()
        for k in reads:
            w = self.lastw.get(k)
            if w is not None:
                deps.add(w)
            if isinstance(k, str) and k.startswith("ps"):
                for e2, i2 in self.rd_eng.get(k, {}).items():
                    if e2 != eng:
                        deps.add(i2)
        for k in writes:
            w = self.lastw.get(k)
            if w is not None:
                deps.add(w)
            deps.update(self.rd_eng.get(k, {}).values())
            deps.update(self.rd_dma.get(k, ()))
        if dma:
            if eng == "pool":
                s = self.n_dma - 8 + (self.rr_pool % 8)
                self.rr_pool += 1
            else:
                s = self.rr % (self.n_dma - 8)
                self.rr += 1
            op.dsem = s
            if self.last_on[s] is not None:
                deps.add(self.last_on[s])
            self.last_on[s] = op.idx
        if self.fence_dep is not None:
            deps.add(self.fence_dep)
        for k in list(reads) + list(writes):
            if k in self.cc_w:
                deps.add(self.cc_w[k])
        if cc:
            for k in writes:
                self.cc_w[k] = op.idx
        op.deps = deps
        if dma or cc:
            self.dma_since.append(op.idx)
        else:
            self.last_eng[eng] = op.idx
        for k in reads:
            if dma or cc:
                self.rd_dma.setdefault(k, []).append(op.idx)
            else:
                self.rd_eng.setdefault(k, {})[eng] = op.idx
        for k in writes:
            self.lastw[k] = op.idx
            self.rd_eng[k] = {}
            self.rd_dma[k] = []
        self.ops.append(op)
        return op

    def fence(self):
        deps = set(self.last_eng.values())
        deps.update(i for i in self.dma_since if not self.ops[i].cc)
        op = self.add("sp", lambda e: e.nop())
        op.deps |= deps
        self.fence_dep = op.idx
        self.dma_since = []
        self.lastw = {}
        self.rd_eng = {}
        self.rd_dma = {}
        self.last_on = [None] * self.n_dma

    def emit(self, es):
        nc = self.nc
        ops = self.ops
        engs = ("pe", "act", "dve", "pool", "sp")
        needs = [False] * len(ops)
        for op in ops:
            for d in op.deps:
                dd = ops[d]
                if dd.dma or dd.cc:
                    continue
                if dd.eng == "pe" and op.eng == "pe" and not (op.dma or op.cc):
                    continue
                needs[d] = True
        esem = {e: es.enter_context(nc.semaphore("s_" + e)) for e in engs}
        dsem = [es.enter_context(nc.semaphore("d%d" % i)) for i in range(self.n_dma)]
        ncc = sum(1 for op in ops if op.cc)
        ccsems = [es.enter_context(nc.semaphore("ccs%d" % i)) for i in range(ncc)]
        cnt = {e: 0 for e in engs}
        dcnt = [0] * self.n_dma
        cccnt = 0
        for op in ops:
            if op.cc:
                op.ev = (ccsems[cccnt], 1)
                cccnt += 1
            elif op.dma:
                dcnt[op.dsem] += 16
                op.ev = (dsem[op.dsem], dcnt[op.dsem])
            elif needs[op.idx]:
                cnt[op.eng] += 1
                op.ev = (esem[op.eng], cnt[op.eng])
            else:
                op.ev = None
        per = {e: [] for e in engs}
        for op in ops:
            per[op.eng].append(op)
        self.stats = {e: len(per[e]) for e in engs}

        def run(ename, eng):
            seen = {}
            for op in per[ename]:
                waits = {}
                for d in op.deps:
                    dd = ops[d]
                    if dd.eng == "pe" and ename == "pe" and not (dd.dma or dd.cc) and not (op.dma or op.cc):
                        continue
                    sem, val = dd.ev
                    key = id(sem)
                    if seen.get(key, 0) >= val:
                        continue
                    if key not in waits or waits[key][1] < val:
                        waits[key] = (sem, val)
                for key, (sem, val) in waits.items():
                    eng.wait_ge(sem, val)
                    seen[key] = val
                ins = op.fn(eng)
                if op.cc:
                    ins.then_inc(op.ev[0])
                elif op.dma:
                    ins.then_inc(op.ev[0], 16)
                elif op.ev is not None:
                    ins.then_inc(op.ev[0], 1)
            if ename == "sp":
                for i in range(self.n_dma):
                    if dcnt[i]:
                        eng.wait_ge(dsem[i], dcnt[i])
                for cs in ccsems:
                    eng.wait_ge(cs, 1)

        with nc.Block() as block:
            @block.tensor
            def _(e):
                run("pe", e)

            @block.scalar
            def _(e):
                run("act", e)

            @block.vector
            def _(e):
                run("dve", e)

            @block.gpsimd
            def _(e):
                run("pool", e)

            @block.sync
            def _(e):
                run("sp", e)


ARENA = 150 * 1024


def build(debug=None):
    nc = bass.Bass("TRN2", target_bir_lowering=False)
    es = ExitStack()
    P = Prog(nc)

    in_names = []

    def din(name, shape, dt=F32):
        in_names.append(name)
        return nc.dram_tensor(name, list(shape), dt, kind="ExternalInput").ap()

    x_d = din("x", [NT, D])
    ctx_d = din("ctx", [CTX, D])
    xh_d = din("xhalo", [4, D])
    cvec_d = din("cvec", [128, 16])
    wada_d = din("w_ada", [D, 6 * D])
    bada_row_d = din("b_ada_row", [1, 6 * D])
    win_d = din("w_in", [D, 9280])
    identc_d = din("identc", [128, 128])
    cst_d = din("cst", [128, NCST])
    smallc_d = din("smallc", [128, NSMALL])
    convw_d = din("convw_fm", [128, 160])
    out_d = nc.dram_tensor("out", [NT, D], F32, kind="ExternalOutput").ap()
    cparts = [nc.dram_tensor("contribA", [128, 2048], F32), nc.dram_tensor("contribB", [128, 2048], F32),
              nc.dram_tensor("contribC", [128, 64], F32)]
    gparts = [nc.dram_tensor("gathA", [512, 2048], F32), nc.dram_tensor("gathB", [512, 2048], F32),
              nc.dram_tensor("gathC", [512, 64], F32)]

    def contrib_ap(g, d):
        return cparts[g // 4].ap()[:, ((g % 4) * 2 + d) * 256:((g % 4) * 2 + d + 1) * 256]

    def gath_ap(g, d):
        return gparts[g // 4].ap().rearrange("(r n) c -> n r c", n=128)[:, :, ((g % 4) * 2 + d) * 256:((g % 4) * 2 + d + 1) * 256]
    sctx_d = nc.dram_tensor("sctx", [128, 4096], F32).ap()
    part_d = nc.dram_tensor("partd", [128, 65536], BF16)
    rs_d = nc.dram_tensor("rsd", [32, 65536], BF16)
    yT_d = nc.dram_tensor("yTd", [16, 128, NT], BF16).ap()
    dbg = {}

    def dout(name, shape, dt):
        dbg[name] = nc.dram_tensor("dbg_" + name, list(shape), dt, kind="ExternalOutput").ap()
        return dbg[name]

    def sb(name, shape, dt=F32):
        return es.enter_context(nc.sbuf_tensor(name, list(shape), dt))

    def ps(name, shape, dt=F32):
        return es.enter_context(nc.psum_tensor(name, list(shape), dt))

    arena = sb("arena", [128, ARENA // 2], BF16)

    class Bump:
        def __init__(self, base=0):
            self.off = base

        def __call__(self, shape, dt=BF16):
            n = int(np.prod(shape[1:]))
            esz = 2 if dt == BF16 else 4
            nb = (n * esz + 31) // 32 * 32
            assert self.off + nb <= ARENA, ("arena overflow", self.off + nb)
            v = arena[0:shape[0], self.off // 2: self.off // 2 + n * esz // 2]
            self.off += nb
            if dt != BF16:
                v = v.bitcast(dt)
            if len(shape) == 3:
                v = v.rearrange("p (a b) -> p a b", a=shape[1])
            elif len(shape) == 4:
                v = v.rearrange("p (a b c) -> p a b c", a=shape[1], b=shape[2])
            return v

    def MM(out, lhsT, rhs, start, stop, reads, writes):
        P.add("pe", lambda e: e.matmul(out, lhsT=lhsT, rhs=rhs, start=start, stop=stop), reads, writes)

    def TR(out, in_, ident, reads, writes):
        P.add("pe", lambda e: e.transpose(out=out, in_=in_, identity=ident), reads, writes)

    def ACTF(out, in_, func, reads, writes, bias=None, scale=None, accum=None):
        kw = {}
        if bias is not None:
            kw["bias"] = bias
        if scale is not None:
            kw["scale"] = scale
        if accum is not None:
            kw["accum_out"] = accum
        P.add("act", lambda e: e.activation(out=out, in_=in_, func=func, **kw), reads, writes)

    def CP(eng, out, in_, reads, writes):
        if eng == "act":
            P.add("act", lambda e: e.copy(out=out, in_=in_), reads, writes)
        else:
            P.add(eng, lambda e: e.tensor_copy(out=out, in_=in_), reads, writes)

    def TT(out, in0, in1, op, reads, writes, eng="dve"):
        P.add(eng, lambda e: e.tensor_tensor(out=out, in0=in0, in1=in1, op=op), reads, writes)

    def TS(out, in0, s1, op0, reads, writes, s2=None, op1=None, eng="dve"):
        if op1 is None:
            P.add(eng, lambda e: e.tensor_scalar(out=out, in0=in0, scalar1=s1, scalar2=None, op0=op0), reads, writes)
        else:
            P.add(eng, lambda e: e.tensor_scalar(out=out, in0=in0, scalar1=s1, scalar2=s2, op0=op0, op1=op1), reads, writes)

    def STT(out, in0, scalar, in1, op0, op1, reads, writes, eng="dve"):
        P.add(eng, lambda e: e.scalar_tensor_tensor(out=out, in0=in0, scalar=scalar, in1=in1, op0=op0, op1=op1),
              reads, writes)

    def RCP(out, in_, reads, writes):
        P.add("dve", lambda e: e.reciprocal(out=out, in_=in_), reads, writes)

    def DMA(q, out, in_, reads, writes):
        P.add(q, lambda e: e.dma_start(out=out, in_=in_), reads, writes, dma=True)

    def MEMSET(ap, val, writes):
        P.add("dve", lambda e: e.memset(ap, val), (), writes)

    def bc4(ap4, n=64):
        return ap4.unsqueeze(2).to_broadcast([128, 4, n])

    ident_f = sb("ident_f", [128, 128], F32)
    ones_f = sb("ones_f", [128, 128], F32)
    epsc = sb("epsc", [128, 2])
    cst = sb("cst_sb", [128, NCST], BF16)
    smallc = sb("smallc_sb", [128, NSMALL])
    convw = sb("convw_sb", [128, 160])
    mods = sb("mods", [128, 48, 2])
    a1 = sb("a1", [128, 8, 2])
    a2 = sb("a2", [128, 8, 2])
    gbc = sb("gbc", [128, 2048])
    hT = sb("hT", [128, 8, HW], BF16)
    ssq = sb("ssq", [128, 32])
    rstd = sb("rstd", [128, 32])
    ident_bf = cst[:, 0:128]
    triU = cst[:, 128:256]
    triL = cst[:, 256:384]
    ones_bf = cst[:, 384:512]
    mask4 = cst[:, 512:1536].rearrange("p (d n) -> p d n", d=2)
    W2 = cst[:, 1536:1792]
    Wch = cst[:, 1792:2304]
    CrBlk = cst[:, 2304:3328].rearrange("p (c a k) -> p c a k", c=4, a=2)
    bada_fm = smallc[:, 0:48]
    n1g = smallc[:, 48:56]
    n2g = smallc[:, 56:64]
    convb = smallc[:, 64:96]
    ssdg = smallc[:, 96:112]
    dskip = smallc[:, 112:144]
    hmask = smallc[:, 144:146]
    dtb = smallc[:, 146:147]
    alog = smallc[:, 147:148]
    cmask = smallc[:, 148:404].rearrange("p (r c) -> p r c", r=4)
    dtb_bc = smallc[:, 404:468]
    alog_bc = smallc[:, 468:532]

    psA = [ps("psA%d" % i, [128, 512]) for i in range(7)]
    psT = ps("psT", [128, 8, 128], BF16)

    DMA("sp", ident_f[:], identc_d[:, :], [], ["ident_f"])
    DMA("pool", cst[:], cst_d[:, :], [], ["cst"])
    DMA("sp", smallc[:], smallc_d[:, :], [], ["smallc"])
    DMA("sp", convw[:], convw_d[:, :], [], ["convw"])
    MEMSET(ones_f[:], 1.0, ["ones_f"])
    MEMSET(epsc[:, 0:1], EPS, ["epsc"])
    MEMSET(epsc[:, 1:2], 1.0, ["epsc"])

    win_v = win_d.rearrange("(k p) n -> p k n", p=128)

    B = Bump()
    wada = [B([128, 8, 512], F32) for _ in range(2)]
    cv = B([128, 16], F32)
    scv = B([128, 16], F32)
    screp = B([128, 8, 128], F32)
    bada_row = B([1, 2048], F32)
    xt = [B([128, D], F32) for _ in range(3)]
    xn = [B([128, D], BF16) for _ in range(2)]
    junk = B([128, D], BF16)

    DMA("sp", cv, cvec_d[:, :], [], ["cv"])
    DMA("sp", bada_row[:, 0:1024], bada_row_d[:, 2048:3072], [], ["bada_row"])
    DMA("sp", bada_row[:, 1024:2048], bada_row_d[:, 5120:6144], [], ["bada_row"])
    ACTF(scv, cv, AF.Silu, ["cv"], ["scv"])
    CP("dve", screp, scv[:, 0:8].unsqueeze(2).to_broadcast([128, 8, 128]), ["scv"], ["screp"])
    wada_v = wada_d.rearrange("(k p) n -> p k n", p=128)
    scv3 = scv.rearrange("p (v k) -> p k v", v=2)
    mods_ps = psA[3]
    blk_order = [0, 1, 2, 3, 6, 7, 8, 9, 4, 5, 10, 11]
    for bi, blk in enumerate(blk_order):
        wt = wada[bi % 2]
        wk = "wada%d" % (bi % 2)
        DMA("sp", wt, wada_v[:, :, blk * 512:(blk + 1) * 512], [], [wk])
        if blk in (4, 5, 10, 11):
            gi = {4: 0, 5: 1, 10: 2, 11: 3}[blk]
            pst = psA[gi % 2]
            pk = "psA%d" % (gi % 2)
            for k in range(8):
                MM(pst[:, :], screp[:, k, :], wt[:, k, :], k == 0, False, ["screp", wk], [pk])
            MM(pst[:, :], ones_f[0:1, :], bada_row[0:1, gi * 512:(gi + 1) * 512], False, True, ["ones_f", "bada_row"], [pk])
            CP("act", gbc[:, gi * 512:(gi + 1) * 512], pst[:, :], [pk], ["gbc"])
        else:
            for jj in range(4):
                j = blk * 4 + jj
                for k in range(8):
                    MM(mods_ps[:, 2 * j:2 * j + 2], wt[:, k, jj * 128:(jj + 1) * 128], scv3[:, k, :], k == 0, k == 7,
                       ["scv", wk], ["psA3"])
    mps3 = mods_ps[:, 0:96].rearrange("p (j v) -> p j v", v=2)
    for lo in (0, 24):
        TT(mods[:, lo:lo + 16, :], mps3[:, lo:lo + 16, :], bada_fm[:, lo:lo + 16].unsqueeze(2).to_broadcast([128, 16, 2]),
           ALU.add, ["psA3", "smallc"], ["mods"])
    for (aa, off, ng) in ((a1, 8, n1g), (a2, 32, n2g)):
        TS(aa[:], mods[:, off:off + 8, :], 1.0, ALU.add, ["mods"], ["a12"])
        TT(aa[:], aa[:], ng.unsqueeze(2).to_broadcast([128, 8, 2]), ALU.mult, ["a12", "smallc"], ["a12"])

    def norm_core(ti, xin, xk, nrows, dst_fn, v, acol, shoff, dkeys, inv_n=1.0 / D):
        s2 = ti % 2
        nk = "xn%d" % s2
        xnn = xn[s2]
        ACTF(junk[0:nrows, :], xin, AF.Square, [xk], ["junk", ("ssq", ti)], accum=ssq[0:nrows, ti:ti + 1])
        ACTF(rstd[0:nrows, ti:ti + 1], ssq[0:nrows, ti:ti + 1], AF.Sqrt, [("ssq", ti), "epsc"], [("rstd", ti)],
             bias=epsc[0:nrows, 0:1], scale=inv_n)
        RCP(rstd[0:nrows, ti:ti + 1], rstd[0:nrows, ti:ti + 1], [("rstd", ti)], [("rstd", ti)])
        TS(xnn[0:nrows, :], xin, rstd[0:nrows, ti:ti + 1], ALU.mult, [xk, ("rstd", ti)], [nk])
        for k in range(8):
            TR(psT[:, k, 0:nrows], xnn[0:nrows, k * 128:(k + 1) * 128], ident_bf[0:nrows, 0:nrows], [nk, "cst"], ["psT"])
        for k in range(8):
            ACTF(dst_fn(k), psT[:, k, 0:nrows], AF.Identity, ["psT", "a12", "mods"], dkeys,
                 bias=mods[:, shoff + k, v:v + 1], scale=acol[:, k, v:v + 1])

    def norm_tile(ti, src_ap, nrows, col0, v):
        s3 = ti % 3
        xk = "xt%d" % s3
        DMA("sp", xt[s3][0:nrows, :], src_ap, [], [xk])
        norm_core(ti, xt[s3][0:nrows, :], xk, nrows, lambda k: hT[:, k, col0:col0 + nrows], v, a1, 0,
                  [("hT", col0 // 128)])

    norm_tile(18, xh_d[:, :], 4, HALO0, 0)
    for t in range(NTILE):
        norm_tile(t, x_d[t * 128:(t + 1) * 128, :], 128, t * 128, 0)
    for t in range(2):
        norm_tile(16 + t, ctx_d[t * 128:(t + 1) * 128, :], 128, NT + t * 128, 1)

    def hk(tb):
        return [("hT", 4 * tb + i) for i in range(4)]

    P.fence()
    STOP = [False]

    class WPool:
        def __init__(self, bufs, name):
            self.bufs, self.name, self.i = bufs, name, 0

        def load(self, src_ap, shape_sel=None):
            i = self.i % len(self.bufs)
            self.i += 1
            buf = self.bufs[i]
            key = "%s%d" % (self.name, i)
            dst = buf if shape_sel is None else shape_sel(buf)
            DMA("pool", dst, src_ap, [], [key])
            return buf, key

    if debug not in ("ssd", "pd") and not os.environ.get("KSKIPF"):
        B = Bump()
        wfft = B([128, 8, 1024])
        f_tm = [B([128, 1024]) for _ in range(2)]
        Z = B([128, 8, 2, NT])
        Y = [B([128, 2, 1024]) for _ in range(2)]
        Pq = [B([128, 4, 1024]) for _ in range(2)]
        for hf in range(2):
            DMA("pool", wfft[:, :, hf * 512:(hf + 1) * 512], win_v[:, :, FFT_START + hf * 512:FFT_START + (hf + 1) * 512],
                [], [("wfft", hf)])
        for t in range(NTILE):
            ft = f_tm[t % 2]
            fk = "f_tm%d" % (t % 2)
            for hf in range(2):
                pst, pk = psA[hf], "psA%d" % hf
                for k in range(8):
                    MM(pst[:, :], hT[:, k, t * 128:(t + 1) * 128], wfft[:, k, hf * 512:(hf + 1) * 512], k == 0, k == 7,
                       [("hT", t), ("wfft", hf)], [pk])
                CP("act", ft[:, hf * 512:(hf + 1) * 512], pst[:, :], [pk], [fk])
            for gp in range(4):
                pst, pk = psA[2 + gp % 2], "psA%d" % (2 + gp % 2)
                for gi in range(2):
                    g = 2 * gp + gi
                    MM(pst[:, gi * 256:(gi + 1) * 256], ft[:, g * 128:(g + 1) * 128], W2, True, True, [fk, "cst"], [pk])
                for gi in range(2):
                    for ab in range(2):
                        zo = Z[:, 2 * gp + gi, ab, :].rearrange("p (q c r) -> p r q c", q=16, c=4, r=32)[:, 2 * t:2 * t + 2, :, :]
                        zi = pst[:, gi * 256 + ab * 128:gi * 256 + (ab + 1) * 128].rearrange("p (r q c) -> p r q c", r=2, q=16, c=4)
                        CP("dve" if gp % 2 else "act", zo, zi, [pk], [("Z", t)])
        zkeys = [("Z", t) for t in range(NTILE)]
        for quad in range(16):
            Yq, yk = Y[quad % 2], "Y%d" % (quad % 2)
            Yv = Yq.rearrange("p a (g k) -> p g a k", g=8)
            for gp in range(4):
                pst, pk = psA[gp % 2], "psA%d" % (gp % 2)
                for gi in range(2):
                    g = 2 * gp + gi
                    for ab in range(2):
                        zsel = Z[:, g, ab, quad * 128:(quad + 1) * 128]
                        MM(pst[:, gi * 256:(gi + 1) * 256], zsel, Wch[:, ab * 256:(ab + 1) * 256], ab == 0, ab == 1,
                           zkeys + ["cst"], [pk])
                CP("dve" if gp % 2 else "act", Yv[:, 2 * gp:2 * gp + 2, :, :],
                   pst[:, :].rearrange("p (g a k) -> p g a k", g=2, a=2), [pk], [yk])
            Pqq, pqk = Pq[quad % 2], "Pq%d" % (quad % 2)
            for c4 in range(4):
                for hf in range(2):
                    pst, pk = psA[2 + hf], "psA%d" % (2 + hf)
                    MM(pst[:, :], CrBlk[:, c4, 0, :], Yq[:, 0, hf * 512:(hf + 1) * 512], True, False, [yk, "cst"], [pk])
                    MM(pst[:, :], CrBlk[:, c4, 1, :], Yq[:, 1, hf * 512:(hf + 1) * 512], False, True, [yk, "cst"], [pk])
                    CP("dve" if hf else "act", Pqq[:, c4, hf * 512:(hf + 1) * 512], pst[:, :], [pk], [pqk])
            DMA("sp", part_d.ap()[:, quad * 4096:(quad + 1) * 4096], Pqq.rearrange("p c k -> p (c k)"), [pqk], ["part_d"])
        P.add("pool", lambda e: e.collective_compute("ReduceScatter", ALU.add, replica_groups=GROUPS,
                                                     ins=[part_d.ap().opt()], outs=[rs_d.ap().opt()]),
              ["part_d"], ["rs_d"], cc=True)
        if debug == "fft":
            dd = dout("rs", [32, 65536], BF16)
            tmpb = Bump(100 * 1024)
            tb_ = tmpb([32, 16384])
            for i in range(4):
                DMA("sp", tb_, rs_d.ap()[:, i * 16384:(i + 1) * 16384], ["rs_d"], ["tb_"])
                DMA("sp", dd[:, i * 16384:(i + 1) * 16384], tb_, ["tb_"], ["dd"])
            STOP[0] = True
        P.fence()

    if not STOP[0] and debug != "pd":
        B = Bump()
        dt_tm = B([128, 18, 64], F32)
        negA = B([128, 18, 64], F32)
        expA = B([128, 18, 64], F32)
        dec_bc = B([128, 18, 64], F32)
        dtw = B([128, 18, 64], F32)
        dta_bf = B([128, 18, 64])
        Dcore = B([128, 64], F32)
        Dm = B([128, 4, 64], F32)
        wq = WPool([B([128, 8, 256]) for _ in range(4)], "wq")
        u_sb = [B([128, UW]) for _ in range(2)]
        acc = B([128, 2048], F32)
        diagw = [B([128, 5, 128]) for _ in range(2)]
        xbcT = B([128, 4, CO])
        xsB = B([128, 18, 384])
        Eb_all = B([128, 16, 256])
        E = [B([128, 256], F32) for _ in range(2)]
        Ebf0 = B([128, 256])
        Fg = [B([128, 4, 256], F32) for _ in range(2)]
        xdtS = [B([128, 256]) for _ in range(2)]
        xdt = [[B([128, 256]) for _ in range(2)] for _ in range(2)]
        xsD = [B([128, 256]) for _ in range(2)]
        drep = [B([128, 8, 128]) for _ in range(2)]
        Lt = [[B([128, 4, 128]) for _ in range(2)] for _ in range(2)]
        GT = [[B([128, 4, 128]) for _ in range(2)] for _ in range(2)]
        T0 = B([128, 256], F32)
        T1 = B([128, 256], F32)
        YG = B([128, 256], F32)
        SZ = [B([128, 256], F32) for _ in range(2)]
        YN = B([128, 256])
        yTg = B([128, 2, NT])
        ss2 = B([128, 16], F32)
        r2 = B([128, 16], F32)
        SCR = B.off
        B2 = Bump(SCR)
        wdt = B2([128, 8, 64])
        nega_bc = B2([128, 64], F32)
        tmpE = B2([128, 64], F32)
        Asb = B2([128, 128], F32)
        DMA("pool", wdt, win_v[:, :, DT_START:DT_START + 64], [], ["wdt"])
        ACTF(nega_bc, alog_bc, AF.Exp, ["smallc"], ["nega"])
        TS(nega_bc, nega_bc, -1.0, ALU.mult, ["nega"], ["nega"])
        psD = psA[6]
        psE = psA[5]
        KDT = float(os.environ.get("KDT", "9"))
        for t in range(18 if KDT >= 2 else 0):
            c0 = t * 128
            for k in range(8):
                MM(psD[:, 128:192], hT[:, k, c0:c0 + 128], wdt[:, k, :], k == 0, k == 7, ["wdt", ("hT", t)], ["psDt"])
            TT(tmpE, psD[:, 128:192], dtb_bc, ALU.add, ["psDt", "smallc"], ["tmpE"])
            ACTF(tmpE, tmpE, AF.Exp, ["tmpE"], ["tmpE"])
            ACTF(dt_tm[:, t, :], tmpE, AF.Ln, ["tmpE", "epsc"], [("dt_tm", t)], bias=epsc[:, 1:2])
            TT(dta_bf[:, t, :], dt_tm[:, t, :], nega_bc, ALU.mult, [("dt_tm", t), "nega"], [("dta_bf", t)])
            if KDT < 3:
                continue
            MM(psD[:, 0:32], triU, dta_bf[:, t, 0:32], True, True, ["cst", ("dta_bf", t)], ["psDa"])
            MM(psD[:, 32:64], triL, dta_bf[:, t, 32:64], True, True, ["cst", ("dta_bf", t)], ["psDa"])
            MM(psD[:, 64:128], ones_bf, dta_bf[:, t, :], True, True, ["cst", ("dta_bf", t)], ["psDb"])
            if t < 16 and not os.environ.get("KNOPSE"):
                MM(psE[:, 0:64], ones_bf, dta_bf[:, t, :], t == 0, t == 15, ["cst", ("dta_bf", t)], ["psE"])
            if KDT < 3.2:
                continue
            CP("dve", Asb, psD[:, 0:128], ["psDa"], ["Asb"])
            TS(negA[:, t, :], Asb[:, 0:64], -1.0, ALU.mult, ["Asb"], [("negA", t)])
            if KDT < 3.4:
                continue
            ACTF(expA[:, t, :], Asb[:, 0:64], AF.Exp, ["Asb"], [("expA", t)])
            ACTF(dec_bc[:, t, :], Asb[:, 64:128], AF.Exp, ["Asb"], [("dec", t)])
            if KDT < 3.6:
                continue
            TT(dtw[:, t, :], Asb[:, 64:128], negA[:, t, :], ALU.add, ["Asb", ("negA", t)], [("dtw", t)])
            ACTF(dtw[:, t, :], dtw[:, t, :], AF.Exp, [("dtw", t)], [("dtw", t)])
            TT(dtw[:, t, :], dtw[:, t, :], dt_tm[:, t, :], ALU.mult, [("dtw", t), ("dt_tm", t)], [("dtw", t)])
        if KDT >= 4:
            ACTF(Dcore, psE[:, 0:64], AF.Exp, ["psE"], ["Dcore"])
            DMA("sp", cparts[2].ap()[:, :], Dcore, ["Dcore"], ["contrib"])
        for ub in (u_sb if KDT >= 5 else []):
            MEMSET(ub[:, 2052:2054], 0.0, ["u_sb0", "u_sb1"])
            MEMSET(ub[:, 2310:2312], 0.0, ["u_sb0", "u_sb1"])

        uctr = [0]
        dctr = [0]

        def ssd_prep(g, p2):
            chunks = [(2 * g, 256 * g), (2 * g + 1, 256 * g + 128), (16 + g, 2048 + 128 * g)]
            if p2:
                chunks.append((24 + g, 3072 + 128 * g))
            CW = NT if p2 else CO
            for ci, (cch, col) in enumerate(chunks):
                wb, wk = wq.load(win_v[:, :, col:col + 128], lambda b: b[:, :, 0:128])
                ui = uctr[0] % 2
                uctr[0] += 1
                ub, uk = u_sb[ui], "u_sb%d" % ui
                for tb in range(4):
                    pst, pk = psA[tb % 2], "psA%d" % (tb % 2)
                    for k in range(8):
                        MM(pst[:, :], wb[:, k, 0:128], hT[:, k, tb * 512:(tb + 1) * 512], k == 0, k == 7, [wk] + hk(tb), [pk])
                    CP("act", ub[:, 2 + tb * 512:2 + (tb + 1) * 512], pst[:, :], [pk], [uk])
                if not p2:
                    pst, pk = psA[0], "psA0"
                    for k in range(8):
                        MM(pst[:, 0:256], wb[:, k, 0:128], hT[:, k, NT:NT + 256], k == 0, k == 7,
                           [wk, ("hT", 16), ("hT", 17)], [pk])
                    CP("act", ub[:, 2054:2310], pst[:, 0:256], [pk], [uk])
                pst, pk = psA[1], "psA1"
                for k in range(8):
                    MM(pst[:, 0:4], wb[:, k, 0:128], hT[:, k, HALO0:HALO0 + 4], k == 0, k == 7, [wk, ("hT", 18)], [pk])
                TS(ub[:, 0:2], pst[:, 0:2], hmask[:, 0:1], ALU.mult, [pk, "smallc"], [uk])
                TS(ub[:, 2050:2052], pst[:, 2:4], hmask[:, 1:2], ALU.mult, [pk, "smallc"], [uk])
                di = dctr[0] % 2
                dctr[0] += 1
                dg, dk = diagw[di], "diagw%d" % di
                for k in range(5):
                    TS(dg[:, k, :], ident_bf, convw[:, cch * 5 + k:cch * 5 + k + 1], ALU.mult, ["cst", "convw"], [dk])
                blocks = [(tb * 512, 512) for tb in range(4)] + ([] if p2 else [(CTXO, 256)])
                for bi, (o0, n) in enumerate(blocks):
                    pst, pk = psA[2 + bi % 2], "psA%d" % (2 + bi % 2)
                    for k in range(5):
                        MM(pst[:, 0:n], dg[:, k, :], ub[:, o0 + k:o0 + k + n], k == 0, k == 4, [dk, uk], [pk])
                    ACTF(xbcT[:, ci, o0:o0 + n], pst[:, 0:n], AF.Silu, [pk, "smallc"], [("xbcT", ci)], bias=convb[:, cch:cch + 1])
            for t in (range(16) if p2 else range(18)):
                c0 = t * 128 if t < 16 else CTXO + (t - 16) * 128
                for ci in range(3):
                    TR(psT[:, ci, :], xbcT[:, ci, c0:c0 + 128], ident_bf, [("xbcT", ci), "cst"], ["psT03"])
                CP("act", xsB[:, t, :].rearrange("p (c k) -> p c k", c=3), psT[:, 0:3, :], ["psT03"], [("xsB", t)])

        EL = os.environ.get("KEL", "dve")

        def ss_a(g, t, d, si):
            c0 = d * 32 + 4 * g
            xs4 = xsB[:, t, 0:256].rearrange("p (j q) -> p j q", j=4)
            psS = psA[5][:, 0:256] if si == 0 else psA[6][:, 0:256]
            TT(xdtS[si].rearrange("p (j q) -> p j q", j=4), xs4, bc4(dtw[:, t, c0:c0 + 4]), ALU.mult,
               [("xsB", t), ("dtw", t)], ["xdtS%d" % si], eng=EL)
            MM(psS, xsB[:, t, 256:384], xdtS[si], True, True, [("xsB", t), "xdtS%d" % si], ["psS%d" % si])

        def ss_b(g, t, d, si, first):
            c0 = d * 32 + 4 * g
            psS = psA[5][:, 0:256] if si == 0 else psA[6][:, 0:256]
            pk = "psS%d" % si
            if first:
                CP("act", E[d], psS, [pk], [("E", d)])
            else:
                E4 = E[d].rearrange("p (j q) -> p j q", j=4)
                TT(E4, E4, bc4(dec_bc[:, t, c0:c0 + 4]), ALU.mult, [("E", d), ("dec", t)], [("E", d)])
                TT(E[d], E[d], psS, ALU.add, [("E", d), pk], [("E", d)])

        def run_states(g, d, tiles, first0, pre=None):
            tiles = list(tiles)
            ss_a(g, tiles[0], d, 0)
            for i, t in enumerate(tiles):
                if i + 1 < len(tiles):
                    ss_a(g, tiles[i + 1], d, (i + 1) % 2)
                if pre is not None:
                    pre(t)
                ss_b(g, t, d, i % 2, first0 and i == 0)

        def ssd_pass1(g):
            ssd_prep(g, False)
            for d, ctx_order, lat_order in ((1, (17, 16), range(15, -1, -1)), (0, (16, 17), range(16))):
                run_states(g, d, ctx_order, True)
                DMA("sp", sctx_d[:, (g * 2 + d) * 256:(g * 2 + d + 1) * 256], E[d], [("E", d)], ["sctx_d"])
                run_states(g, d, lat_order, True)
                DMA("sp", contrib_ap(g, d), E[d], [("E", d)], ["contrib"])

        NG = int(os.environ.get("KNG", "8"))
        if debug == "ssd_dt":
            dd = dout("dt", [128, 5, 18 * 64], F32)
            for i, arr in enumerate((dt_tm, negA, expA, dec_bc, dtw)):
                DMA("sp", dd[:, i, :], arr.rearrange("p t c -> p (t c)"), [(nm, t) for nm in ("dt_tm", "negA", "expA", "dec", "dtw") for t in range(18)], ["dd"])
            P.emit(es)
            es.close()
            P.in_names = in_names
            return nc, P
        for g in range(NG):
            ssd_pass1(g)
        if NG < 8:
            MEMSET(acc[:, 0:2048], 0.0, ["acc"])
            for gz in range(NG, 8):
                for dz in range(2):
                    DMA("sp", contrib_ap(gz, dz), acc[:, 0:256], ["acc"], ["contrib"])
        for ci in range(3):
            def _ag(e, ci=ci):
                return e.collective_compute("AllGather", ALU.bypass, replica_groups=GROUPS,
                                            ins=[cparts[ci].ap().opt()], outs=[gparts[ci].ap().opt()])
            P.add("pool", _ag, ["contrib", "rs_d", "ccchain"], ["gath", "ccchain"], cc=True)
        P.fence()
        DMA("sp", Dm, gparts[2].ap().rearrange("(r n) c -> n r c", n=128), ["gath"], ["Dm"])
        TS(Dm, Dm, -1.0, ALU.add, ["Dm"], ["Dm"])
        TT(Dm, Dm, cmask, ALU.mult, ["Dm", "smallc"], ["Dm"])
        TS(Dm, Dm, 1.0, ALU.add, ["Dm"], ["Dm"])

        def ssd_pass2(g):
            ssd_prep(g, True)
            wz, wzk = wq.load(win_v[:, :, Z_START + 256 * g:Z_START + 256 * (g + 1)])
            for d in range(2):
                c0 = d * 32 + 4 * g
                DMA("sp", Fg[d], gath_ap(g, d), ["gath"], [("Fg", d)])
                DMA("sp", E[d], sctx_d[:, (g * 2 + d) * 256:(g * 2 + d + 1) * 256], [], [("E", d)])
                E4 = E[d].rearrange("p (j q) -> p j q", j=4)
                for r in (range(4) if d == 0 else range(3, -1, -1)):
                    TT(E4, E4, bc4(Dm[:, r, c0:c0 + 4]), ALU.mult, [("E", d), "Dm"], [("E", d)])
                    STT(E[d], Fg[d][:, r, :], cmask[:, r, c0:c0 + 1], E[d], ALU.mult, ALU.add,
                        [("E", d), ("Fg", d), "smallc"], [("E", d)])
            run_states(g, 1, range(15, -1, -1), False,
                       pre=lambda t: CP("act", Eb_all[:, t, :], E[1], [("E", 1)], [("Eb", t)]))
            CP("act", Ebf0, E[0], [("E", 0)], ["Ebf0"])
            psC = psA[0][:, 0:128]
            psZ = psA[6][:, 0:256]
            psY = psA[1][:, 0:256]
            psO = psA[4]
            def h1a(t):
                p = t % 2
                cols = slice(t * 128, (t + 1) * 128)
                MM(psC, xbcT[:, 2, cols], xbcT[:, 3, cols], True, True, [("xbcT", 2), ("xbcT", 3)], ["psC"])
                for d in range(2):
                    CP(EL, drep[p][:, 4 * d:4 * d + 4, :],
                       dta_bf[:, t, d * 32 + 4 * g:d * 32 + 4 * g + 4].unsqueeze(2).to_broadcast([128, 4, 128]),
                       [("dta_bf", t)], [("drep", d, p)])
                for d in range(2):
                    c0 = d * 32 + 4 * g
                    pD = psA[2 + d]
                    for j in range(4):
                        MM(pD[:, 128 * j:128 * (j + 1)], ident_bf, mask4[:, d, 0:128], True, False, ["cst"], [("psD", d)])
                        MM(pD[:, 128 * j:128 * (j + 1)], drep[p][:, 4 * d + j, :], triU if d == 0 else triL, False, True,
                           [("drep", d, p), "cst"], [("psD", d)])
                    for j in range(4):
                        ACTF(Lt[d][p][:, j, :], pD[:, 128 * j:128 * (j + 1)], AF.Exp, [("psD", d), ("negA", t)], [("Lt", d, p)],
                             bias=negA[:, t, c0 + j:c0 + j + 1])
                for k in range(8):
                    MM(psZ, hT[:, k, cols], wz[:, k, :], k == 0, k == 7, [("hT", t), wzk], ["psZ"])
                ACTF(SZ[p], psZ, AF.Silu, ["psZ"], [("SZ", p)])

            def h1b(t):
                p = t % 2
                xs4 = xsB[:, t, 0:256].rearrange("p (j q) -> p j q", j=4)
                for d in range(2):
                    c0 = d * 32 + 4 * g
                    TT(GT[d][p], Lt[d][p], psC.unsqueeze(1).to_broadcast([128, 4, 128]), ALU.mult, [("Lt", d, p), "psC"],
                       [("GT", d, p)])
                    TT(xdt[d][p].rearrange("p (j q) -> p j q", j=4), xs4, bc4(dt_tm[:, t, c0:c0 + 4]), ALU.mult,
                       [("xsB", t), ("dt_tm", t)], [("xdt", d, p)], eng=EL)
                TT(xsD[p].rearrange("p (j q) -> p j q", j=4), xs4, bc4(dskip[:, 4 * g:4 * g + 4]), ALU.mult,
                   [("xsB", t), "smallc"], [("xsD", p)], eng=EL)

            def h2a(t):
                p = t % 2
                cols = slice(t * 128, (t + 1) * 128)
                for j in range(4):
                    js = slice(64 * j, 64 * (j + 1))
                    MM(psY[:, js], ident_bf, xsD[p][:, js], True, False, ["cst", ("xsD", p)], ["psY"])
                    MM(psY[:, js], GT[0][p][:, j, :], xdt[0][p][:, js], False, False, [("GT", 0, p), ("xdt", 0, p)], ["psY"])
                    MM(psY[:, js], GT[1][p][:, j, :], xdt[1][p][:, js], False, True, [("GT", 1, p), ("xdt", 1, p)], ["psY"])
                MM(psO[:, 0:256], xbcT[:, 3, cols], Ebf0, True, True, [("xbcT", 3), "Ebf0"], ["psO0"])
                MM(psO[:, 256:512], xbcT[:, 3, cols], Eb_all[:, t, :], True, True, [("xbcT", 3), ("Eb", t)], ["psO1"])
                ss_a(g, t, 0, 0)
                ss_b(g, t, 0, 0, False)
                CP("act", Ebf0, E[0], [("E", 0)], ["Ebf0"])
                TT(T0.rearrange("p (j q) -> p j q", j=4), psO[:, 0:256].rearrange("p (j q) -> p j q", j=4),
                   bc4(expA[:, t, 4 * g:4 * g + 4]), ALU.mult, ["psO0", ("expA", t)], ["T0"])
                TT(T1.rearrange("p (j q) -> p j q", j=4), psO[:, 256:512].rearrange("p (j q) -> p j q", j=4),
                   bc4(expA[:, t, 32 + 4 * g:32 + 4 * g + 4]), ALU.mult, ["psO1", ("expA", t)], ["T1"])
                TT(T0, T0, T1, ALU.add, ["T0", "T1"], ["T0"])
                TT(T0, T0, psY, ALU.add, ["T0", "psY"], ["T0"])
                TT(YG, T0, SZ[p], ALU.mult, ["T0", ("SZ", p)], ["YG"])
                ACTF(T1, YG, AF.Square, ["YG"], ["T1", ("ss2", t)], accum=ss2[:, t:t + 1])
                ACTF(r2[:, t:t + 1], ss2[:, t:t + 1], AF.Sqrt, [("ss2", t), "epsc"], [("r2", t)], bias=epsc[:, 0:1], scale=1.0 / 256)

            def h2b(t):
                cols = slice(t * 128, (t + 1) * 128)
                RCP(r2[:, t:t + 1], r2[:, t:t + 1], [("r2", t)], [("r2", t)])
                TS(YN, YG, r2[:, t:t + 1], ALU.mult, ["YG", ("r2", t)], ["YN"])
                for i in range(2):
                    TR(psT[:, 4 + i, :], YN[:, i * 128:(i + 1) * 128], ident_bf, ["YN", "cst"], ["psT45"])
                for i in range(2):
                    ACTF(yTg[:, i, cols], psT[:, 4 + i, :], AF.Identity, ["psT45", "smallc"], ["yTg"],
                         scale=ssdg[:, 2 * g + i:2 * g + i + 1])

            h1a(0)
            h1b(0)
            for t in range(16):
                if t + 1 < 16:
                    h1a(t + 1)
                h2a(t)
                if t + 1 < 16:
                    h1b(t + 1)
                h2b(t)
            for i in range(2):
                DMA("sp", yT_d[2 * g + i, :, :], yTg[:, i, :], ["yTg"], ["yT_d"])

        for g in range(NG):
            ssd_pass2(g)
        if debug == "ssd":
            dd = dout("yT", [16, 128, NT], BF16)
            P.fence()
            tb_ = Bump(100 * 1024)([128, NT])
            for i in range(2 * NG):
                DMA("sp", tb_, yT_d[i, :, :], [], ["tb_"])
                DMA("sp", dd[i, :, :], tb_, ["tb_"], ["dd"])
            STOP[0] = True
        P.fence()

    if not STOP[0]:
        B = Bump()
        xt = [B([128, D], F32) for _ in range(2)]
        xn = [B([128, D], BF16) for _ in range(2)]
        junk = B([128, D], BF16)
        fgb = B([128, D], F32)
        xnew = [B([128, D], F32) for _ in range(4)]
        h2T = B([128, 8, 512])
        BASE = B.off
        B1 = Bump(BASE)
        wo = B1([128, 8, 1024])
        w8 = WPool([B1([128, 8, 128]) for _ in range(6)], "w8")
        w16 = WPool([B1([128, 16, 128]) for _ in range(2)], "w16")
        yTb = B1([128, 16, 512])
        fmT = B1([128, 8, 512])
        ftl = [B1([128, 1024]) for _ in range(2)]
        mergedT = B1([128, 8, 512])
        S0 = B1([128, 512], F32)
        S1 = B1([128, 512], F32)
        M0 = B1([128, 512], F32)
        M1 = B1([128, 512], F32)
        TMP = B1([128, 512], F32)
        B2 = Bump(BASE)
        wfi = WPool([B2([128, 8, 128]) for _ in range(4)], "wfi")
        wfo = [B2([128, 22, 512]) for _ in range(2)]
        actT = B2([128, 22, 512])
        SA = [B2([128, 512], F32) for _ in range(2)]
        xo = [B2([128, D], F32) for _ in range(4)]
        TMP2 = B2([128, 512], F32)
        ot = [B2([128, D], F32) for _ in range(2)]

        fg_d = din("fg_bc", [128, D])
        wssd_t = din("w_ssd_t", [8, 128, 16 * 128])
        wfft_t = din("w_fft_t", [8, 128, 8 * 128])
        wgate_t = din("w_gate_t", [16, 128, 8 * 128])
        wo_d = din("w_o", [D, D])
        wfi_t = din("w_ffn_in_t", [44, 128, 8 * 128])
        wfo_d = din("w_ffn_out", [D_FF, D])
        DMA("sp", fgb, fg_d[:, :], [], ["fgb"])
        if debug == "pd":
            yin = din("dbg_yT_in", [16, 128, NT], BF16)
            rin = din("dbg_rs_in", [32, 65536], BF16)
            tb_ = Bump(100 * 1024)([128, 16384])
            for i in range(16):
                DMA("sp", tb_[:, 0:NT], yin[i, :, :], [], ["tb_"])
                DMA("sp", yT_d[i, :, :], tb_[:, 0:NT], ["tb_"], ["yT_d"])
            for i in range(4):
                DMA("sp", tb_[0:32, :], rin[:, i * 16384:(i + 1) * 16384], [], ["tb_"])
                DMA("sp", rs_d.ap()[:, i * 16384:(i + 1) * 16384], tb_[0:32, :], ["tb_"], ["rs_d"])
            P.fence()
        wo_v = wo_d.rearrange("(k p) n -> p k n", p=128)
        wfo_v = wfo_d.rearrange("(j p) n -> p j n", p=128)
        yT_v = yT_d.rearrange("k p t -> p k t")

        for tb in range(int(os.environ.get("KTB", "4"))):
            tcols = slice(tb * 512, (tb + 1) * 512)
            DMA("sp", yTb, yT_v[:, :, tcols], [], ["yTb"])
            for hf in range(2):
                DMA("pool", wo[:, :, hf * 512:(hf + 1) * 512], wo_v[:, :, hf * 512:(hf + 1) * 512], [], [("wo", hf)])
            for tt in range(4):
                t = 4 * tb + tt
                ft, fk = ftl[tt % 2], "ftl%d" % (tt % 2)
                DMA("sp", ft, rs_d.ap()[2 * t:2 * t + 2, :].rearrange("r (c k) -> (r c) k", k=1024), ["rs_d"], [fk])
                for k in range(8):
                    TR(psT[:, k, :], ft[:, k * 128:(k + 1) * 128], ident_bf, [fk, "cst"], ["psT"])
                CP("act", fmT[:, :, tt * 128:(tt + 1) * 128], psT[:, :, :], ["psT"], ["fmT"])
            for fc in range(8):
                wss, wssk = w16.load(wssd_t[fc].rearrange("p (k c) -> p k c", k=16))
                wff, wffk = w8.load(wfft_t[fc].rearrange("p (k c) -> p k c", k=8))
                wg0, wg0k = w8.load(wgate_t[fc].rearrange("p (k c) -> p k c", k=8))
                wg1, wg1k = w8.load(wgate_t[8 + fc].rearrange("p (k c) -> p k c", k=8))
                for k in range(16):
                    MM(psA[0][:, :], wss[:, k, :], yTb[:, k, :], k == 0, k == 15, [wssk, "yTb"], ["psA0"])
                for k in range(8):
                    MM(psA[1][:, :], wff[:, k, :], fmT[:, k, :], k == 0, k == 7, [wffk, "fmT"], ["psA1"])
                for k in range(8):
                    MM(psA[2][:, :], wg0[:, k, :], hT[:, k, tcols], k == 0, k == 7, [wg0k] + hk(tb), ["psA2"])
                for k in range(8):
                    MM(psA[3][:, :], wg1[:, k, :], hT[:, k, tcols], k == 0, k == 7, [wg1k] + hk(tb), ["psA3"])
                ACTF(S0, psA[2][:, :], AF.Sigmoid, ["psA2"], ["S0"])
                ACTF(S1, psA[3][:, :], AF.Sigmoid, ["psA3"], ["S1"])
                TT(M0, S0, psA[0][:, :], ALU.mult, ["S0", "psA0"], ["M0"])
                TT(M1, S1, psA[1][:, :], ALU.mult, ["S1", "psA1"], ["M1"])
                TT(mergedT[:, fc, :], M0, M1, ALU.add, ["M0", "M1"], ["mergedT"])
            for tt in range(4):
                t = 4 * tb + tt
                xi = tt % 2
                xk = "xt%d" % xi
                DMA("sp", xt[xi], x_d[t * 128:(t + 1) * 128, :], [], [xk])
                for hf in range(2):
                    pst, pk = psA[4 + hf], "psA%d" % (4 + hf)
                    hs = slice(hf * 512, (hf + 1) * 512)
                    for k in range(8):
                        MM(pst[:, :], mergedT[:, k, tt * 128:(tt + 1) * 128], wo[:, k, hs], k == 0, k == 7,
                           ["mergedT", ("wo", hf)], [pk])
                    TT(TMP, pst[:, :], gbc[:, hs], ALU.mult, [pk, "gbc"], ["TMP"])
                    TT(xnew[tt][:, hs], TMP, xt[xi][:, hs], ALU.add, ["TMP", xk], [("xnew", tt)])
                norm_core(20 + tt, xnew[tt], ("xnew", tt), 128, lambda k, tt=tt: h2T[:, k, tt * 128:(tt + 1) * 128], 0, a2, 24,
                          ["h2T"])
            P.fence()
            DMA("pool", wfo[0], wfo_v[:, :, 0:512], [], [("wfo", 0)])
            for j in range(22):
                if j == 8:
                    DMA("pool", wfo[1], wfo_v[:, :, 512:1024], [], [("wfo", 1)])
                wa, wak = wfi.load(wfi_t[j].rearrange("p (k c) -> p k c", k=8))
                wb, wbk = wfi.load(wfi_t[22 + j].rearrange("p (k c) -> p k c", k=8))
                pa, pak = psA[2 * (j % 2)], "psA%d" % (2 * (j % 2))
                pb, pbk = psA[2 * (j % 2) + 1], "psA%d" % (2 * (j % 2) + 1)
                for k in range(8):
                    MM(pa[:, :], wa[:, k, :], h2T[:, k, :], k == 0, k == 7, [wak, "h2T"], [pak])
                for k in range(8):
                    MM(pb[:, :], wb[:, k, :], h2T[:, k, :], k == 0, k == 7, [wbk, "h2T"], [pbk])
                sa, sak = SA[j % 2], "SA%d" % (j % 2)
                ACTF(sa, pa[:, :], AF.Silu, [pak], [sak])
                TT(actT[:, j, :], sa, pb[:, :], ALU.mult, [sak, pbk], ["actT"])
            for hf in range(2):
                hs = slice(hf * 512, (hf + 1) * 512)
                for tt in range(4):
                    pst, pk = psA[4 + tt % 2], "psA%d" % (4 + tt % 2)
                    for j in range(22):
                        MM(pst[:, :], actT[:, j, tt * 128:(tt + 1) * 128], wfo[hf][:, j, :], j == 0, j == 21,
                           ["actT", ("wfo", hf)], [pk])
                    TT(TMP2, pst[:, :], gbc[:, 1024 + hf * 512:1024 + (hf + 1) * 512], ALU.mult, [pk, "gbc"], ["TMP2"])
                    TT(xo[tt][:, hs], TMP2, xnew[tt][:, hs], ALU.add, ["TMP2", ("xnew", tt)], [("xo", tt)])
            for tt in range(4):
                t = 4 * tb + tt
                ti = 24 + tt
                oi = tt % 2
                ACTF(junk, xo[tt], AF.Square, [("xo", tt)], ["junk", ("ssq", ti)], accum=ssq[:, ti:ti + 1])
                ACTF(rstd[:, ti:ti + 1], ssq[:, ti:ti + 1], AF.Sqrt, [("ssq", ti), "epsc"], [("rstd", ti)],
                     bias=epsc[:, 0:1], scale=1.0 / D)
                RCP(rstd[:, ti:ti + 1], rstd[:, ti:ti + 1], [("rstd", ti)], [("rstd", ti)])
                STT(ot[oi], xo[tt], rstd[:, ti:ti + 1], fgb, ALU.mult, ALU.mult, [("xo", tt), ("rstd", ti), "fgb"], [("ot", oi)])
                DMA("sp", out_d[t * 128:(t + 1) * 128, :], ot[oi], [("ot", oi)], ["out"])
            P.fence()

    P.emit(es)
    es.close()
    P.in_names = in_names
    return nc, P


def _consts(q):
    c = np.zeros((128, NCST), np.float32)
    i = np.arange(128)
    c[:, 0:128] = np.eye(128)
    c[:, 128:256] = (i[:, None] <= i[None, :])
    c[:, 256:384] = (i[:, None] >= i[None, :])
    c[:, 384:512] = 1.0
    mf = np.where(i[:, None] <= i[None, :], 0.0, -30000.0)
    mb = np.where(i[:, None] >= i[None, :], 0.0, -30000.0)
    c[:, 512:1024] = np.tile(mf, (1, 4))
    c[:, 1024:1536] = np.tile(mb, (1, 4))
    cc = np.arange(64)
    ang = 2 * np.pi * np.outer(cc, cc) / 64.0
    w2c = np.zeros((128, 128))
    w2s = np.zeros((128, 128))
    for r2 in range(2):
        w2c[r2 * 64:(r2 + 1) * 64, r2 * 64:(r2 + 1) * 64] = np.cos(ang) / 8.0
        w2s[r2 * 64:(r2 + 1) * 64, r2 * 64:(r2 + 1) * 64] = np.sin(ang) / 8.0
    c[:, 1536:1664] = w2c
    c[:, 1664:1792] = w2s
    angc = 2 * np.pi * np.outer(i, i) / 128.0
    Cc = np.cos(angc) / math.sqrt(128.0)
    Sc = np.sin(angc) / math.sqrt(128.0)
    c[:, 1792:1920] = Cc
    c[:, 1920:2048] = Sc
    c[:, 2048:2176] = -Sc
    c[:, 2176:2304] = Cc
    rl = np.arange(32)
    angr = 2 * np.pi * np.outer(32 * q + rl, i) / 128.0
    blk = np.zeros((128, 4, 2, 128))
    for c4 in range(4):
        blk[c4 * 32:(c4 + 1) * 32, c4, 0, :] = np.cos(angr) / math.sqrt(128.0)
        blk[c4 * 32:(c4 + 1) * 32, c4, 1, :] = -np.sin(angr) / math.sqrt(128.0)
    c[:, 2304:3328] = blk.reshape(128, 1024)
    return c


def _tiles(w):
    kk, nb = w.shape[0] // 128, w.shape[1] // 128
    return np.ascontiguousarray(w.reshape(kk, 128, nb, 128).transpose(2, 1, 0, 3).reshape(nb, 128, kk * 128))


def _prep_inputs(inp):
    f32 = np.float32
    x = np.asarray(inp["x"], f32)
    ctx = np.asarray(inp["ctx"], f32)
    c = np.asarray(inp["c"], f32)
    c_ctx = np.asarray(inp["c_ctx"], f32)
    b_ada = np.asarray(inp["b_ada"], f32)[0]
    conv_w = np.asarray(inp["conv_w"], f32)[0]
    shared = {
        "w_ada": np.ascontiguousarray(inp["w_ada"][0], f32),
        "b_ada_row": np.ascontiguousarray(b_ada.reshape(1, -1)),
        "w_in": np.ascontiguousarray(inp["w_in"][0], f32),
        "identc": np.eye(128, dtype=f32),
        "convw_fm": np.ascontiguousarray(conv_w.reshape(5, 32, 128).transpose(2, 1, 0).reshape(128, 160)),
        "fg_bc": np.ascontiguousarray(np.broadcast_to(np.asarray(inp["final_g"], f32)[None, :], (128, D))),
        "w_ssd_t": _tiles(np.asarray(inp["w_ssd_out"][0], f32)),
        "w_fft_t": _tiles(np.asarray(inp["w_fft_out"][0], f32)),
        "w_gate_t": _tiles(np.asarray(inp["w_in"][0], f32)[:, GATE_START:GATE_START + 2048]),
        "w_o": np.ascontiguousarray(inp["w_o"][0], f32),
        "w_ffn_in_t": _tiles(np.asarray(inp["w_ffn_in"][0], f32)),
        "w_ffn_out": np.ascontiguousarray(inp["w_ffn_out"][0], f32),
    }
    sm = np.zeros((128, NSMALL), f32)
    sm[:, 0:48] = b_ada.reshape(48, 128).T
    sm[:, 48:56] = np.asarray(inp["norm1_g"], f32)[0].reshape(8, 128).T
    sm[:, 56:64] = np.asarray(inp["norm2_g"], f32)[0].reshape(8, 128).T
    sm[:, 64:96] = np.asarray(inp["conv_b"], f32)[0].reshape(32, 128).T
    sm[:, 96:112] = np.asarray(inp["ssd_norm_g"], f32)[0].reshape(16, 128).T
    sm[:, 112:144] = np.asarray(inp["d_skip"], f32)[0][None, :]
    sm[:, 404:468] = np.asarray(inp["dt_bias"], f32)[0].reshape(1, 64)
    sm[:, 468:532] = np.asarray(inp["a_log"], f32)[0].reshape(1, 64)
    maps = []
    for core in range(8):
        b, q = core // 4, core % 4
        t0 = q * NT
        m = dict(shared)
        m["x"] = np.ascontiguousarray(x[b, t0:t0 + NT])
        m["ctx"] = np.ascontiguousarray(ctx[b])
        xh = np.zeros((4, D), f32)
        for i, tt in enumerate((t0 - 2, t0 - 1, t0 + NT, t0 + NT + 1)):
            if 0 <= tt < 8192:
                xh[i] = x[b, tt]
        m["xhalo"] = xh
        cv = np.zeros((128, 16), f32)
        cv[:, 0:8] = c[b].reshape(8, 128).T
        cv[:, 8:16] = c_ctx.reshape(8, 128).T
        m["cvec"] = cv
        s = sm.copy()
        s[:, 144] = 0.0 if q == 0 else 1.0
        s[:, 145] = 0.0 if q == 3 else 1.0
        cm = np.zeros((4, 64), f32)
        for r in range(4):
            cm[r, 0:32] = 1.0 if r < q else 0.0
            cm[r, 32:64] = 1.0 if r > q else 0.0
        s[:, 148:404] = cm.reshape(1, 256)
        m["smallc"] = s
        m["cst"] = _consts(q)
        maps.append(m)
    return maps


_CACHE = {}


def kernel(**inp):
    if "nc" not in _CACHE:
        _CACHE["nc"] = build()[0]
    nc = _CACHE["nc"]
    maps = _prep_inputs(inp)
    res = run_bass_kernel_spmd(nc, maps, core_ids=list(range(8)))
    out = np.zeros((2, 8192, D), np.float32)
    for core in range(8):
        b, q = core // 4, core % 4
        out[b, q * NT:(q + 1) * NT] = np.asarray(res.results[core]["out"], np.float32)
    return out
```

```python
import math
import os
from contextlib import ExitStack
import numpy as np
import ml_dtypes
import concourse.bass as bass
import concourse.mybir as mybir
from concourse.bass_utils import run_bass_kernel_spmd

F32 = mybir.dt.float32
BF16 = mybir.dt.bfloat16
AF = mybir.ActivationFunctionType
ALU = mybir.AluOpType
AX = mybir.AxisListType

D = 1024
NCST = 3328
NSMALL = 532
NT = 2048
NTILE = 16
CTX = 256
EPS = 1e-6
HW = NT + CTX + 4
HALO0 = NT + CTX
DT_START = 4096
Z_START = 4160
FFT_START = Z_START + 2048
GATE_START = FFT_START + 1024
D_FF = 2816
UW = 2312
CO = 2308
CTXO = 2052
GROUPS = [[0, 1, 2, 3], [4, 5, 6, 7]]


class _Op:
    pass


class Prog:
    def __init__(self, nc, n_dma=32):
        self.nc = nc
        self.ops = []
        self.lastw = {}
        self.rd_eng = {}
        self.rd_dma = {}
        self.n_dma = n_dma
        self.rr = 0
        self.rr_pool = 0
        self.last_on = [None] * n_dma
        self.last_eng = {}
        self.dma_since = []
        self.fence_dep = None
        self.cc_w = {}

    PSKEY = {"psC": "psA0", "psZ": "psA6", "psY": "psA1", "psS0": "psA5", "psS1": "psA6", ("psD", 0): "psA2",
             ("psD", 1): "psA3", "psO0": "psA4", "psO1": "psA4", "psDa": "psA6", "psDb": "psA6", "psDt": "psA6",
             "psE": "psA5", "psT03": "psT", "psT45": "psT", "psT7": "psT"}

    def add(self, eng, fn, reads=(), writes=(), dma=False, cc=False):
        reads = [self.PSKEY.get(k, k) for k in reads]
        writes = [self.PSKEY.get(k, k) for k in writes]
        op = _Op()
        op.eng, op.fn, op.dma, op.cc = eng, fn, dma, cc
        op.idx = len(self.ops)
        deps = set()
        for k in reads:
            w = self.lastw.get(k)
            if w is not None:
                deps.add(w)
            if isinstance(k, str) and k.startswith("ps"):
                for e2, i2 in self.rd_eng.get(k, {}).items():
                    if e2 != eng:
                        deps.add(i2)
        for k in writes:
            w = self.lastw.get(k)
            if w is not None:
                deps.add(w)
            deps.update(self.rd_eng.get(k, {}).values())
            deps.update(self.rd_dma.get(k, ()))
        if dma:
            if eng == "pool":
                s = self.n_dma - 8 + (self.rr_pool % 8)
                self.rr_pool += 1
            else:
                s = self.rr % (self.n_dma - 8)
                self.rr += 1
            op.dsem = s
            if self.last_on[s] is not None:
                deps.add(self.last_on[s])
            self.last_on[s] = op.idx
        if self.fence_dep is not None:
            deps.add(self.fence_dep)
        for k in list(reads) + list(writes):
            if k in self.cc_w:
                deps.add(self.cc_w[k])
        if cc:
            for k in writes:
                self.cc_w[k] = op.idx
        op.deps = deps
        if dma or cc:
            self.dma_since.append(op.idx)
        else:
            self.last_eng[eng] = op.idx
        for k in reads:
            if dma or cc:
                self.rd_dma.setdefault(k, []).append(op.idx)
            else:
                self.rd_eng.setdefault(k, {})[eng] = op.idx
        for k in writes:
            self.lastw[k] = op.idx
            self.rd_eng[k] = {}
            self.rd_dma[k] = []
        self.ops.append(op)
        return op

    def fence(self):
        deps = set(self.last_eng.values())
        deps.update(i for i in self.dma_since if not self.ops[i].cc)
        op = self.add("sp", lambda e: e.nop())
        op.deps |= deps
        self.fence_dep = op.idx
        self.dma_since = []
        self.lastw = {}
        self.rd_eng = {}
        self.rd_dma = {}
        self.last_on = [None] * self.n_dma

    def emit(self, es):
        nc = self.nc
        ops = self.ops
        engs = ("pe", "act", "dve", "pool", "sp")
        needs = [False] * len(ops)
        for op in ops:
            for d in op.deps:
                dd = ops[d]
                if dd.dma or dd.cc:
                    continue
                if dd.eng == "pe" and op.eng == "pe" and not (op.dma or op.cc):
                    continue
                needs[d] = True
        esem = {e: es.enter_context(nc.semaphore("s_" + e)) for e in engs}
        dsem = [es.enter_context(nc.semaphore("d%d" % i)) for i in range(self.n_dma)]
        ncc = sum(1 for op in ops if op.cc)
        ccsems = [es.enter_context(nc.semaphore("ccs%d" % i)) for i in range(ncc)]
        cnt = {e: 0 for e in engs}
        dcnt = [0] * self.n_dma
        cccnt = 0
        for op in ops:
            if op.cc:
                op.ev = (ccsems[cccnt], 1)
                cccnt += 1
            elif op.dma:
                dcnt[op.dsem] += 16
                op.ev = (dsem[op.dsem], dcnt[op.dsem])
            elif needs[op.idx]:
                cnt[op.eng] += 1
                op.ev = (esem[op.eng], cnt[op.eng])
            else:
                op.ev = None
        per = {e: [] for e in engs}
        for op in ops:
            per[op.eng].append(op)
        self.stats = {e: len(per[e]) for e in engs}

        def run(ename, eng):
            seen = {}
            for op in per[ename]:
                waits = {}
                for d in op.deps:
                    dd = ops[d]
                    if dd.eng == "pe" and ename == "pe" and not (dd.dma or dd.cc) and not (op.dma or op.cc):
                        continue
                    sem, val = dd.ev
                    key = id(sem)
                    if seen.get(key, 0) >= val:
                        continue
                    if key not in waits or waits[key][1] < val:
                        waits[key] = (sem, val)
                for key, (sem, val) in waits.items():
                    eng.wait_ge(sem, val)
                    seen[key] = val
                ins = op.fn(eng)
                if op.cc:
                    ins.then_inc(op.ev[0])
                elif op.dma:
                    ins.then_inc(op.ev[0], 16)
                elif op.ev is not None:
                    ins.then_inc(op.ev[0], 1)
            if ename == "sp":
                for i in range(self.n_dma):
                    if dcnt[i]:
                        eng.wait_ge(dsem[i], dcnt[i])
                for cs in ccsems:
                    eng.wait_ge(cs, 1)

        with nc.Block() as block:
            @block.tensor
            def _(e):
                run("pe", e)

            @block.scalar
            def _(e):
                run("act", e)

            @block.vector
            def _(e):
                run("dve", e)

            @block.gpsimd
            def _(e):
                run("pool", e)

            @block.sync
            def _(e):
                run("sp", e)


ARENA = 150 * 1024


def build(debug=None):
    nc = bass.Bass("TRN2", target_bir_lowering=False)
    es = ExitStack()
    P = Prog(nc)

    in_names = []

    def din(name, shape, dt=F32):
        in_names.append(name)
        return nc.dram_tensor(name, list(shape), dt, kind="ExternalInput").ap()

    x_d = din("x", [NT, D])
    ctx_d = din("ctx", [CTX, D])
    xh_d = din("xhalo", [4, D])
    cvec_d = din("cvec", [128, 16])
    wada_d = din("w_ada", [D, 6 * D])
    bada_row_d = din("b_ada_row", [1, 6 * D])
    win_d = din("w_in", [D, 9280])
    identc_d = din("identc", [128, 128])
    cst_d = din("cst", [128, NCST])
    smallc_d = din("smallc", [128, NSMALL])
    convw_d = din("convw_fm", [128, 160])
    out_d = nc.dram_tensor("out", [NT, D], F32, kind="ExternalOutput").ap()
    cparts = [nc.dram_tensor("contribA", [128, 2048], F32), nc.dram_tensor("contribB", [128, 2048], F32),
              nc.dram_tensor("contribC", [128, 64], F32)]
    gparts = [nc.dram_tensor("gathA", [512, 2048], F32), nc.dram_tensor("gathB", [512, 2048], F32),
              nc.dram_tensor("gathC", [512, 64], F32)]

    def contrib_ap(g, d):
        return cparts[g // 4].ap()[:, ((g % 4) * 2 + d) * 256:((g % 4) * 2 + d + 1) * 256]

    def gath_ap(g, d):
        return gparts[g // 4].ap().rearrange("(r n) c -> n r c", n=128)[:, :, ((g % 4) * 2 + d) * 256:((g % 4) * 2 + d + 1) * 256]
    sctx_d = nc.dram_tensor("sctx", [128, 4096], F32).ap()
    part_d = nc.dram_tensor("partd", [128, 65536], BF16)
    rs_d = nc.dram_tensor("rsd", [32, 65536], BF16)
    yT_d = nc.dram_tensor("yTd", [16, 128, NT], BF16).ap()
    dbg = {}

    def dout(name, shape, dt):
        dbg[name] = nc.dram_tensor("dbg_" + name, list(shape), dt, kind="ExternalOutput").ap()
        return dbg[name]

    def sb(name, shape, dt=F32):
        return es.enter_context(nc.sbuf_tensor(name, list(shape), dt))

    def ps(name, shape, dt=F32):
        return es.enter_context(nc.psum_tensor(name, list(shape), dt))

    arena = sb("arena", [128, ARENA // 2], BF16)

    class Bump:
        def __init__(self, base=0):
            self.off = base

        def __call__(self, shape, dt=BF16):
            n = int(np.prod(shape[1:]))
            esz = 2 if dt == BF16 else 4
            nb = (n * esz + 31) // 32 * 32
            assert self.off + nb <= ARENA, ("arena overflow", self.off + nb)
            v = arena[0:shape[0], self.off // 2: self.off // 2 + n * esz // 2]
            self.off += nb
            if dt != BF16:
                v = v.bitcast(dt)
            if len(shape) == 3:
                v = v.rearrange("p (a b) -> p a b", a=shape[1])
            elif len(shape) == 4:
                v = v.rearrange("p (a b c) -> p a b c", a=shape[1], b=shape[2])
            return v

    def MM(out, lhsT, rhs, start, stop, reads, writes):
        P.add("pe", lambda e: e.matmul(out, lhsT=lhsT, rhs=rhs, start=start, stop=stop), reads, writes)

    def TR(out, in_, ident, reads, writes):
        P.add("pe", lambda e: e.transpose(out=out, in_=in_, identity=ident), reads, writes)

    def ACTF(out, in_, func, reads, writes, bias=None, scale=None, accum=None):
        kw = {}
        if bias is not None:
            kw["bias"] = bias
        if scale is not None:
            kw["scale"] = scale
        if accum is not None:
            kw["accum_out"] = accum
        P.add("act", lambda e: e.activation(out=out, in_=in_, func=func, **kw), reads, writes)

    def CP(eng, out, in_, reads, writes):
        if eng == "act":
            P.add("act", lambda e: e.copy(out=out, in_=in_), reads, writes)
        else:
            P.add(eng, lambda e: e.tensor_copy(out=out, in_=in_), reads, writes)

    def TT(out, in0, in1, op, reads, writes, eng="dve"):
        P.add(eng, lambda e: e.tensor_tensor(out=out, in0=in0, in1=in1, op=op), reads, writes)

    def TS(out, in0, s1, op0, reads, writes, s2=None, op1=None, eng="dve"):
        if op1 is None:
            P.add(eng, lambda e: e.tensor_scalar(out=out, in0=in0, scalar1=s1, scalar2=None, op0=op0), reads, writes)
        else:
            P.add(eng, lambda e: e.tensor_scalar(out=out, in0=in0, scalar1=s1, scalar2=s2, op0=op0, op1=op1), reads, writes)

    def STT(out, in0, scalar, in1, op0, op1, reads, writes, eng="dve"):
        P.add(eng, lambda e: e.scalar_tensor_tensor(out=out, in0=in0, scalar=scalar, in1=in1, op0=op0, op1=op1),
              reads, writes)

    def RCP(out, in_, reads, writes):
        P.add("dve", lambda e: e.reciprocal(out=out, in_=in_), reads, writes)

    def DMA(q, out, in_, reads, writes):
        P.add(q, lambda e: e.dma_start(out=out, in_=in_), reads, writes, dma=True)

    def MEMSET(ap, val, writes):
        P.add("dve", lambda e: e.memset(ap, val), (), writes)

    def bc4(ap4, n=64):
        return ap4.unsqueeze(2).to_broadcast([128, 4, n])

    ident_f = sb("ident_f", [128, 128], F32)
    ones_f = sb("ones_f", [128, 128], F32)
    epsc = sb("epsc", [128, 2])
    cst = sb("cst_sb", [128, NCST], BF16)
    smallc = sb("smallc_sb", [128, NSMALL])
    convw = sb("convw_sb", [128, 160])
    mods = sb("mods", [128, 48, 2])
    a1 = sb("a1", [128, 8, 2])
    a2 = sb("a2", [128, 8, 2])
    gbc = sb("gbc", [128, 2048])
    hT = sb("hT", [128, 8, HW], BF16)
    ssq = sb("ssq", [128, 32])
    rstd = sb("rstd", [128, 32])
    ident_bf = cst[:, 0:128]
    triU = cst[:, 128:256]
    triL = cst[:, 256:384]
    ones_bf = cst[:, 384:512]
    mask4 = cst[:, 512:1536].rearrange("p (d n) -> p d n", d=2)
    W2 = cst[:, 1536:1792]
    Wch = cst[:, 1792:2304]
    CrBlk = cst[:, 2304:3328].rearrange("p (c a k) -> p c a k", c=4, a=2)
    bada_fm = smallc[:, 0:48]
    n1g = smallc[:, 48:56]
    n2g = smallc[:, 56:64]
    convb = smallc[:, 64:96]
    ssdg = smallc[:, 96:112]
    dskip = smallc[:, 112:144]
    hmask = smallc[:, 144:146]
    dtb = smallc[:, 146:147]
    alog = smallc[:, 147:148]
    cmask = smallc[:, 148:404].rearrange("p (r c) -> p r c", r=4)
    dtb_bc = smallc[:, 404:468]
    alog_bc = smallc[:, 468:532]

    psA = [ps("psA%d" % i, [128, 512]) for i in range(7)]
    psT = ps("psT", [128, 8, 128], BF16)

    DMA("sp", ident_f[:], identc_d[:, :], [], ["ident_f"])
    DMA("pool", cst[:], cst_d[:, :], [], ["cst"])
    DMA("sp", smallc[:], smallc_d[:, :], [], ["smallc"])
    DMA("sp", convw[:], convw_d[:, :], [], ["convw"])
    MEMSET(ones_f[:], 1.0, ["ones_f"])
    MEMSET(epsc[:, 0:1], EPS, ["epsc"])
    MEMSET(epsc[:, 1:2], 1.0, ["epsc"])

    win_v = win_d.rearrange("(k p) n -> p k n", p=128)

    B = Bump()
    wada = [B([128, 8, 512], F32) for _ in range(2)]
    cv = B([128, 16], F32)
    scv = B([128, 16], F32)
    screp = B([128, 8, 128], F32)
    bada_row = B([1, 2048], F32)
    xt = [B([128, D], F32) for _ in range(3)]
    xn = [B([128, D], BF16) for _ in range(2)]
    junk = B([128, D], BF16)

    DMA("sp", cv, cvec_d[:, :], [], ["cv"])
    DMA("sp", bada_row[:, 0:1024], bada_row_d[:, 2048:3072], [], ["bada_row"])
    DMA("sp", bada_row[:, 1024:2048], bada_row_d[:, 5120:6144], [], ["bada_row"])
    ACTF(scv, cv, AF.Silu, ["cv"], ["scv"])
    CP("dve", screp, scv[:, 0:8].unsqueeze(2).to_broadcast([128, 8, 128]), ["scv"], ["screp"])
    wada_v = wada_d.rearrange("(k p) n -> p k n", p=128)
    scv3 = scv.rearrange("p (v k) -> p k v", v=2)
    mods_ps = psA[3]
    blk_order = [0, 1, 2, 3, 6, 7, 8, 9, 4, 5, 10, 11]
    for bi, blk in enumerate(blk_order):
        wt = wada[bi % 2]
        wk = "wada%d" % (bi % 2)
        DMA("sp", wt, wada_v[:, :, blk * 512:(blk + 1) * 512], [], [wk])
        if blk in (4, 5, 10, 11):
            gi = {4: 0, 5: 1, 10: 2, 11: 3}[blk]
            pst = psA[gi % 2]
            pk = "psA%d" % (gi % 2)
            for k in range(8):
                MM(pst[:, :], screp[:, k, :], wt[:, k, :], k == 0, False, ["screp", wk], [pk])
            MM(pst[:, :], ones_f[0:1, :], bada_row[0:1, gi * 512:(gi + 1) * 512], False, True, ["ones_f", "bada_row"], [pk])
            CP("act", gbc[:, gi * 512:(gi + 1) * 512], pst[:, :], [pk], ["gbc"])
        else:
            for jj in range(4):
                j = blk * 4 + jj
                for k in range(8):
                    MM(mods_ps[:, 2 * j:2 * j + 2], wt[:, k, jj * 128:(jj + 1) * 128], scv3[:, k, :], k == 0, k == 7,
                       ["scv", wk], ["psA3"])
    mps3 = mods_ps[:, 0:96].rearrange("p (j v) -> p j v", v=2)
    for lo in (0, 24):
        TT(mods[:, lo:lo + 16, :], mps3[:, lo:lo + 16, :], bada_fm[:, lo:lo + 16].unsqueeze(2).to_broadcast([128, 16, 2]),
           ALU.add, ["psA3", "smallc"], ["mods"])
    for (aa, off, ng) in ((a1, 8, n1g), (a2, 32, n2g)):
        TS(aa[:], mods[:, off:off + 8, :], 1.0, ALU.add, ["mods"], ["a12"])
        TT(aa[:], aa[:], ng.unsqueeze(2).to_broadcast([128, 8, 2]), ALU.mult, ["a12", "smallc"], ["a12"])

    def norm_core(ti, xin, xk, nrows, dst_fn, v, acol, shoff, dkeys, inv_n=1.0 / D):
        s2 = ti % 2
        nk = "xn%d" % s2
        xnn = xn[s2]
        ACTF(junk[0:nrows, :], xin, AF.Square, [xk], ["junk", ("ssq", ti)], accum=ssq[0:nrows, ti:ti + 1])
        ACTF(rstd[0:nrows, ti:ti + 1], ssq[0:nrows, ti:ti + 1], AF.Sqrt, [("ssq", ti), "epsc"], [("rstd", ti)],
             bias=epsc[0:nrows, 0:1], scale=inv_n)
        RCP(rstd[0:nrows, ti:ti + 1], rstd[0:nrows, ti:ti + 1], [("rstd", ti)], [("rstd", ti)])
        TS(xnn[0:nrows, :], xin, rstd[0:nrows, ti:ti + 1], ALU.mult, [xk, ("rstd", ti)], [nk])
        for k in range(8):
            TR(psT[:, k, 0:nrows], xnn[0:nrows, k * 128:(k + 1) * 128], ident_bf[0:nrows, 0:nrows], [nk, "cst"], ["psT"])
        for k in range(8):
            ACTF(dst_fn(k), psT[:, k, 0:nrows], AF.Identity, ["psT", "a12", "mods"], dkeys,
                 bias=mods[:, shoff + k, v:v + 1], scale=acol[:, k, v:v + 1])

    def norm_tile(ti, src_ap, nrows, col0, v):
        s3 = ti % 3
        xk = "xt%d" % s3
        DMA("sp", xt[s3][0:nrows, :], src_ap, [], [xk])
        norm_core(ti, xt[s3][0:nrows, :], xk, nrows, lambda k: hT[:, k, col0:col0 + nrows], v, a1, 0,
                  [("hT", col0 // 128)])

    norm_tile(18, xh_d[:, :], 4, HALO0, 0)
    for t in range(NTILE):
        norm_tile(t, x_d[t * 128:(t + 1) * 128, :], 128, t * 128, 0)
    for t in range(2):
        norm_tile(16 + t, ctx_d[t * 128:(t + 1) * 128, :], 128, NT + t * 128, 1)

    def hk(tb):
        return [("hT", 4 * tb + i) for i in range(4)]

    P.fence()
    STOP = [False]

    class WPool:
        def __init__(self, bufs, name):
            self.bufs, self.name, self.i = bufs, name, 0

        def load(self, src_ap, shape_sel=None):
            i = self.i % len(self.bufs)
            self.i += 1
            buf = self.bufs[i]
            key = "%s%d" % (self.name, i)
            dst = buf if shape_sel is None else shape_sel(buf)
            DMA("pool", dst, src_ap, [], [key])
            return buf, key

    if debug not in ("ssd", "pd") and not os.environ.get("KSKIPF"):
        B = Bump()
        wfft = B([128, 8, 1024])
        f_tm = [B([128, 1024]) for _ in range(2)]
        Z = B([128, 8, 2, NT])
        Y = [B([128, 2, 1024]) for _ in range(2)]
        Pq = [B([128, 4, 1024]) for _ in range(2)]
        for hf in range(2):
            DMA("pool", wfft[:, :, hf * 512:(hf + 1) * 512], win_v[:, :, FFT_START + hf * 512:FFT_START + (hf + 1) * 512],
                [], [("wfft", hf)])
        for t in range(NTILE):
            ft = f_tm[t % 2]
            fk = "f_tm%d" % (t % 2)
            for hf in range(2):
                pst, pk = psA[hf], "psA%d" % hf
                for k in range(8):
                    MM(pst[:, :], hT[:, k, t * 128:(t + 1) * 128], wfft[:, k, hf * 512:(hf + 1) * 512], k == 0, k == 7,
                       [("hT", t), ("wfft", hf)], [pk])
                CP("act", ft[:, hf * 512:(hf + 1) * 512], pst[:, :], [pk], [fk])
            for gp in range(4):
                pst, pk = psA[2 + gp % 2], "psA%d" % (2 + gp % 2)
                for gi in range(2):
                    g = 2 * gp + gi
                    MM(pst[:, gi * 256:(gi + 1) * 256], ft[:, g * 128:(g + 1) * 128], W2, True, True, [fk, "cst"], [pk])
                for gi in range(2):
                    for ab in range(2):
                        zo = Z[:, 2 * gp + gi, ab, :].rearrange("p (q c r) -> p r q c", q=16, c=4, r=32)[:, 2 * t:2 * t + 2, :, :]
                        zi = pst[:, gi * 256 + ab * 128:gi * 256 + (ab + 1) * 128].rearrange("p (r q c) -> p r q c", r=2, q=16, c=4)
                        CP("dve" if gp % 2 else "act", zo, zi, [pk], [("Z", t)])
        zkeys = [("Z", t) for t in range(NTILE)]
        for quad in range(16):
            Yq, yk = Y[quad % 2], "Y%d" % (quad % 2)
            Yv = Yq.rearrange("p a (g k) -> p g a k", g=8)
            for gp in range(4):
                pst, pk = psA[gp % 2], "psA%d" % (gp % 2)
                for gi in range(2):
                    g = 2 * gp + gi
                    for ab in range(2):
                        zsel = Z[:, g, ab, quad * 128:(quad + 1) * 128]
                        MM(pst[:, gi * 256:(gi + 1) * 256], zsel, Wch[:, ab * 256:(ab + 1) * 256], ab == 0, ab == 1,
                           zkeys + ["cst"], [pk])
                CP("dve" if gp % 2 else "act", Yv[:, 2 * gp:2 * gp + 2, :, :],
                   pst[:, :].rearrange("p (g a k) -> p g a k", g=2, a=2), [pk], [yk])
            Pqq, pqk = Pq[quad % 2], "Pq%d" % (quad % 2)
            for c4 in range(4):
                for hf in range(2):
                    pst, pk = psA[2 + hf], "psA%d" % (2 + hf)
                    MM(pst[:, :], CrBlk[:, c4, 0, :], Yq[:, 0, hf * 512:(hf + 1) * 512], True, False, [yk, "cst"], [pk])
                    MM(pst[:, :], CrBlk[:, c4, 1, :], Yq[:, 1, hf * 512:(hf + 1) * 512], False, True, [yk, "cst"], [pk])
                    CP("dve" if hf else "act", Pqq[:, c4, hf * 512:(hf + 1) * 512], pst[:, :], [pk], [pqk])
            DMA("sp", part_d.ap()[:, quad * 4096:(quad + 1) * 4096], Pqq.rearrange("p c k -> p (c k)"), [pqk], ["part_d"])
        P.add("pool", lambda e: e.collective_compute("ReduceScatter", ALU.add, replica_groups=GROUPS,
                                                     ins=[part_d.ap().opt()], outs=[rs_d.ap().opt()]),
              ["part_d"], ["rs_d"], cc=True)
        if debug == "fft":
            dd = dout("rs", [32, 65536], BF16)
            tmpb = Bump(100 * 1024)
            tb_ = tmpb([32, 16384])
            for i in range(4):
                DMA("sp", tb_, rs_d.ap()[:, i * 16384:(i + 1) * 16384], ["rs_d"], ["tb_"])
                DMA("sp", dd[:, i * 16384:(i + 1) * 16384], tb_, ["tb_"], ["dd"])
            STOP[0] = True
        P.fence()

    if not STOP[0] and debug != "pd":
        B = Bump()
        dt_tm = B([128, 18, 64], F32)
        negA = B([128, 18, 64], F32)
        expA = B([128, 18, 64], F32)
        dec_bc = B([128, 18, 64], F32)
        dtw = B([128, 18, 64], F32)
        dta_bf = B([128, 18, 64])
        Dcore = B([128, 64], F32)
        Dm = B([128, 4, 64], F32)
        wq = WPool([B([128, 8, 256]) for _ in range(4)], "wq")
        u_sb = [B([128, UW]) for _ in range(2)]
        acc = B([128, 2048], F32)
        diagw = [B([128, 5, 128]) for _ in range(2)]
        xbcT = B([128, 4, CO])
        xsB = B([128, 18, 384])
        Eb_all = B([128, 16, 256])
        E = [B([128, 256], F32) for _ in range(2)]
        Ebf0 = B([128, 256])
        Fg = [B([128, 4, 256], F32) for _ in range(2)]
        xdtS = [B([128, 256]) for _ in range(2)]
        xdt = [[B([128, 256]) for _ in range(2)] for _ in range(2)]
        xsD = [B([128, 256]) for _ in range(2)]
        drep = [B([128, 8, 128]) for _ in range(2)]
        Lt = [[B([128, 4, 128]) for _ in range(2)] for _ in range(2)]
        GT = [[B([128, 4, 128]) for _ in range(2)] for _ in range(2)]
        T0 = B([128, 256], F32)
        T1 = B([128, 256], F32)
        YG = B([128, 256], F32)
        SZ = [B([128, 256], F32) for _ in range(2)]
        YN = B([128, 256])
        yTg = B([128, 2, NT])
        ss2 = B([128, 16], F32)
        r2 = B([128, 16], F32)
        SCR = B.off
        B2 = Bump(SCR)
        wdt = B2([128, 8, 64])
        nega_bc = B2([128, 64], F32)
        tmpE = B2([128, 64], F32)
        Asb = B2([128, 128], F32)
        DMA("pool", wdt, win_v[:, :, DT_START:DT_START + 64], [], ["wdt"])
        ACTF(nega_bc, alog_bc, AF.Exp, ["smallc"], ["nega"])
        TS(nega_bc, nega_bc, -1.0, ALU.mult, ["nega"], ["nega"])
        psD = psA[6]
        psE = psA[5]
        KDT = float(os.environ.get("KDT", "9"))
        for t in range(18 if KDT >= 2 else 0):
            c0 = t * 128
            for k in range(8):
                MM(psD[:, 128:192], hT[:, k, c0:c0 + 128], wdt[:, k, :], k == 0, k == 7, ["wdt", ("hT", t)], ["psDt"])
            TT(tmpE, psD[:, 128:192], dtb_bc, ALU.add, ["psDt", "smallc"], ["tmpE"])
            ACTF(tmpE, tmpE, AF.Exp, ["tmpE"], ["tmpE"])
            ACTF(dt_tm[:, t, :], tmpE, AF.Ln, ["tmpE", "epsc"], [("dt_tm", t)], bias=epsc[:, 1:2])
            TT(dta_bf[:, t, :], dt_tm[:, t, :], nega_bc, ALU.mult, [("dt_tm", t), "nega"], [("dta_bf", t)])
            if KDT < 3:
                continue
            MM(psD[:, 0:32], triU, dta_bf[:, t, 0:32], True, True, ["cst", ("dta_bf", t)], ["psDa"])
            MM(psD[:, 32:64], triL, dta_bf[:, t, 32:64], True, True, ["cst", ("dta_bf", t)], ["psDa"])
            MM(psD[:, 64:128], ones_bf, dta_bf[:, t, :], True, True, ["cst", ("dta_bf", t)], ["psDb"])
            if t < 16 and not os.environ.get("KNOPSE"):
                MM(psE[:, 0:64], ones_bf, dta_bf[:, t, :], t == 0, t == 15, ["cst", ("dta_bf", t)], ["psE"])
            if KDT < 3.2:
                continue
            CP("dve", Asb, psD[:, 0:128], ["psDa"], ["Asb"])
            TS(negA[:, t, :], Asb[:, 0:64], -1.0, ALU.mult, ["Asb"], [("negA", t)])
            if KDT < 3.4:
                continue
            ACTF(expA[:, t, :], Asb[:, 0:64], AF.Exp, ["Asb"], [("expA", t)])
            ACTF(dec_bc[:, t, :], Asb[:, 64:128], AF.Exp, ["Asb"], [("dec", t)])
            if KDT < 3.6:
                continue
            TT(dtw[:, t, :], Asb[:, 64:128], negA[:, t, :], ALU.add, ["Asb", ("negA", t)], [("dtw", t)])
            ACTF(dtw[:, t, :], dtw[:, t, :], AF.Exp, [("dtw", t)], [("dtw", t)])
            TT(dtw[:, t, :], dtw[:, t, :], dt_tm[:, t, :], ALU.mult, [("dtw", t), ("dt_tm", t)], [("dtw", t)])
        if KDT >= 4:
            ACTF(Dcore, psE[:, 0:64], AF.Exp, ["psE"], ["Dcore"])
            DMA("sp", cparts[2].ap()[:, :], Dcore, ["Dcore"], ["contrib"])
        for ub in (u_sb if KDT >= 5 else []):
            MEMSET(ub[:, 2052:2054], 0.0, ["u_sb0", "u_sb1"])
            MEMSET(ub[:, 2310:2312], 0.0, ["u_sb0", "u_sb1"])

        uctr = [0]
        dctr = [0]

        def ssd_prep(g, p2):
            chunks = [(2 * g, 256 * g), (2 * g + 1, 256 * g + 128), (16 + g, 2048 + 128 * g)]
            if p2:
                chunks.append((24 + g, 3072 + 128 * g))
            CW = NT if p2 else CO
            for ci, (cch, col) in enumerate(chunks):
                wb, wk = wq.load(win_v[:, :, col:col + 128], lambda b: b[:, :, 0:128])
                ui = uctr[0] % 2
                uctr[0] += 1
                ub, uk = u_sb[ui], "u_sb%d" % ui
                for tb in range(4):
                    pst, pk = psA[tb % 2], "psA%d" % (tb % 2)
                    for k in range(8):
                        MM(pst[:, :], wb[:, k, 0:128], hT[:, k, tb * 512:(tb + 1) * 512], k == 0, k == 7, [wk] + hk(tb), [pk])
                    CP("act", ub[:, 2 + tb * 512:2 + (tb + 1) * 512], pst[:, :], [pk], [uk])
                if not p2:
                    pst, pk = psA[0], "psA0"
                    for k in range(8):
                        MM(pst[:, 0:256], wb[:, k, 0:128], hT[:, k, NT:NT + 256], k == 0, k == 7,
                           [wk, ("hT", 16), ("hT", 17)], [pk])
                    CP("act", ub[:, 2054:2310], pst[:, 0:256], [pk], [uk])
                pst, pk = psA[1], "psA1"
                for k in range(8):
                    MM(pst[:, 0:4], wb[:, k, 0:128], hT[:, k, HALO0:HALO0 + 4], k == 0, k == 7, [wk, ("hT", 18)], [pk])
                TS(ub[:, 0:2], pst[:, 0:2], hmask[:, 0:1], ALU.mult, [pk, "smallc"], [uk])
                TS(ub[:, 2050:2052], pst[:, 2:4], hmask[:, 1:2], ALU.mult, [pk, "smallc"], [uk])
                di = dctr[0] % 2
                dctr[0] += 1
                dg, dk = diagw[di], "diagw%d" % di
                for k in range(5):
                    TS(dg[:, k, :], ident_bf, convw[:, cch * 5 + k:cch * 5 + k + 1], ALU.mult, ["cst", "convw"], [dk])
                blocks = [(tb * 512, 512) for tb in range(4)] + ([] if p2 else [(CTXO, 256)])
                for bi, (o0, n) in enumerate(blocks):
                    pst, pk = psA[2 + bi % 2], "psA%d" % (2 + bi % 2)
                    for k in range(5):
                        MM(pst[:, 0:n], dg[:, k, :], ub[:, o0 + k:o0 + k + n], k == 0, k == 4, [dk, uk], [pk])
                    ACTF(xbcT[:, ci, o0:o0 + n], pst[:, 0:n], AF.Silu, [pk, "smallc"], [("xbcT", ci)], bias=convb[:, cch:cch + 1])
            for t in (range(16) if p2 else range(18)):
                c0 = t * 128 if t < 16 else CTXO + (t - 16) * 128
                for ci in range(3):
                    TR(psT[:, ci, :], xbcT[:, ci, c0:c0 + 128], ident_bf, [("xbcT", ci), "cst"], ["psT03"])
                CP("act", xsB[:, t, :].rearrange("p (c k) -> p c k", c=3), psT[:, 0:3, :], ["psT03"], [("xsB", t)])

        EL = os.environ.get("KEL", "dve")

        def ss_a(g, t, d, si):
            c0 = d * 32 + 4 * g
            xs4 = xsB[:, t, 0:256].rearrange("p (j q) -> p j q", j=4)
            psS = psA[5][:, 0:256] if si == 0 else psA[6][:, 0:256]
            TT(xdtS[si].rearrange("p (j q) -> p j q", j=4), xs4, bc4(dtw[:, t, c0:c0 + 4]), ALU.mult,
               [("xsB", t), ("dtw", t)], ["xdtS%d" % si], eng=EL)
            MM(psS, xsB[:, t, 256:384], xdtS[si], True, True, [("xsB", t), "xdtS%d" % si], ["psS%d" % si])

        def ss_b(g, t, d, si, first):
            c0 = d * 32 + 4 * g
            psS = psA[5][:, 0:256] if si == 0 else psA[6][:, 0:256]
            pk = "psS%d" % si
            if first:
                CP("act", E[d], psS, [pk], [("E", d)])
            else:
                E4 = E[d].rearrange("p (j q) -> p j q", j=4)
                TT(E4, E4, bc4(dec_bc[:, t, c0:c0 + 4]), ALU.mult, [("E", d), ("dec", t)], [("E", d)])
                TT(E[d], E[d], psS, ALU.add, [("E", d), pk], [("E", d)])

        def run_states(g, d, tiles, first0, pre=None):
            tiles = list(tiles)
            ss_a(g, tiles[0], d, 0)
            for i, t in enumerate(tiles):
                if i + 1 < len(tiles):
                    ss_a(g, tiles[i + 1], d, (i + 1) % 2)
                if pre is not None:
                    pre(t)
                ss_b(g, t, d, i % 2, first0 and i == 0)

        def ssd_pass1(g):
            ssd_prep(g, False)
            for d, ctx_order, lat_order in ((1, (17, 16), range(15, -1, -1)), (0, (16, 17), range(16))):
                run_states(g, d, ctx_order, True)
                DMA("sp", sctx_d[:, (g * 2 + d) * 256:(g * 2 + d + 1) * 256], E[d], [("E", d)], ["sctx_d"])
                run_states(g, d, lat_order, True)
                DMA("sp", contrib_ap(g, d), E[d], [("E", d)], ["contrib"])

        NG = int(os.environ.get("KNG", "8"))
        if debug == "ssd_dt":
            dd = dout("dt", [128, 5, 18 * 64], F32)
            for i, arr in enumerate((dt_tm, negA, expA, dec_bc, dtw)):
                DMA("sp", dd[:, i, :], arr.rearrange("p t c -> p (t c)"), [(nm, t) for nm in ("dt_tm", "negA", "expA", "dec", "dtw") for t in range(18)], ["dd"])
            P.emit(es)
            es.close()
            P.in_names = in_names
            return nc, P
        for g in range(NG):
            ssd_pass1(g)
        if NG < 8:
            MEMSET(acc[:, 0:2048], 0.0, ["acc"])
            for gz in range(NG, 8):
                for dz in range(2):
                    DMA("sp", contrib_ap(gz, dz), acc[:, 0:256], ["acc"], ["contrib"])
        for ci in range(3):
            def _ag(e, ci=ci):
                return e.collective_compute("AllGather", ALU.bypass, replica_groups=GROUPS,
                                            ins=[cparts[ci].ap().opt()], outs=[gparts[ci].ap().opt()])
            P.add("pool", _ag, ["contrib", "rs_d", "ccchain"], ["gath", "ccchain"], cc=True)
        DMA("sp", Dm, gparts[2].ap().rearrange("(r n) c -> n r c", n=128), ["gath"], ["Dm"])
        TS(Dm, Dm, -1.0, ALU.add, ["Dm"], ["Dm"])
        TT(Dm, Dm, cmask, ALU.mult, ["Dm", "smallc"], ["Dm"])
        TS(Dm, Dm, 1.0, ALU.add, ["Dm"], ["Dm"])

        def ssd_pass2(g):
            ssd_prep(g, True)
            wz, wzk = wq.load(win_v[:, :, Z_START + 256 * g:Z_START + 256 * (g + 1)])
            for d in range(2):
                c0 = d * 32 + 4 * g
                DMA("sp", Fg[d], gath_ap(g, d), ["gath"], [("Fg", d)])
                DMA("sp", E[d], sctx_d[:, (g * 2 + d) * 256:(g * 2 + d + 1) * 256], ["sctx_d"], [("E", d)])
                E4 = E[d].rearrange("p (j q) -> p j q", j=4)
                for r in (range(4) if d == 0 else range(3, -1, -1)):
                    TT(E4, E4, bc4(Dm[:, r, c0:c0 + 4]), ALU.mult, [("E", d), "Dm"], [("E", d)])
                    STT(E[d], Fg[d][:, r, :], cmask[:, r, c0:c0 + 1], E[d], ALU.mult, ALU.add,
                        [("E", d), ("Fg", d), "smallc"], [("E", d)])
            run_states(g, 1, range(15, -1, -1), False,
                       pre=lambda t: CP("act", Eb_all[:, t, :], E[1], [("E", 1)], [("Eb", t)]))
            CP("act", Ebf0, E[0], [("E", 0)], ["Ebf0"])
            psC = psA[0][:, 0:128]
            psZ = psA[6][:, 0:256]
            psY = psA[1][:, 0:256]
            psO = psA[4]
            def h1a(t):
                p = t % 2
                cols = slice(t * 128, (t + 1) * 128)
                MM(psC, xbcT[:, 2, cols], xbcT[:, 3, cols], True, True, [("xbcT", 2), ("xbcT", 3)], ["psC"])
                for d in range(2):
                    CP(EL, drep[p][:, 4 * d:4 * d + 4, :],
                       dta_bf[:, t, d * 32 + 4 * g:d * 32 + 4 * g + 4].unsqueeze(2).to_broadcast([128, 4, 128]),
                       [("dta_bf", t)], [("drep", d, p)])
                for d in range(2):
                    c0 = d * 32 + 4 * g
                    pD = psA[2 + d]
                    for j in range(4):
                        MM(pD[:, 128 * j:128 * (j + 1)], ident_bf, mask4[:, d, 0:128], True, False, ["cst"], [("psD", d)])
                        MM(pD[:, 128 * j:128 * (j + 1)], drep[p][:, 4 * d + j, :], triU if d == 0 else triL, False, True,
                           [("drep", d, p), "cst"], [("psD", d)])
                    for j in range(4):
                        ACTF(Lt[d][p][:, j, :], pD[:, 128 * j:128 * (j + 1)], AF.Exp, [("psD", d), ("negA", t)], [("Lt", d, p)],
                             bias=negA[:, t, c0 + j:c0 + j + 1])
                for k in range(8):
                    MM(psZ, hT[:, k, cols], wz[:, k, :], k == 0, k == 7, [("hT", t), wzk], ["psZ"])
                ACTF(SZ[p], psZ, AF.Silu, ["psZ"], [("SZ", p)])

            def h1b(t):
                p = t % 2
                xs4 = xsB[:, t, 0:256].rearrange("p (j q) -> p j q", j=4)
                for d in range(2):
                    c0 = d * 32 + 4 * g
                    TT(GT[d][p], Lt[d][p], psC.unsqueeze(1).to_broadcast([128, 4, 128]), ALU.mult, [("Lt", d, p), "psC"],
                       [("GT", d, p)])
                    TT(xdt[d][p].rearrange("p (j q) -> p j q", j=4), xs4, bc4(dt_tm[:, t, c0:c0 + 4]), ALU.mult,
                       [("xsB", t), ("dt_tm", t)], [("xdt", d, p)], eng=EL)
                TT(xsD[p].rearrange("p (j q) -> p j q", j=4), xs4, bc4(dskip[:, 4 * g:4 * g + 4]), ALU.mult,
                   [("xsB", t), "smallc"], [("xsD", p)], eng=EL)

            def h2a(t):
                p = t % 2
                cols = slice(t * 128, (t + 1) * 128)
                for j in range(4):
                    js = slice(64 * j, 64 * (j + 1))
                    MM(psY[:, js], ident_bf, xsD[p][:, js], True, False, ["cst", ("xsD", p)], ["psY"])
                    MM(psY[:, js], GT[0][p][:, j, :], xdt[0][p][:, js], False, False, [("GT", 0, p), ("xdt", 0, p)], ["psY"])
                    MM(psY[:, js], GT[1][p][:, j, :], xdt[1][p][:, js], False, True, [("GT", 1, p), ("xdt", 1, p)], ["psY"])
                MM(psO[:, 0:256], xbcT[:, 3, cols], Ebf0, True, True, [("xbcT", 3), "Ebf0"], ["psO0"])
                MM(psO[:, 256:512], xbcT[:, 3, cols], Eb_all[:, t, :], True, True, [("xbcT", 3), ("Eb", t)], ["psO1"])
                ss_a(g, t, 0, 0)
                ss_b(g, t, 0, 0, False)
                CP("act", Ebf0, E[0], [("E", 0)], ["Ebf0"])
                TT(T0.rearrange("p (j q) -> p j q", j=4), psO[:, 0:256].rearrange("p (j q) -> p j q", j=4),
                   bc4(expA[:, t, 4 * g:4 * g + 4]), ALU.mult, ["psO0", ("expA", t)], ["T0"])
                TT(T1.rearrange("p (j q) -> p j q", j=4), psO[:, 256:512].rearrange("p (j q) -> p j q", j=4),
                   bc4(expA[:, t, 32 + 4 * g:32 + 4 * g + 4]), ALU.mult, ["psO1", ("expA", t)], ["T1"])
                TT(T0, T0, T1, ALU.add, ["T0", "T1"], ["T0"])
                TT(T0, T0, psY, ALU.add, ["T0", "psY"], ["T0"])
                TT(YG, T0, SZ[p], ALU.mult, ["T0", ("SZ", p)], ["YG"])
                ACTF(T1, YG, AF.Square, ["YG"], ["T1", ("ss2", t)], accum=ss2[:, t:t + 1])
                ACTF(r2[:, t:t + 1], ss2[:, t:t + 1], AF.Sqrt, [("ss2", t), "epsc"], [("r2", t)], bias=epsc[:, 0:1], scale=1.0 / 256)

            def h2b(t):
                cols = slice(t * 128, (t + 1) * 128)
                RCP(r2[:, t:t + 1], r2[:, t:t + 1], [("r2", t)], [("r2", t)])
                TS(YN, YG, r2[:, t:t + 1], ALU.mult, ["YG", ("r2", t)], ["YN"])
                for i in range(2):
                    TR(psT[:, 4 + i, :], YN[:, i * 128:(i + 1) * 128], ident_bf, ["YN", "cst"], ["psT45"])
                for i in range(2):
                    ACTF(yTg[:, i, cols], psT[:, 4 + i, :], AF.Identity, ["psT45", "smallc"], ["yTg"],
                         scale=ssdg[:, 2 * g + i:2 * g + i + 1])

            h1a(0)
            h1b(0)
            for t in range(16):
                if t + 1 < 16:
                    h1a(t + 1)
                h2a(t)
                if t + 1 < 16:
                    h1b(t + 1)
                h2b(t)
            for i in range(2):
                DMA("sp", yT_d[2 * g + i, :, :], yTg[:, i, :], ["yTg"], ["yT_d"])

        for g in range(NG):
            ssd_pass2(g)
        if debug == "ssd":
            dd = dout("yT", [16, 128, NT], BF16)
            P.fence()
            tb_ = Bump(100 * 1024)([128, NT])
            for i in range(2 * NG):
                DMA("sp", tb_, yT_d[i, :, :], [], ["tb_"])
                DMA("sp", dd[i, :, :], tb_, ["tb_"], ["dd"])
            STOP[0] = True
        P.fence()

    if not STOP[0]:
        B = Bump()
        xt = [B([128, D], F32) for _ in range(2)]
        xn = [B([128, D], BF16) for _ in range(2)]
        junk = B([128, D], BF16)
        fgb = B([128, D], F32)
        xnew = [B([128, D], F32) for _ in range(4)]
        h2T = B([128, 8, 512])
        BASE = B.off
        B1 = Bump(BASE)
        wo = B1([128, 8, 1024])
        w8 = WPool([B1([128, 8, 128]) for _ in range(6)], "w8")
        w16 = WPool([B1([128, 16, 128]) for _ in range(2)], "w16")
        yTb = B1([128, 16, 512])
        fmT = B1([128, 8, 512])
        ftl = [B1([128, 1024]) for _ in range(2)]
        mergedT = B1([128, 8, 512])
        S0 = B1([128, 512], F32)
        S1 = B1([128, 512], F32)
        M0 = B1([128, 512], F32)
        M1 = B1([128, 512], F32)
        TMP = B1([128, 512], F32)
        B2 = Bump(BASE)
        wfi = WPool([B2([128, 8, 128]) for _ in range(4)], "wfi")
        wfo = [B2([128, 22, 512]) for _ in range(2)]
        actT = B2([128, 22, 512])
        SA = [B2([128, 512], F32) for _ in range(2)]
        xo = [B2([128, D], F32) for _ in range(4)]
        TMP2 = B2([128, 512], F32)
        ot = [B2([128, D], F32) for _ in range(2)]

        fg_d = din("fg_bc", [128, D])
        wssd_t = din("w_ssd_t", [8, 128, 16 * 128])
        wfft_t = din("w_fft_t", [8, 128, 8 * 128])
        wgate_t = din("w_gate_t", [16, 128, 8 * 128])
        wo_d = din("w_o", [D, D])
        wfi_t = din("w_ffn_in_t", [44, 128, 8 * 128])
        wfo_d = din("w_ffn_out", [D_FF, D])
        DMA("sp", fgb, fg_d[:, :], [], ["fgb"])
        if debug == "pd":
            yin = din("dbg_yT_in", [16, 128, NT], BF16)
            rin = din("dbg_rs_in", [32, 65536], BF16)
            tb_ = Bump(100 * 1024)([128, 16384])
            for i in range(16):
                DMA("sp", tb_[:, 0:NT], yin[i, :, :], [], ["tb_"])
                DMA("sp", yT_d[i, :, :], tb_[:, 0:NT], ["tb_"], ["yT_d"])
            for i in range(4):
                DMA("sp", tb_[0:32, :], rin[:, i * 16384:(i + 1) * 16384], [], ["tb_"])
                DMA("sp", rs_d.ap()[:, i * 16384:(i + 1) * 16384], tb_[0:32, :], ["tb_"], ["rs_d"])
            P.fence()
        wo_v = wo_d.rearrange("(k p) n -> p k n", p=128)
        wfo_v = wfo_d.rearrange("(j p) n -> p j n", p=128)
        yT_v = yT_d.rearrange("k p t -> p k t")

        for tb in range(int(os.environ.get("KTB", "4"))):
            tcols = slice(tb * 512, (tb + 1) * 512)
            DMA("sp", yTb, yT_v[:, :, tcols], [], ["yTb"])
            for hf in range(2):
                DMA("pool", wo[:, :, hf * 512:(hf + 1) * 512], wo_v[:, :, hf * 512:(hf + 1) * 512], [], [("wo", hf)])
            for tt in range(4):
                t = 4 * tb + tt
                ft, fk = ftl[tt % 2], "ftl%d" % (tt % 2)
                DMA("sp", ft, rs_d.ap()[2 * t:2 * t + 2, :].rearrange("r (c k) -> (r c) k", k=1024), ["rs_d"], [fk])
                for k in range(8):
                    TR(psT[:, k, :], ft[:, k * 128:(k + 1) * 128], ident_bf, [fk, "cst"], ["psT"])
                CP("act", fmT[:, :, tt * 128:(tt + 1) * 128], psT[:, :, :], ["psT"], ["fmT"])
            for fc in range(8):
                wss, wssk = w16.load(wssd_t[fc].rearrange("p (k c) -> p k c", k=16))
                wff, wffk = w8.load(wfft_t[fc].rearrange("p (k c) -> p k c", k=8))
                wg0, wg0k = w8.load(wgate_t[fc].rearrange("p (k c) -> p k c", k=8))
                wg1, wg1k = w8.load(wgate_t[8 + fc].rearrange("p (k c) -> p k c", k=8))
                for k in range(16):
                    MM(psA[0][:, :], wss[:, k, :], yTb[:, k, :], k == 0, k == 15, [wssk, "yTb"], ["psA0"])
                for k in range(8):
                    MM(psA[1][:, :], wff[:, k, :], fmT[:, k, :], k == 0, k == 7, [wffk, "fmT"], ["psA1"])
                for k in range(8):
                    MM(psA[2][:, :], wg0[:, k, :], hT[:, k, tcols], k == 0, k == 7, [wg0k] + hk(tb), ["psA2"])
                for k in range(8):
                    MM(psA[3][:, :], wg1[:, k, :], hT[:, k, tcols], k == 0, k == 7, [wg1k] + hk(tb), ["psA3"])
                ACTF(S0, psA[2][:, :], AF.Sigmoid, ["psA2"], ["S0"])
                ACTF(S1, psA[3][:, :], AF.Sigmoid, ["psA3"], ["S1"])
                TT(M0, S0, psA[0][:, :], ALU.mult, ["S0", "psA0"], ["M0"])
                TT(M1, S1, psA[1][:, :], ALU.mult, ["S1", "psA1"], ["M1"])
                TT(mergedT[:, fc, :], M0, M1, ALU.add, ["M0", "M1"], ["mergedT"])
            for tt in range(4):
                t = 4 * tb + tt
                xi = tt % 2
                xk = "xt%d" % xi
                DMA("sp", xt[xi], x_d[t * 128:(t + 1) * 128, :], [], [xk])
                for hf in range(2):
                    pst, pk = psA[4 + hf], "psA%d" % (4 + hf)
                    hs = slice(hf * 512, (hf + 1) * 512)
                    for k in range(8):
                        MM(pst[:, :], mergedT[:, k, tt * 128:(tt + 1) * 128], wo[:, k, hs], k == 0, k == 7,
                           ["mergedT", ("wo", hf)], [pk])
                    TT(TMP, pst[:, :], gbc[:, hs], ALU.mult, [pk, "gbc"], ["TMP"])
                    TT(xnew[tt][:, hs], TMP, xt[xi][:, hs], ALU.add, ["TMP", xk], [("xnew", tt)])
                norm_core(20 + tt, xnew[tt], ("xnew", tt), 128, lambda k, tt=tt: h2T[:, k, tt * 128:(tt + 1) * 128], 0, a2, 24,
                          ["h2T"])
            P.fence()
            DMA("pool", wfo[0], wfo_v[:, :, 0:512], [], [("wfo", 0)])
            for j in range(22):
                if j == 8:
                    DMA("pool", wfo[1], wfo_v[:, :, 512:1024], [], [("wfo", 1)])
                wa, wak = wfi.load(wfi_t[j].rearrange("p (k c) -> p k c", k=8))
                wb, wbk = wfi.load(wfi_t[22 + j].rearrange("p (k c) -> p k c", k=8))
                pa, pak = psA[2 * (j % 2)], "psA%d" % (2 * (j % 2))
                pb, pbk = psA[2 * (j % 2) + 1], "psA%d" % (2 * (j % 2) + 1)
                for k in range(8):
                    MM(pa[:, :], wa[:, k, :], h2T[:, k, :], k == 0, k == 7, [wak, "h2T"], [pak])
                for k in range(8):
                    MM(pb[:, :], wb[:, k, :], h2T[:, k, :], k == 0, k == 7, [wbk, "h2T"], [pbk])
                sa, sak = SA[j % 2], "SA%d" % (j % 2)
                ACTF(sa, pa[:, :], AF.Silu, [pak], [sak])
                TT(actT[:, j, :], sa, pb[:, :], ALU.mult, [sak, pbk], ["actT"])
            for hf in range(2):
                hs = slice(hf * 512, (hf + 1) * 512)
                for tt in range(4):
                    pst, pk = psA[4 + tt % 2], "psA%d" % (4 + tt % 2)
                    for j in range(22):
                        MM(pst[:, :], actT[:, j, tt * 128:(tt + 1) * 128], wfo[hf][:, j, :], j == 0, j == 21,
                           ["actT", ("wfo", hf)], [pk])
                    TT(TMP2, pst[:, :], gbc[:, 1024 + hf * 512:1024 + (hf + 1) * 512], ALU.mult, [pk, "gbc"], ["TMP2"])
                    TT(xo[tt][:, hs], TMP2, xnew[tt][:, hs], ALU.add, ["TMP2", ("xnew", tt)], [("xo", tt)])
            for tt in range(4):
                t = 4 * tb + tt
                ti = 24 + tt
                oi = tt % 2
                ACTF(junk, xo[tt], AF.Square, [("xo", tt)], ["junk", ("ssq", ti)], accum=ssq[:, ti:ti + 1])
                ACTF(rstd[:, ti:ti + 1], ssq[:, ti:ti + 1], AF.Sqrt, [("ssq", ti), "epsc"], [("rstd", ti)],
                     bias=epsc[:, 0:1], scale=1.0 / D)
                RCP(rstd[:, ti:ti + 1], rstd[:, ti:ti + 1], [("rstd", ti)], [("rstd", ti)])
                STT(ot[oi], xo[tt], rstd[:, ti:ti + 1], fgb, ALU.mult, ALU.mult, [("xo", tt), ("rstd", ti), "fgb"], [("ot", oi)])
                DMA("sp", out_d[t * 128:(t + 1) * 128, :], ot[oi], [("ot", oi)], ["out"])
            P.fence()

    P.emit(es)
    es.close()
    P.in_names = in_names
    return nc, P


def _consts(q):
    c = np.zeros((128, NCST), np.float32)
    i = np.arange(128)
    c[:, 0:128] = np.eye(128)
    c[:, 128:256] = (i[:, None] <= i[None, :])
    c[:, 256:384] = (i[:, None] >= i[None, :])
    c[:, 384:512] = 1.0
    mf = np.where(i[:, None] <= i[None, :], 0.0, -30000.0)
    mb = np.where(i[:, None] >= i[None, :], 0.0, -30000.0)
    c[:, 512:1024] = np.tile(mf, (1, 4))
    c[:, 1024:1536] = np.tile(mb, (1, 4))
    cc = np.arange(64)
    ang = 2 * np.pi * np.outer(cc, cc) / 64.0
    w2c = np.zeros((128, 128))
    w2s = np.zeros((128, 128))
    for r2 in range(2):
        w2c[r2 * 64:(r2 + 1) * 64, r2 * 64:(r2 + 1) * 64] = np.cos(ang) / 8.0
        w2s[r2 * 64:(r2 + 1) * 64, r2 * 64:(r2 + 1) * 64] = np.sin(ang) / 8.0
    c[:, 1536:1664] = w2c
    c[:, 1664:1792] = w2s
    angc = 2 * np.pi * np.outer(i, i) / 128.0
    Cc = np.cos(angc) / math.sqrt(128.0)
    Sc = np.sin(angc) / math.sqrt(128.0)
    c[:, 1792:1920] = Cc
    c[:, 1920:2048] = Sc
    c[:, 2048:2176] = -Sc
    c[:, 2176:2304] = Cc
    rl = np.arange(32)
    angr = 2 * np.pi * np.outer(32 * q + rl, i) / 128.0
    blk = np.zeros((128, 4, 2, 128))
    for c4 in range(4):
        blk[c4 * 32:(c4 + 1) * 32, c4, 0, :] = np.cos(angr) / math.sqrt(128.0)
        blk[c4 * 32:(c4 + 1) * 32, c4, 1, :] = -np.sin(angr) / math.sqrt(128.0)
    c[:, 2304:3328] = blk.reshape(128, 1024)
    return c


def _tiles(w):
    kk, nb = w.shape[0] // 128, w.shape[1] // 128
    return np.ascontiguousarray(w.reshape(kk, 128, nb, 128).transpose(2, 1, 0, 3).reshape(nb, 128, kk * 128))


def _prep_inputs(inp):
    f32 = np.float32
    x = np.asarray(inp["x"], f32)
    ctx = np.asarray(inp["ctx"], f32)
    c = np.asarray(inp["c"], f32)
    c_ctx = np.asarray(inp["c_ctx"], f32)
    b_ada = np.asarray(inp["b_ada"], f32)[0]
    conv_w = np.asarray(inp["conv_w"], f32)[0]
    shared = {
        "w_ada": np.ascontiguousarray(inp["w_ada"][0], f32),
        "b_ada_row": np.ascontiguousarray(b_ada.reshape(1, -1)),
        "w_in": np.ascontiguousarray(inp["w_in"][0], f32),
        "identc": np.eye(128, dtype=f32),
        "convw_fm": np.ascontiguousarray(conv_w.reshape(5, 32, 128).transpose(2, 1, 0).reshape(128, 160)),
        "fg_bc": np.ascontiguousarray(np.broadcast_to(np.asarray(inp["final_g"], f32)[None, :], (128, D))),
        "w_ssd_t": _tiles(np.asarray(inp["w_ssd_out"][0], f32)),
        "w_fft_t": _tiles(np.asarray(inp["w_fft_out"][0], f32)),
        "w_gate_t": _tiles(np.asarray(inp["w_in"][0], f32)[:, GATE_START:GATE_START + 2048]),
        "w_o": np.ascontiguousarray(inp["w_o"][0], f32),
        "w_ffn_in_t": _tiles(np.asarray(inp["w_ffn_in"][0], f32)),
        "w_ffn_out": np.ascontiguousarray(inp["w_ffn_out"][0], f32),
    }
    sm = np.zeros((128, NSMALL), f32)
    sm[:, 0:48] = b_ada.reshape(48, 128).T
    sm[:, 48:56] = np.asarray(inp["norm1_g"], f32)[0].reshape(8, 128).T
    sm[:, 56:64] = np.asarray(inp["norm2_g"], f32)[0].reshape(8, 128).T
    sm[:, 64:96] = np.asarray(inp["conv_b"], f32)[0].reshape(32, 128).T
    sm[:, 96:112] = np.asarray(inp["ssd_norm_g"], f32)[0].reshape(16, 128).T
    sm[:, 112:144] = np.asarray(inp["d_skip"], f32)[0][None, :]
    sm[:, 404:468] = np.asarray(inp["dt_bias"], f32)[0].reshape(1, 64)
    sm[:, 468:532] = np.asarray(inp["a_log"], f32)[0].reshape(1, 64)
    maps = []
    for core in range(8):
        b, q = core // 4, core % 4
        t0 = q * NT
        m = dict(shared)
        m["x"] = np.ascontiguousarray(x[b, t0:t0 + NT])
        m["ctx"] = np.ascontiguousarray(ctx[b])
        xh = np.zeros((4, D), f32)
        for i, tt in enumerate((t0 - 2, t0 - 1, t0 + NT, t0 + NT + 1)):
            if 0 <= tt < 8192:
                xh[i] = x[b, tt]
        m["xhalo"] = xh
        cv = np.zeros((128, 16), f32)
        cv[:, 0:8] = c[b].reshape(8, 128).T
        cv[:, 8:16] = c_ctx.reshape(8, 128).T
        m["cvec"] = cv
        s = sm.copy()
        s[:, 144] = 0.0 if q == 0 else 1.0
        s[:, 145] = 0.0 if q == 3 else 1.0
        cm = np.zeros((4, 64), f32)
        for r in range(4):
            cm[r, 0:32] = 1.0 if r < q else 0.0
            cm[r, 32:64] = 1.0 if r > q else 0.0
        s[:, 148:404] = cm.reshape(1, 256)
        m["smallc"] = s
        m["cst"] = _consts(q)
        maps.append(m)
    return maps


_CACHE = {}


def kernel(**inp):
    if "nc" not in _CACHE:
        _CACHE["nc"] = build()[0]
    nc = _CACHE["nc"]
    maps = _prep_inputs(inp)
    res = run_bass_kernel_spmd(nc, maps, core_ids=list(range(8)))
    out = np.zeros((2, 8192, D), np.float32)
    for core in range(8):
        b, q = core // 4, core % 4
        out[b, q * NT:(q + 1) * NT] = np.asarray(res.results[core]["out"], np.float32)
    return out
```
